# Optimizing a Trainium2 kernel written in Bass

```python
import math
import jax
import jax.numpy as jnp
from jax import lax
import numpy as np

D_MODEL = 2048
BATCH = 4
SEQ = 4096
DEPTH = 4

GRID_W = 64
CTX_LEN = 256
N_MIXERS = 4
D_FF = 4 * D_MODEL
Q_BLOCK = 128
ROPE_BASE = 10000.0
EPS = 1e-6
DN_ALPHA = (2.0 * DEPTH) ** 0.25
DN_BETA = (8.0 * DEPTH) ** -0.25
ADA_SCALE = 0.5
A_HEADS = D_MODEL // 128
A_DK = 64
A_DV = 2 * A_DK
B_HEADS = D_MODEL // 128
B_KV_HEADS = 4
B_GROUP = B_HEADS // B_KV_HEADS
B_HD = 128
B_IN_DIM = D_MODEL + 2 * B_KV_HEADS * B_HD
C_DI = 2 * D_MODEL
C_HD = 64
C_HEADS = C_DI // C_HD
C_GROUPS = 8
C_HPG = C_HEADS // C_GROUPS
C_STATE = 128
C_CONV = 5
C_CHUNK = 128
C_CONV_DIM = C_DI + 2 * C_GROUPS * C_STATE
C_IN_DIM = C_DI + C_CONV_DIM + 2 * C_HEADS
D_HEADS = D_MODEL // 128
D_HD = 128
NA_ROWS_MAX = 8
NA_COLS = 16

kernel_name = "hybrid_diffusion_backbone_interleaved"

F32 = jnp.float32


def n_of_mixer(m):
    return len(range(m, DEPTH, N_MIXERS))


def layer_norm(x, g, b):
    x32 = x.astype(F32)
    mu = jnp.mean(x32, axis=-1, keepdims=True)
    var = jnp.mean(jnp.square(x32 - mu), axis=-1, keepdims=True)
    return ((x32 - mu) * lax.rsqrt(var + EPS) * g.astype(F32) + b.astype(F32)).astype(x.dtype)


def rms_norm(x, g):
    x32 = x.astype(F32)
    y = x32 * lax.rsqrt(jnp.mean(x32 * x32, axis=-1, keepdims=True) + EPS)
    return (y * g.astype(F32)).astype(x.dtype)


def axial_rope_angles(n, dim):
    t = jnp.arange(n, dtype=jnp.int32)
    row = (t // GRID_W).astype(F32)
    col = (t % GRID_W).astype(F32)
    n_pairs = dim // 4
    inv = ROPE_BASE ** (-jnp.arange(n_pairs, dtype=F32) / n_pairs)
    ang = jnp.concatenate([row[:, None] * inv, col[:, None] * inv], axis=-1)
    return jnp.cos(ang), jnp.sin(ang)


def apply_rope(x, cos, sin):
    half = x.shape[-1] // 2
    shp = (1, x.shape[1]) + (1,) * (x.ndim - 3) + (half,)
    cs, sn = cos.reshape(shp), sin.reshape(shp)
    xp = x.astype(F32).reshape(x.shape[:-1] + (half, 2))
    x1, x2 = xp[..., 0], xp[..., 1]
    out = jnp.stack([x1 * cs - x2 * sn, x1 * sn + x2 * cs], axis=-1).reshape(x.shape)
    return out.astype(x.dtype)


def sweep_query_blocks(fn, q):
    b, n = q.shape[0], q.shape[1]
    nb = n // Q_BLOCK
    qb = jnp.moveaxis(q.reshape((b, nb, Q_BLOCK) + q.shape[2:]), 1, 0)
    out = jnp.moveaxis(lax.map(fn, qb), 0, 1)
    return out.reshape((b, n) + out.shape[3:])


def diff_attention(u_lat, u_ctx, w_in, lam_vec, sub_g, w_out, cos, sin, lam_init, need_ctx):
    def proj(u):
        b, n, _ = u.shape
        qkv = u @ w_in
        q = qkv[..., :D_MODEL].reshape(b, n, A_HEADS, 2, A_DK)
        k = qkv[..., D_MODEL:2 * D_MODEL].reshape(b, n, A_HEADS, 2, A_DK)
        v = qkv[..., 2 * D_MODEL:].reshape(b, n, A_HEADS, A_DV)
        return q, k, v
    q_l, k_l, v_l = proj(u_lat)
    q_c, k_c, v_c = proj(u_ctx)
    q_l, k_l = apply_rope(q_l, cos, sin), apply_rope(k_l, cos, sin)
    lv = lam_vec.astype(F32)
    lam = jnp.exp(jnp.sum(lv[0] * lv[1])) - jnp.exp(jnp.sum(lv[2] * lv[3])) + lam_init
    scale = A_DK ** -0.5

    def attend(k, v):
        def fn(qb):
            s = jnp.einsum('bqhmd,bkhmd->bhmqk', qb, k).astype(F32) * scale
            p = jax.nn.softmax(s, axis=-1)
            a = (p[:, :, 0] - lam * p[:, :, 1]).astype(v.dtype)
            return jnp.einsum('bhqk,bkhd->bqhd', a, v)
        return fn

    def post(o):
        o = rms_norm(o, sub_g) * (1.0 - lam_init)
        return o.reshape(o.shape[0], o.shape[1], D_MODEL) @ w_out

    k_all = jnp.concatenate([k_c, k_l], axis=1)
    v_all = jnp.concatenate([v_c, v_l], axis=1)
    y_l = post(sweep_query_blocks(attend(k_all, v_all), q_l))
    y_c = post(attend(k_c, v_c)(q_c)) if need_ctx else None
    return y_l, y_c


def gqa_attention(u_lat, u_ctx, w_in, qn_g, kn_g, w_out, cos, sin, need_ctx):
    def proj(u):
        b, n, _ = u.shape
        qkv = u @ w_in
        q = rms_norm(qkv[..., :D_MODEL].reshape(b, n, B_HEADS, B_HD), qn_g)
        k = rms_norm(qkv[..., D_MODEL:D_MODEL + B_KV_HEADS * B_HD].reshape(b, n, B_KV_HEADS, B_HD), kn_g)
        v = qkv[..., D_MODEL + B_KV_HEADS * B_HD:].reshape(b, n, B_KV_HEADS, B_HD)
        return q, k, v
    q_l, k_l, v_l = proj(u_lat)
    q_c, k_c, v_c = proj(u_ctx)
    q_l, k_l = apply_rope(q_l, cos, sin), apply_rope(k_l, cos, sin)
    scale = B_HD ** -0.5

    def attend(k, v):
        def fn(qb):
            s = jnp.einsum('bqhgd,bkhd->bhgqk', qb, k).astype(F32) * scale
            p = jax.nn.softmax(s, axis=-1).astype(v.dtype)
            return jnp.einsum('bhgqk,bkhd->bqhgd', p, v)
        return fn

    def grp(q):
        return q.reshape(q.shape[0], q.shape[1], B_KV_HEADS, B_GROUP, B_HD)

    def post(o):
        return o.reshape(o.shape[0], o.shape[1], D_MODEL) @ w_out

    k_all = jnp.concatenate([k_c, k_l], axis=1)
    v_all = jnp.concatenate([v_c, v_l], axis=1)
    y_l = post(sweep_query_blocks(attend(k_all, v_all), grp(q_l)))
    y_c = post(attend(k_c, v_c)(grp(q_c))) if need_ctx else None
    return y_l, y_c


def depthwise_conv_centred(x, w, b):
    k = w.shape[0]
    y = lax.conv_general_dilated(x, w[:, None, :], window_strides=(1,), padding=[(k // 2, k // 2)],
                                 dimension_numbers=('NWC', 'WIO', 'NWC'), feature_group_count=x.shape[-1])
    return y + b


def ssd_chunk_scan(x, dt, a, bm, cm, h0, want_y):
    b, n = x.shape[0], x.shape[1]
    nc = n // C_CHUNK

    def chunked(t):
        return jnp.moveaxis(t.reshape((b, nc, C_CHUNK) + t.shape[2:]), 1, 0)

    tri = jnp.tril(jnp.ones((C_CHUNK, C_CHUNK), dtype=bool))

    def step(h, xs):
        xc, dtc, bc, cc = xs
        cs = jnp.moveaxis(jnp.cumsum(dtc * a, axis=1), 1, -1)
        xdt = xc * dtc[..., None]
        cs_end = cs[..., -1:]
        h_new = h * jnp.exp(cs_end)[..., None] + jnp.einsum('bkgn,bghk,bkghp->bghpn', bc, jnp.exp(cs_end - cs), xdt)
        if not want_y:
            return h_new, None
        lmat = jnp.exp(jnp.where(tri, cs[..., :, None] - cs[..., None, :], -jnp.inf))
        cb = jnp.einsum('bqgn,bkgn->bgqk', cc, bc)
        y = jnp.einsum('bgqk,bghqk,bkghp->bqghp', cb, lmat, xdt)
        y = y + jnp.einsum('bqgn,bghpn,bghq->bqghp', cc, h, jnp.exp(cs))
        return h_new, y

    h, ys = lax.scan(step, h0, (chunked(x), chunked(dt), chunked(bm), chunked(cm)))
    if not want_y:
        return None, h
    return jnp.moveaxis(ys, 0, 1).reshape(x.shape), h


def mamba2_ssd(u_lat, u_ctx, w_in, conv_w, conv_b, a_log, dt_bias, d_skip, norm_g, w_out, need_ctx):
    def prep(u):
        b, n, _ = u.shape
        zxbcdt = u @ w_in
        z = zxbcdt[..., :C_DI]
        xbc = jax.nn.silu(depthwise_conv_centred(zxbcdt[..., C_DI:C_DI + C_CONV_DIM], conv_w, conv_b)).astype(F32)
        dt_raw = zxbcdt[..., C_DI + C_CONV_DIM:].astype(F32).reshape(b, n, 2, C_GROUPS, C_HPG)
        dt = jax.nn.softplus(dt_raw + dt_bias.astype(F32).reshape(2, C_GROUPS, C_HPG))
        xs = xbc[..., :C_DI].reshape(b, n, C_GROUPS, C_HPG, C_HD)
        bm = xbc[..., C_DI:C_DI + C_GROUPS * C_STATE].reshape(b, n, C_GROUPS, C_STATE)
        cm = xbc[..., C_DI + C_GROUPS * C_STATE:].reshape(b, n, C_GROUPS, C_STATE)
        return z, xs, bm, cm, dt

    a = -jnp.exp(a_log.astype(F32)).reshape(2, C_GROUPS, C_HPG)
    dsk = d_skip.astype(F32).reshape(2, C_GROUPS, C_HPG)

    def scan_dir(xs, bm, cm, dt, d, h0, want_y):
        dtd = dt[:, :, d]
        if d == 1:
            xs, bm, cm, dtd = (jnp.flip(t, axis=1) for t in (xs, bm, cm, dtd))
        y, h = ssd_chunk_scan(xs, dtd, a[d], bm, cm, h0, want_y)
        if want_y:
            y = y + dsk[d][..., None] * xs
            if d == 1:
                y = jnp.flip(y, axis=1)
        return y, h

    z_l, x_l, b_l, c_l, dt_l = prep(u_lat)
    z_c, x_c, b_c, c_c, dt_c = prep(u_ctx)
    bsz = u_lat.shape[0]
    h_zero = jnp.zeros((bsz, C_GROUPS, C_HPG, C_HD, C_STATE), F32)
    y_l = None
    y_c = None
    for d in range(2):
        yc, hc = scan_dir(x_c, b_c, c_c, dt_c, d, h_zero, need_ctx)
        yl, _ = scan_dir(x_l, b_l, c_l, dt_l, d, hc, True)
        y_l = yl if y_l is None else y_l + yl
        if need_ctx:
            y_c = yc if y_c is None else y_c + yc

    def post(y, z):
        b, n = y.shape[0], y.shape[1]
        g = rms_norm(y.reshape(b, n, C_DI) * jax.nn.silu(z.astype(F32)), norm_g)
        return g.astype(u_lat.dtype) @ w_out

    return post(y_l, z_l), (post(y_c, z_c) if need_ctx else None)


def neighbourhood_attention(u_lat, u_ctx, w_in, rpb, w_out, need_ctx):
    def proj(u):
        b, n, _ = u.shape
        qkv = u @ w_in
        return tuple(qkv[..., i * D_MODEL:(i + 1) * D_MODEL].reshape(b, n, D_HEADS, D_HD) for i in range(3))
    q_l, k_l, v_l = proj(u_lat)
    q_c, k_c, v_c = proj(u_ctx)
    bsz, n = u_lat.shape[0], u_lat.shape[1]
    rows = n // GRID_W
    kr = min(NA_ROWS_MAX, rows)
    nwin = kr * NA_COLS
    scale = D_HD ** -0.5
    kg = k_l.reshape(bsz, rows, GRID_W, D_HEADS, D_HD)
    vg = v_l.reshape(bsz, rows, GRID_W, D_HEADS, D_HD)
    qg = q_l.reshape(bsz, rows, GRID_W, D_HEADS, D_HD)
    col = jnp.arange(GRID_W, dtype=jnp.int32)
    col_start = jnp.clip(col - NA_COLS // 2, 0, GRID_W - NA_COLS)
    col_idx = col_start[:, None] + jnp.arange(NA_COLS, dtype=jnp.int32)
    col_bias_idx = col_idx - col[:, None] + (NA_COLS - 1)
    rpb32 = rpb.astype(F32)

    def row_fn(args):
        r, q_row = args
        r_start = jnp.clip(r - kr // 2, 0, rows - kr)
        k_band = lax.dynamic_slice_in_dim(kg, r_start, kr, axis=1)
        v_band = lax.dynamic_slice_in_dim(vg, r_start, kr, axis=1)
        k_sel = k_band[:, :, col_idx]
        v_sel = v_band[:, :, col_idx]
        row_bias_idx = r_start + jnp.arange(kr, dtype=jnp.int32) - r + (NA_ROWS_MAX - 1)
        bias = rpb32[:, row_bias_idx[:, None, None], col_bias_idx[None]]
        s_win = jnp.einsum('bqhd,brqkhd->bhqrk', q_row, k_sel).astype(F32) * scale + jnp.transpose(bias, (0, 2, 1, 3))[None]
        s_ctx = jnp.einsum('bqhd,bkhd->bhqk', q_row, k_c).astype(F32) * scale
        s = jnp.concatenate([s_win.reshape(bsz, D_HEADS, GRID_W, nwin), s_ctx], axis=-1)
        p = jax.nn.softmax(s, axis=-1).astype(v_l.dtype)
        p_win = p[..., :nwin].reshape(bsz, D_HEADS, GRID_W, kr, NA_COLS)
        return jnp.einsum('bhqrk,brqkhd->bqhd', p_win, v_sel) + jnp.einsum('bhqk,bkhd->bqhd', p[..., nwin:], v_c)

    o = lax.map(row_fn, (jnp.arange(rows, dtype=jnp.int32), jnp.moveaxis(qg, 1, 0)))
    y_l = jnp.moveaxis(o, 0, 1).reshape(bsz, n, D_MODEL) @ w_out
    y_c = None
    if need_ctx:
        s = jnp.einsum('bqhd,bkhd->bhqk', q_c, k_c).astype(F32) * scale
        p = jax.nn.softmax(s, axis=-1).astype(v_c.dtype)
        o_c = jnp.einsum('bhqk,bkhd->bqhd', p, v_c)
        y_c = o_c.reshape(o_c.shape[0], o_c.shape[1], D_MODEL) @ w_out
    return y_l, y_c


def sq_relu_mlp(u, w1, w2):
    h = jax.nn.relu(u @ w1)
    return (h * h) @ w2


def setup_inputs(seed: int = 0) -> dict:
    key = jax.random.key(seed)
    ks = iter(jax.random.split(key, 40))

    def nrm(shape, s):
        return jax.random.normal(next(ks), shape, F32) * s

    n_a, n_b, n_c, n_d = (n_of_mixer(m) for m in range(N_MIXERS))
    d = D_MODEL
    dt0 = jnp.exp(jax.random.uniform(next(ks), (n_c, 2, C_HEADS), F32, math.log(1e-3), math.log(1e-1)))
    return {
        'x': nrm((BATCH, SEQ, d), 1.0),
        'c': nrm((BATCH, d), 1.0),
        'ctx': nrm((BATCH, CTX_LEN, d), 1.0),
        'c_ctx': nrm((d,), 1.0),
        'ada_w': nrm((DEPTH, d, 6 * d), ADA_SCALE * d ** -0.5),
        'ada_b': nrm((DEPTH, 6 * d), 0.02),
        'ln_g': 1.0 + nrm((DEPTH, 2, d), 0.02),
        'ln_b': nrm((DEPTH, 2, d), 0.02),
        'mlp_w1': nrm((DEPTH, d, D_FF), d ** -0.5),
        'mlp_w2': nrm((DEPTH, D_FF, d), DN_BETA * D_FF ** -0.5),
        'a_w_in': nrm((n_a, d, 3 * d), d ** -0.5),
        'a_lambda': nrm((n_a, 4, A_DK), 0.1),
        'a_sub_g': 1.0 + nrm((n_a, A_DV), 0.02),
        'a_w_out': nrm((n_a, d, d), DN_BETA * d ** -0.5),
        'b_w_in': nrm((n_b, d, B_IN_DIM), d ** -0.5),
        'b_qn_g': 1.0 + nrm((n_b, B_HD), 0.02),
        'b_kn_g': 1.0 + nrm((n_b, B_HD), 0.02),
        'b_w_out': nrm((n_b, d, d), DN_BETA * d ** -0.5),
        'c_w_in': nrm((n_c, d, C_IN_DIM), d ** -0.5),
        'c_conv_w': nrm((n_c, C_CONV, C_CONV_DIM), C_CONV ** -0.5),
        'c_conv_b': nrm((n_c, C_CONV_DIM), 0.02),
        'c_A_log': jnp.log(jax.random.uniform(next(ks), (n_c, 2, C_HEADS), F32, 1.0, 16.0)),
        'c_dt_bias': dt0 + jnp.log(-jnp.expm1(-dt0)),
        'c_D': 1.0 + nrm((n_c, 2, C_HEADS), 0.1),
        'c_norm_g': 1.0 + nrm((n_c, C_DI), 0.02),
        'c_w_out': nrm((n_c, C_DI, d), DN_BETA * C_DI ** -0.5),
        'd_w_in': nrm((n_d, d, 3 * d), d ** -0.5),
        'd_rpb': nrm((n_d, D_HEADS, 2 * NA_ROWS_MAX - 1, 2 * NA_COLS - 1), 0.02),
        'd_w_out': nrm((n_d, d, d), DN_BETA * d ** -0.5),
    }


def reference(x, c, ctx, c_ctx, ada_w, ada_b, ln_g, ln_b, mlp_w1, mlp_w2,
              a_w_in, a_lambda, a_sub_g, a_w_out,
              b_w_in, b_qn_g, b_kn_g, b_w_out,
              c_w_in, c_conv_w, c_conv_b, c_A_log, c_dt_bias, c_D, c_norm_g, c_w_out,
              d_w_in, d_rpb, d_w_out):
    n = x.shape[1]
    cos_a, sin_a = axial_rope_angles(n, A_DK)
    cos_b, sin_b = axial_rope_angles(n, B_HD)
    x_lat, x_ctx = x, ctx
    for i in range(DEPTH):
        m, j = i % N_MIXERS, i // N_MIXERS
        last = i == DEPTH - 1
        mod_l = jnp.split(jax.nn.silu(c) @ ada_w[i] + ada_b[i], 6, axis=-1)
        mod_c = jnp.split(jax.nn.silu(c_ctx) @ ada_w[i] + ada_b[i], 6, axis=-1)
        u_l = x_lat * (1.0 + mod_l[1][:, None]) + mod_l[0][:, None]
        u_c = x_ctx * (1.0 + mod_c[1]) + mod_c[0]
        if m == 0:
            y_l, y_c = diff_attention(u_l, u_c, a_w_in[j], a_lambda[j], a_sub_g[j], a_w_out[j], cos_a, sin_a,
                                      0.8 - 0.6 * math.exp(-0.3 * i), not last)
        elif m == 1:
            y_l, y_c = gqa_attention(u_l, u_c, b_w_in[j], b_qn_g[j], b_kn_g[j], b_w_out[j], cos_b, sin_b, not last)
        elif m == 2:
            y_l, y_c = mamba2_ssd(u_l, u_c, c_w_in[j], c_conv_w[j], c_conv_b[j], c_A_log[j], c_dt_bias[j], c_D[j],
                                  c_norm_g[j], c_w_out[j], not last)
        else:
            y_l, y_c = neighbourhood_attention(u_l, u_c, d_w_in[j], d_rpb[j], d_w_out[j], not last)
        x_lat = layer_norm(DN_ALPHA * x_lat + mod_l[2][:, None] * y_l, ln_g[i, 0], ln_b[i, 0])
        u_l = x_lat * (1.0 + mod_l[4][:, None]) + mod_l[3][:, None]
        x_lat = layer_norm(DN_ALPHA * x_lat + mod_l[5][:, None] * sq_relu_mlp(u_l, mlp_w1[i], mlp_w2[i]),
                           ln_g[i, 1], ln_b[i, 1])
        if not last:
            x_ctx = layer_norm(DN_ALPHA * x_ctx + mod_c[2] * y_c, ln_g[i, 0], ln_b[i, 0])
            u_c = x_ctx * (1.0 + mod_c[4]) + mod_c[3]
            x_ctx = layer_norm(DN_ALPHA * x_ctx + mod_c[5] * sq_relu_mlp(u_c, mlp_w1[i], mlp_w2[i]),
                               ln_g[i, 1], ln_b[i, 1])
    return x_lat
```

```python
import math
import numpy as np
import ml_dtypes
import concourse.bass as bass
import concourse.mybir as mybir
from concourse.bass_utils import run_bass_kernel_spmd

F32 = mybir.dt.float32
BF16 = mybir.dt.bfloat16
AF = mybir.ActivationFunctionType
ALU = mybir.AluOpType

D = 2048
KC = 16
T_CTX = 256
T_LAT = 4096
T = T_CTX + T_LAT
DEPTH = 4
D_FF = 8192
EPS = 1e-6
DN_ALPHA = (2.0 * DEPTH) ** 0.25
GRID_W = 64
C_DI = 4096
NWB = 10
TO = 2176
NH = 8
RG = [[0, 1], [2, 3], [4, 5], [6, 7]]


class Trk:
    __slots__ = ("t", "w", "r", "name")

    def __init__(self, t, name=""):
        self.t = t
        self.w = []
        self.r = []
        self.name = name


class Kern:
    ENG = ("pe", "act", "dve", "pool", "sp")

    def __init__(self, n_dma_slots=(14, 8)):
        self.nc = bass.Bass("TRN2", target_bir_lowering=False)
        nc = self.nc
        self.e = {"pe": nc.tensor, "act": nc.scalar, "dve": nc.vector, "pool": nc.gpsimd, "sp": nc.sync}
        self._ctx = []
        self.sem = {}
        self.cnt = {}
        for k in ("pe", "act", "dve", "pool"):
            self.sem[k] = self._enter(nc.semaphore("s_" + k))
            self.cnt[k] = 0
        self.dq = {}
        for q, n in zip(("sp", "pool"), n_dma_slots):
            slots = []
            for i in range(n):
                key = "d_%s_%d" % (q, i)
                self.sem[key] = self._enter(nc.semaphore(key))
                self.cnt[key] = 0
                slots.append(key)
            self.dq[q] = [slots, 0]
        self.seen = {k: {} for k in self.ENG}
        self.uid = 0

    def _enter(self, cm):
        v = cm.__enter__()
        self._ctx.append(cm)
        return v

    def close(self):
        for cm in reversed(self._ctx):
            cm.__exit__(None, None, None)
        self._ctx = []

    def sb(self, shape, dtype, name=None):
        self.uid += 1
        name = (name or "t") + "_%d" % self.uid
        t = self._enter(self.nc.sbuf_tensor(name, list(shape), dtype))
        return Trk(t, name)

    def ps(self, shape=(128, 512), dtype=F32, name=None):
        self.uid += 1
        name = (name or "p") + "_%d" % self.uid
        t = self._enter(self.nc.psum_tensor(name, list(shape), dtype))
        return Trk(t, name)

    def dram(self, name, shape, dtype, kind="Internal"):
        t = self.nc.dram_tensor(name, list(shape), dtype, kind=kind).ap()
        return Trk(t, name)

    def _wait(self, eng, evs):
        seen = self.seen[eng]
        need = {}
        for (k, v) in evs:
            if seen.get(k, 0) >= v:
                continue
            if need.get(k, 0) < v:
                need[k] = v
        for k, v in need.items():
            self.e[eng].wait_ge(self.sem[k], v)
            seen[k] = v

    def _deps(self, eng, reads, writes, acc):
        evs = []
        for o in reads:
            evs.extend(o.w)
        for o in writes:
            if not acc:
                evs.extend(o.w)
            evs.extend(o.r)
        if eng == "pe":
            evs = [ev for ev in evs if ev[0] != "pe"]
        return evs

    @staticmethod
    def _compact(evs):
        m = {}
        for k, v in evs:
            if m.get(k, 0) < v:
                m[k] = v
        return list(m.items())

    def _commit(self, ev, reads, writes, acc):
        for o in reads:
            o.r.append(ev)
            if len(o.r) > 16:
                o.r = self._compact(o.r)
        for o in writes:
            if acc:
                o.w.append(ev)
                if len(o.w) > 16:
                    o.w = self._compact(o.w)
            else:
                o.w = [ev]
                o.r = []

    def op(self, eng, fn, reads=(), writes=(), acc=False):
        self._wait(eng, self._deps(eng, reads, writes, acc))
        ins = fn()
        self.cnt[eng] += 1
        ins.then_inc(self.sem[eng], 1)
        ev = (eng, self.cnt[eng])
        self._commit(ev, reads, writes, acc)
        return ev

    def mm(self, out, pairs, reads, first=True, last=True, out_ap=None):
        self._wait("pe", self._deps("pe", reads, [out], not first))
        oap = out_ap if out_ap is not None else out.t[:]
        n = len(pairs)
        ins = None
        for i, (l, r) in enumerate(pairs):
            ins = self.nc.tensor.matmul(oap, l, r, start=(first and i == 0), stop=(last and i == n - 1))
        self.cnt["pe"] += 1
        ins.then_inc(self.sem["pe"], 1)
        ev = ("pe", self.cnt["pe"])
        self._commit(ev, reads, [out], not first)
        return ev

    def mmg(self, items, reads):
        evs = []
        for o in reads:
            evs.extend(o.w)
        for it in items:
            evs.extend(it[0].r)
        evs = [ev for ev in evs if ev[0] != "pe"]
        self._wait("pe", evs)
        ins = None
        for it in items:
            ins = self.nc.tensor.matmul(it[1], it[2], it[3], start=it[4], stop=it[5])
        self.cnt["pe"] += 1
        ins.then_inc(self.sem["pe"], 1)
        ev = ("pe", self.cnt["pe"])
        for o in reads:
            o.r.append(ev)
            if len(o.r) > 16:
                o.r = self._compact(o.r)
        for it in items:
            o = it[0]
            if it[6]:
                o.w = [ev]
                o.r = []
            else:
                o.w.append(ev)
                if len(o.w) > 16:
                    o.w = self._compact(o.w)
        return ev

    def transpose(self, out, out_ap, in_, in_ap, ident):
        self._wait("pe", self._deps("pe", [in_, ident], [out], True))
        ins = self.nc.tensor.transpose(out_ap, in_ap, ident.t[:])
        self.cnt["pe"] += 1
        ins.then_inc(self.sem["pe"], 1)
        ev = ("pe", self.cnt["pe"])
        self._commit(ev, [in_, ident], [out], True)
        return ev

    def dma(self, q, out, out_ap, in_, in_ap, acc=True, **kw):
        slots, idx = self.dq[q]
        key = slots[idx % len(slots)]
        self.dq[q][1] = idx + 1
        evs = self._deps(q, [in_], [out], acc)
        if self.cnt[key] > 0:
            evs.append((key, self.cnt[key]))
        self._wait(q, evs)
        ins = self.e[q].dma_start(out=out_ap, in_=in_ap, **kw)
        self.cnt[key] += 16
        ins.then_inc(self.sem[key], 16)
        ev = (key, self.cnt[key])
        self._commit(ev, [in_], [out], acc)
        return ev

    def collective(self, kind, op, in_, in_ap, out, out_ap):
        self.uid += 1
        key = "cc%d" % self.uid
        self.sem[key] = self._enter(self.nc.semaphore(key))
        self.cnt[key] = 0
        self._wait("pool", self._deps("pool", [in_], [out], False))
        ins = self.nc.gpsimd.collective_compute(kind, op, replica_groups=RG, ins=[in_ap], outs=[out_ap])
        ins.then_inc(self.sem[key], 1)
        self.cnt[key] = 1
        ev = (key, 1)
        self._commit(ev, [in_], [out], False)
        return ev

    def fence(self, obj, eng="sp"):
        self._wait(eng, list(obj.w))

    def barrier(self):
        evs = [(key, v) for key, v in self.cnt.items() if v > 0]
        for eng in self.ENG:
            self._wait(eng, [ev for ev in evs if not (eng == "pe" and ev[0] == "pe")])

    def mark(self):
        return len(self._ctx)

    def release(self, mark):
        self.barrier()
        while len(self._ctx) > mark:
            self._ctx.pop().__exit__(None, None, None)


BLOCKS = [(0, 256, 1)] + [(T_CTX + i * 512, 512, 0) for i in range(8)]
OWN_BLOCKS = [(0, 128, 1)] + [(128 + i * 512, 512, 0) for i in range(4)]
ALL_BLOCKS = [(0, 0, 128, 0, 1), (1, 0, 128, 128, 1)] + \
    [(r, 128 + i * 512, 512, T_CTX + r * 2048 + i * 512, 0) for r in range(2) for i in range(4)]


class Prog:
    def __init__(self, nlayers=4, dbg=False):
        self.nlayers = nlayers
        self.k = k = Kern()
        self.nc = nc = k.nc
        self.dbg = dbg
        ein = lambda name, shape, dt=F32: k.dram(name, shape, dt, kind="ExternalInput")
        self.xT = ein("xT", [KC, 128, TO])
        self.cT = ein("cT", [128, KC, 2])
        self.ada_w = ein("ada_w", [DEPTH, D, 6 * D])
        self.ada_bT = ein("ada_bT", [128, DEPTH, 96])
        self.lnT = ein("lnT", [128, DEPTH, 2, 2, KC])
        self.mlp_w1 = ein("mlp_w1", [DEPTH, D, D_FF])
        self.mlp_w2 = ein("mlp_w2", [DEPTH, D_FF, D])
        self.ident_in = ein("ident", [128, 128], BF16)
        self.a_w_in = ein("a_w_in", [D, 3072])
        self.a_w_sw = ein("a_w_sw", [D, 2048])
        self.a_lam = ein("a_lam", [1, 256])
        self.a_sub_g = ein("a_sub_g", [128, 1])
        self.a_w_out = ein("a_w_out", [1024, D])
        self.ropeA = ein("ropeA", [2, 128, T])
        self.b_w_in = ein("b_w_in", [D, 1536])
        self.b_w_sw = ein("b_w_sw", [D, 1280])
        self.b_g = ein("b_g", [128, 4])
        self.b_w_out = ein("b_w_out", [1024, D])
        self.ropeB = ein("ropeB", [2, 128, T])
        self.c_w_in = ein("c_w_in", [D, 5248])
        self.c_conv_wT = ein("c_conv_wT", [128, 24, 5])
        self.c_conv_bT = ein("c_conv_bT", [128, 24])
        self.c_hp = ein("c_hp", [64, 4])
        self.c_norm_gT = ein("c_norm_gT", [128, 16])
        self.c_w_out = ein("c_w_out", [2048, D])
        self.c_tri = ein("c_tri", [2, 128, 128])
        self.d_w_in = ein("d_w_in", [D, 3072])
        self.d_ee = ein("d_ee", [8, 128, 14, 64])
        self.d_mask = ein("d_mask", [128, 14, 64])
        self.d_w_out = ein("d_w_out", [1024, D])
        self.out = k.dram("out", [KC, 128, 2048], F32, kind="ExternalOutput")
        if dbg:
            self.out_ctx = k.dram("out_ctx", [KC, 128, 128], F32, kind="ExternalOutput")
        self.XT = k.dram("XTs", [KC, 128, TO], F32)
        self.UGi = [k.dram("UGi%d" % j, [KC * 128, n // 2], F32) for j, (l0, n, col) in enumerate(OWN_BLOCKS)]
        self.UG = [k.dram("UG%d" % j, [2 * KC * 128, n // 2], F32) for j, (l0, n, col) in enumerate(OWN_BLOCKS)]
        self.YP = [k.dram("YP%d" % j, [2 * 2 * 128, TO], F32) for j in range(8)] + [k.dram("YP8", [2 * 128, TO], F32)]
        self.YR = [k.dram("YR%d" % j, [2 * 128, TO], F32) for j in range(8)] + [k.dram("YR8", [128, TO], F32)]
        self.AO = k.dram("AOs", [16, 128, T], BF16)
        self.QT = k.dram("QTs", [NH, 128, T], BF16)
        self.KT = k.dram("KTs", [NH, 128, T], BF16)
        self.VK = k.dram("VKs", [NH, T, 128], BF16)
        self.ZT = k.dram("ZTs", [16, 128, T], F32)
        self.XBC = k.dram("XBCs", [24, 128, T], F32)
        self.DTR = k.dram("DTRs", [64, T], F32)
        self.XS = k.dram("XSs", [16, 128, T], F32)
        self.BCb = k.dram("BCbs", [8, 128, T], BF16)
        self.YS = k.dram("YSs", [32, 64, T], F32)
        self.P = [k.ps([128, 512], F32, "P%d" % i) for i in range(7)]
        self.PT = k.ps([128, 1024], BF16, "PT")
        self.ident = k.sb([128, 128], BF16, "ident")
        self.ones_f = k.sb([128, 128], F32, "ones_f")
        self.ones_b = k.sb([128, 128], BF16, "ones_b")
        self.mod = k.sb([128, DEPTH, 96, 2], F32, "mod")
        self.ln = k.sb([128, DEPTH, 2, 2, KC], F32, "ln")
        self.wb = [k.sb([128, 16, 128], BF16, "wb%d" % i) for i in range(NWB)]
        self.wbi = 0
        k.dma("sp", self.ident, self.ident.t[:], self.ident_in, self.ident_in.t[:])
        k.dma("sp", self.ln, self.ln.t[:], self.lnT, self.lnT.t[:])
        k.op("dve", lambda: nc.vector.memset(self.ones_f.t[:], 1.0), [], [self.ones_f])
        k.op("dve", lambda: nc.vector.memset(self.ones_b.t[:], 1.0), [], [self.ones_b])
        self.wcache = {}
        self.Xc = [Trk(None, "Xc%d" % i) for i in range(KC)]

    def conv_w(self, name, src, src_ap, K, ncols):
        k = self.k
        nch = ncols // 128
        kc = K // 128
        dst = k.dram(name, [nch, 128, kc, 128], BF16)
        for c in range(nch):
            for k0 in range(0, kc, 16):
                k1 = min(kc, k0 + 16)
                k.dma("pool", dst, dst.t[c][:, k0:k1, :], src,
                      src_ap[k0 * 128:k1 * 128, c * 128:(c + 1) * 128].rearrange("(kc p) n -> p kc n", p=128))
        return dst

    def gemm(self, wblk, chunks, kgroups, rhs_ap, rhs_trks, n, epi, psums, kper=16):
        k = self.k
        seq = [(ci, c, kg) for ci, c in enumerate(chunks) for kg in range(kgroups)]
        bufs = {}
        issued = 0
        PF = 8
        ps = None
        for j in range(len(seq)):
            while issued < min(len(seq), j + PF):
                ci, c, kg = seq[issued]
                wb = self.wb[self.wbi % NWB]
                self.wbi += 1
                k.dma("sp", wb, wb.t[:, :kper, :], wblk, wblk.t[c][:, kg * kper:(kg + 1) * kper, :], acc=False)
                bufs[issued] = wb
                issued += 1
            ci, c, kg = seq[j]
            if kg == 0:
                ps = psums[ci % len(psums)]
            wb = bufs.pop(j)
            pairs = [(wb.t[:, i, :], rhs_ap(kg * kper + i)) for i in range(kper)]
            k.mm(ps, pairs, [wb] + list(rhs_trks), first=(kg == 0), last=(kg == kgroups - 1), out_ap=ps.t[:, :n])
            if kg == kgroups - 1:
                epi(ci, c, ps)

    def mod_init(self):
        k, nc = self.k, self.nc
        self.sc = k.sb([128, KC, 2], BF16, "sc")
        self.adb = k.sb([128, DEPTH, 96], F32, "adb")
        mark = k.mark()
        cs = k.sb([128, KC, 2], F32, "cs")
        k.dma("sp", cs, cs.t[:], self.cT, self.cT.t[:])
        k.dma("sp", self.adb, self.adb.t[:], self.ada_bT, self.ada_bT.t[:])
        k.op("act", lambda: nc.scalar.activation(self.sc.t[:], cs.t[:], AF.Silu), [cs], [self.sc])
        k.release(mark)

    def conv_ada(self, L):
        self.wcache[("ada", L)] = self.conv_w("adaB%d" % L, self.ada_w, self.ada_w.t[L], D, 6 * D)

    def compute_mod(self, L):
        k, nc = self.k, self.nc
        wblk = self.wcache[("ada", L)]
        sc, adb = self.sc, self.adb

        def epi(ci, c, ps):
            k.op("dve", lambda: nc.vector.tensor_scalar(self.mod.t[:, L, c, :], ps.t[:, 0:2], adb.t[:, L, c:c + 1], None, ALU.add),
                 [ps, adb], [self.mod], acc=True)
        self.gemm(wblk, list(range(96)), 1, lambda kc: sc.t[:, kc, :], [sc], 2, epi, self.P[0:3])
        for j in (1, 4):
            k.op("dve", lambda: nc.vector.tensor_scalar(self.mod.t[:, L, j * 16:(j + 1) * 16, :], self.mod.t[:, L, j * 16:(j + 1) * 16, :], 1.0, None, ALU.add),
                 [self.mod], [self.mod])

    def mv(self, L, j, dc, col):
        return self.mod.t[:, L, j * 16 + dc, col:col + 1]

    def layer_norm(self, X, n, L, which, tmp, st):
        k, nc = self.k, self.nc
        S1, S2 = self.P[3], self.P[4]
        for dc in range(KC):
            k.mm(S1, [(self.ones_f.t[:], X.t[:, dc, :n])], [self.ones_f, X], first=(dc == 0), last=(dc == KC - 1), out_ap=S1.t[:, :n])
            k.op("act", lambda: nc.scalar.activation(tmp.t[:, :n], X.t[:, dc, :n], AF.Square), [X], [tmp])
            k.mm(S2, [(self.ones_f.t[:], tmp.t[:, :n])], [self.ones_f, tmp], first=(dc == 0), last=(dc == KC - 1), out_ap=S2.t[:, :n])
        mean, rstd = st["mean"], st["rstd"]
        k.op("act", lambda: nc.scalar.mul(mean.t[:, :n], S1.t[:, :n], 1.0 / D), [S1], [mean])
        k.op("dve", lambda: nc.vector.tensor_tensor(rstd.t[:, :n], mean.t[:, :n], mean.t[:, :n], ALU.mult), [mean], [rstd])
        k.op("dve", lambda: nc.vector.scalar_tensor_tensor(rstd.t[:, :n], S2.t[:, :n], 1.0 / D, rstd.t[:, :n], ALU.mult, ALU.subtract), [S2, rstd], [rstd])
        k.op("dve", lambda: nc.vector.tensor_scalar(rstd.t[:, :n], rstd.t[:, :n], EPS, None, ALU.add), [rstd], [rstd])
        k.op("act", lambda: nc.scalar.activation(rstd.t[:, :n], rstd.t[:, :n], AF.Sqrt), [rstd], [rstd])
        k.op("dve", lambda: nc.vector.reciprocal(rstd.t[:, :n], rstd.t[:, :n]), [rstd], [rstd])
        for dc in range(KC):
            self.Xc[dc].w = list(X.w)
            self.Xc[dc].r = list(X.r)
        for dc in range(KC):
            en, ee_ = ("dve", nc.vector)
            Xc = self.Xc[dc]
            k.op(en, lambda: ee_.tensor_tensor(X.t[:, dc, :n], X.t[:, dc, :n], mean.t[:, :n], ALU.subtract), [Xc, mean], [Xc])
            k.op(en, lambda: ee_.tensor_tensor(X.t[:, dc, :n], X.t[:, dc, :n], rstd.t[:, :n], ALU.mult), [Xc, rstd], [Xc])
            k.op(en, lambda: ee_.tensor_scalar(X.t[:, dc, :n], X.t[:, dc, :n], self.ln.t[:, L, which, 0, dc:dc + 1], self.ln.t[:, L, which, 1, dc:dc + 1], ALU.mult, ALU.add),
                 [Xc, self.ln], [Xc])
        X.w = []
        X.r = []
        for dc in range(KC):
            X.w.extend(self.Xc[dc].w)
            self.Xc[dc].w = []
            self.Xc[dc].r = []

    def modulate(self, U, X, n, L, js, jb, col):
        k, nc = self.k, self.nc
        for dc in range(KC):
            k.op("act", lambda: nc.scalar.activation(U.t[:, dc, :n], X.t[:, dc, :n], AF.Identity, bias=self.mv(L, jb, dc, col), scale=self.mv(L, js, dc, col)),
                 [X, self.mod], [U])

    def store_v_tok(self, ps, n, t0, head, st):
        k, nc = self.k, self.nc
        vt, vtok = st["vt"], st["vtok"][head % 2]
        k.op("act", lambda: nc.scalar.copy(vt.t[:, :n], ps.t[:, :n]), [ps], [vt])
        npc = n // 128
        for pc in range(npc):
            k.transpose(self.PT, self.PT.t[:, pc * 128:(pc + 1) * 128], vt, vt.t[:, pc * 128:(pc + 1) * 128], self.ident)
        k.op("dve", lambda: nc.vector.tensor_copy(vtok.t[:, :npc, :], self.PT.t[:, :npc * 128].rearrange("p (a b) -> p a b", b=128)), [self.PT], [vtok])
        k.dma("sp", self.VK, self.VK.t[head, t0:t0 + n, :].rearrange("(a p) d -> p a d", p=128), vtok, vtok.t[:, :npc, :])

    def inproj(self, L, U, n, t0, st):
        k, nc = self.k, self.nc
        m = L % 4
        W = self.wcache[("in", L)]
        rhs = lambda kc: U.t[:, kc, :n]
        if m == 0 or m == 1:
            rope = self.ropeA if m == 0 else self.ropeB
            cs = st["rope"]
            k.dma("sp", cs, cs.t[:, :, :n], rope, rope.t[:, :, t0:t0 + n].rearrange("a p t -> p a t"), acc=False)
            nq = NH
            nk = NH if m == 0 else 2
            nqk = nq + nk
            hold = {}

            def epi(ci, c, ps):
                if ci < 2 * nqk:
                    idx, sw = ci // 2, ci % 2
                    if sw == 0:
                        hold["p"] = ps
                        return
                    p1, p2 = hold["p"], ps
                    isq = idx < nq
                    dst = self.QT if isq else self.KT
                    dch = idx if isq else idx - nq
                    t1, t2, ob = st["t1"], st["t2"], st["ob"][idx % 2]
                    if m == 0:
                        k.op("dve", lambda: nc.vector.tensor_tensor(t1.t[:, :n], p1.t[:, :n], cs.t[:, 0, :n], ALU.mult), [p1, cs], [t1])
                        k.op("dve", lambda: nc.vector.tensor_tensor(t2.t[:, :n], p2.t[:, :n], cs.t[:, 1, :n], ALU.mult), [p2, cs], [t2])
                        k.op("dve", lambda: nc.vector.tensor_tensor(ob.t[:, :n], t1.t[:, :n], t2.t[:, :n], ALU.add), [t1, t2], [ob])
                    else:
                        g0 = 0 if isq else 2
                        R = self.P[5]
                        k.op("act", lambda: nc.scalar.activation(t1.t[:, :n], p1.t[:, :n], AF.Square), [p1], [t1])
                        k.mm(R, [(self.ones_f.t[:], t1.t[:, :n])], [self.ones_f, t1], out_ap=R.t[:, :n])
                        rs = st["rs"]
                        k.op("dve", lambda: nc.vector.tensor_scalar(rs.t[:, :n], R.t[:, :n], 1.0 / 128, EPS, ALU.mult, ALU.add), [R], [rs])
                        k.op("act", lambda: nc.scalar.activation(rs.t[:, :n], rs.t[:, :n], AF.Sqrt), [rs], [rs])
                        k.op("dve", lambda: nc.vector.reciprocal(rs.t[:, :n], rs.t[:, :n]), [rs], [rs])
                        k.op("dve", lambda: nc.vector.scalar_tensor_tensor(t1.t[:, :n], p1.t[:, :n], st["bg"].t[:, g0:g0 + 1], cs.t[:, 0, :n], ALU.mult, ALU.mult), [p1, cs, st["bg"]], [t1])
                        k.op("dve", lambda: nc.vector.scalar_tensor_tensor(t2.t[:, :n], p2.t[:, :n], st["bg"].t[:, g0 + 1:g0 + 2], cs.t[:, 1, :n], ALU.mult, ALU.mult), [p2, cs, st["bg"]], [t2])
                        k.op("dve", lambda: nc.vector.tensor_tensor(t1.t[:, :n], t1.t[:, :n], t2.t[:, :n], ALU.add), [t1, t2], [t1])
                        k.op("dve", lambda: nc.vector.tensor_tensor(ob.t[:, :n], t1.t[:, :n], rs.t[:, :n], ALU.mult), [t1, rs], [ob])
                    k.dma("sp", dst, dst.t[dch][:, t0:t0 + n], ob, ob.t[:, :n])
                else:
                    self.store_v_tok(ps, n, t0, ci - 2 * nqk, st)
            nch = 2 * nqk + (NH if m == 0 else 2)
            self.gemm(W, list(range(nch)), 1, rhs, [U], n, epi, self.P[0:3])
        elif m == 3:
            def epi(ci, c, ps):
                if ci < 2 * NH:
                    dst = self.QT if ci < NH else self.KT
                    ob = st["ob"][ci % 2]
                    k.op("act", lambda: nc.scalar.copy(ob.t[:, :n], ps.t[:, :n]), [ps], [ob])
                    k.dma("sp", dst, dst.t[ci % NH][:, t0:t0 + n], ob, ob.t[:, :n])
                else:
                    self.store_v_tok(ps, n, t0, ci - 2 * NH, st)
            self.gemm(W, list(range(3 * NH)), 1, rhs, [U], n, epi, self.P[0:3])
        else:
            self.inproj_mamba(W, rhs, U, n, t0, st)

    def make_st(self):
        k = self.k
        st = {
            "mean": k.sb([128, 512], F32), "rstd": k.sb([128, 512], F32),
            "t1": k.sb([128, 512], F32), "t2": k.sb([128, 512], F32), "rs": k.sb([128, 512], F32),
            "ob": [k.sb([128, 512], BF16), k.sb([128, 512], BF16)],
            "rope": k.sb([128, 2, 512], F32), "vt": k.sb([128, 512], BF16),
            "vtok": [k.sb([128, 4, 128], BF16), k.sb([128, 4, 128], BF16)],
            "bg": k.sb([128, 4], F32), "ng": k.sb([128, 16], F32),
        }
        k.dma("sp", st["bg"], st["bg"].t[:], self.b_g, self.b_g.t[:])
        k.dma("sp", st["ng"], st["ng"].t[:], self.c_norm_gT, self.c_norm_gT.t[:])
        return st

    def inproj_all(self, L):
        k, nc = self.k, self.nc
        mark = k.mark()
        st = self.make_st()
        Us = [k.sb([128, KC, 512], BF16, "Ua%d" % i) for i in range(2)]
        for bi, (r, l0, n, t0, col) in enumerate(ALL_BLOCKS):
            U = Us[bi % 2]
            j = [x[0] for x in OWN_BLOCKS].index(l0)
            UGv = self.UG[j].t.bitcast(BF16).rearrange("(r c p) t -> r p c t", r=2, p=128)
            k.dma("sp", U, U.t[:, :, :n], self.UG[j], UGv[r], acc=False)
            self.inproj(L, U, n, t0, st)
        k.release(mark)

    def outproj_all(self, L):
        k, nc = self.k, self.nc
        m = L % 4
        mark = k.mark()
        kco = 16 if m == 2 else NH
        st = self.make_st()
        As = [k.sb([128, kco, 512], BF16, "Aa%d" % i) for i in range(2)]
        yo = [k.sb([128, 512], F32, "yo%d" % i) for i in range(3)]
        Wo = self.wcache[("out", L)]
        YPv = [self.YP[j].t.rearrange("(r c p) t -> r c p t", r=2, p=128) for j in range(9)]
        yi = 0

        def run_block(bi, blk, chunks):
            nonlocal yi
            (r, l0, n, t0, col) = blk
            A = As[bi % 2]
            k.dma("sp", A, A.t[:, :, :n], self.AO, self.AO.t[0:kco, :, t0:t0 + n].rearrange("c p t -> p c t"), acc=False)
            if m == 2:
                R = self.P[5]
                for c in range(16):
                    tq = st["t1"] if c % 2 == 0 else st["t2"]
                    k.op("act", lambda: nc.scalar.activation(tq.t[:, :n], A.t[:, c, :n], AF.Square), [A], [tq])
                    k.mm(R, [(self.ones_f.t[:], tq.t[:, :n])], [self.ones_f, tq], first=(c == 0), last=(c == 15), out_ap=R.t[:, :n])
                y = yo[yi % 3]
                yi += 1
                k.op("act", lambda: nc.scalar.copy(y.t[:, :n], R.t[:, :n]), [R], [y])
                k.dma("sp", self.YP[8], YPv[8][r][0][:, l0:l0 + n], y, y.t[:, :n])
                for c in range(16):
                    k.op("dve", lambda: nc.vector.tensor_scalar(A.t[:, c, :n], A.t[:, c, :n], st["ng"].t[:, c:c + 1], None, ALU.mult), [A, st["ng"]], [A])

            def epi(ci, c, ps):
                nonlocal yi
                y = yo[yi % 3]
                yi += 1
                k.op("act", lambda: nc.scalar.copy(y.t[:, :n], ps.t[:, :n]), [ps], [y])
                k.dma("sp", self.YP[c // 2], YPv[c // 2][r][c % 2][:, l0:l0 + n], y, y.t[:, :n])
            self.gemm(Wo, chunks, 1, lambda kc: A.t[:, kc, :n], [A], n, epi, self.P[0:3], kper=kco)

        if m == 2:
            for bi, blk in enumerate(ALL_BLOCKS):
                run_block(bi, blk, list(range(16)))
            for j in range(9):
                k.collective("ReduceScatter", ALU.add, self.YP[j], self.YP[j].t[:], self.YR[j], self.YR[j].t[:])
        else:
            AR = k.sb([128, NH, T], BF16, "AOres")
            for c in range(NH):
                k.dma("sp", AR, AR.t[:, c, :], self.AO, self.AO.t[c], acc=(c > 0))
            wq = {}

            def loadw(j):
                ws = []
                for ci in range(2):
                    wb = self.wb[self.wbi % NWB]
                    self.wbi += 1
                    k.dma("sp", wb, wb.t[:, :NH, :], Wo, Wo.t[2 * j + ci][:, 0:NH, :], acc=False)
                    ws.append(wb)
                wq[j] = ws
            loadw(0)
            pi = 0
            for j in range(8):
                if j + 1 < 8:
                    loadw(j + 1)
                ws = wq.pop(j)
                for (r, l0, n, t0, col) in ALL_BLOCKS:
                    for ci in range(2):
                        ps = self.P[pi % 3]
                        pi += 1
                        k.mm(ps, [(ws[ci].t[:, i, :], AR.t[:, i, t0:t0 + n]) for i in range(NH)], [ws[ci], AR], out_ap=ps.t[:, :n])
                        y = yo[yi % 3]
                        yi += 1
                        k.op("act", lambda: nc.scalar.copy(y.t[:, :n], ps.t[:, :n]), [ps], [y])
                        k.dma("sp", self.YP[j], YPv[j][r][ci][:, l0:l0 + n], y, y.t[:, :n])
                k.collective("ReduceScatter", ALU.add, self.YP[j], self.YP[j].t[:], self.YR[j], self.YR[j].t[:])
        k.release(mark)

    def phase_b(self, L):
        k, nc = self.k, self.nc
        last = (L == self.nlayers - 1)
        mark = k.mark()
        X = k.sb([128, KC, 512], F32, "X")
        U = k.sb([128, KC, 512], BF16, "U")
        st = {
            "mean": k.sb([128, 512], F32), "rstd": k.sb([128, 512], F32),
            "t1": k.sb([128, 512], F32), "t2": k.sb([128, 512], F32), "rs": k.sb([128, 512], F32),
        }
        ys = [k.sb([128, 512], F32, "ys%d" % i) for i in range(3)]
        YRv = [self.YR[j].t.rearrange("(c p) t -> c p t", p=128) for j in range(9)]
        if L >= 0:
            H = k.sb([128, 64, 512], BF16, "H")
            W1 = self.wcache[("w1", L)]
            W2 = self.wcache[("w2", L)]
        for (l0, n, col) in OWN_BLOCKS:
            if last and col == 1 and self.nlayers == DEPTH:
                continue
            src = self.XT if L >= 0 else self.xT
            k.dma("sp", X, X.t[:, :, :n], src, src.t[:, :, l0:l0 + n].rearrange("c p t -> p c t"), acc=False)
            if L >= 0:
                k.op("act", lambda: nc.scalar.mul(X.t[:, :, :n], X.t[:, :, :n], DN_ALPHA), [X], [X])
                if L % 4 == 2:
                    rs = st["rs"]
                    k.dma("sp", rs, rs.t[:, :n], self.YR[8], YRv[8][0][:, l0:l0 + n], acc=False)
                    k.op("dve", lambda: nc.vector.tensor_scalar(rs.t[:, :n], rs.t[:, :n], 1.0 / C_DI, EPS, ALU.mult, ALU.add), [rs], [rs])
                    k.op("act", lambda: nc.scalar.activation(rs.t[:, :n], rs.t[:, :n], AF.Sqrt), [rs], [rs])
                    k.op("dve", lambda: nc.vector.reciprocal(rs.t[:, :n], rs.t[:, :n]), [rs], [rs])
                for c in range(KC):
                    y = ys[c % 3]
                    k.dma("sp", y, y.t[:, :n], self.YR[c // 2], YRv[c // 2][c % 2][:, l0:l0 + n], acc=False)
                    if L % 4 == 2:
                        k.op("dve", lambda: nc.vector.tensor_tensor(y.t[:, :n], y.t[:, :n], st["rs"].t[:, :n], ALU.mult), [y, st["rs"]], [y])
                    k.op("dve", lambda: nc.vector.scalar_tensor_tensor(X.t[:, c, :n], y.t[:, :n], self.mv(L, 2, c, col), X.t[:, c, :n], ALU.mult, ALU.add),
                         [y, X, self.mod], [X])
                self.layer_norm(X, n, L, 0, st["t1"], st)
                self.modulate(U, X, n, L, 4, 3, col)
                k.op("act", lambda: nc.scalar.mul(X.t[:, :, :n], X.t[:, :, :n], DN_ALPHA), [X], [X])

                def epi_1(ci, c, ps):
                    tmp = st["t1"] if ci % 2 == 0 else st["t2"]
                    k.op("act", lambda: nc.scalar.activation(tmp.t[:, :n], ps.t[:, :n], AF.Relu), [ps], [tmp])
                    k.op("dve", lambda: nc.vector.tensor_tensor(H.t[:, c, :n], tmp.t[:, :n], tmp.t[:, :n], ALU.mult), [tmp], [H], acc=True)
                self.gemm(W1, list(range(64)), 1, lambda kc: U.t[:, kc, :n], [U], n, epi_1, self.P[0:3])

                def epi_2(ci, c, ps):
                    k.op("dve", lambda: nc.vector.scalar_tensor_tensor(X.t[:, c, :n], ps.t[:, :n], self.mv(L, 5, c, col), X.t[:, c, :n], ALU.mult, ALU.add),
                         [ps, X, self.mod], [X])
                self.gemm(W2, list(range(16)), 4, lambda kc: H.t[:, kc, :n], [H], n, epi_2, self.P[0:3])
                self.layer_norm(X, n, L, 1, st["t1"], st)
            if last:
                if col == 0:
                    k.dma("sp", self.out, self.out.t[:, :, l0 - 128:l0 - 128 + n].rearrange("c p t -> p c t"), X, X.t[:, :, :n])
                else:
                    k.dma("sp", self.out_ctx, self.out_ctx.t[:, :, 0:n].rearrange("c p t -> p c t"), X, X.t[:, :, :n])
            else:
                k.dma("sp", self.XT, self.XT.t[:, :, l0:l0 + n].rearrange("c p t -> p c t"), X, X.t[:, :, :n])
                self.modulate(U, X, n, L + 1, 1, 0, col)
                j = [x[0] for x in OWN_BLOCKS].index(l0)
                UGiv = self.UGi[j].t.bitcast(BF16).rearrange("(c p) t -> p c t", p=128)
                k.dma("sp", self.UGi[j], UGiv, U, U.t[:, :, :n])
                k.collective("AllGather", ALU.bypass, self.UGi[j], self.UGi[j].t[:], self.UG[j], self.UG[j].t[:])
        k.release(mark)

    def phase_a_dense(self, L):
        k, nc = self.k, self.nc
        m = L % 4
        last = (L == self.nlayers - 1) and self.nlayers == DEPTH
        nmap = 2 if m == 0 else 1
        dk = 64 if m == 0 else 128
        scale = dk ** -0.5
        mark = k.mark()
        KTs = [k.sb([128, T], BF16, "KTs%d" % i) for i in range(2)]
        QTs = [k.sb([128, T], BF16, "QTs%d" % i) for i in range(2)]
        Vs = [k.sb([128, 34, 128], BF16, "Vs%d" % i) for i in range(2)]
        Pb = [k.sb([128, 512], BF16, "Pb%d" % i) for i in range(3)]
        rz = k.sb([128, 512], F32, "rz")
        Am = [k.sb([128, 512], F32, "Am%d" % i) for i in range(2)]
        sq = k.sb([128, 512], F32, "sq")
        rs = k.sb([128, 512], F32, "rs")
        aob = [k.sb([128, 512], BF16, "aob%d" % i) for i in range(2)]
        Sps = [self.P[0], self.P[1], self.P[2]]
        Ops = [self.P[3], self.P[5]]
        Zps = [self.P[4], self.P[6]]
        if m == 0:
            lam_init = 0.8 - 0.6 * math.exp(-0.3 * L)
            lv = k.sb([1, 256], F32, "lv")
            k.dma("sp", lv, lv.t[:], self.a_lam, self.a_lam.t[:])
            pr = k.sb([1, 128], F32, "pr")
            k.op("dve", lambda: nc.vector.tensor_tensor(pr.t[:], lv.t[:, 0:128], lv.t[:, 128:256], ALU.mult), [lv], [pr])
            s2 = k.sb([1, 2], F32, "s2")
            k.op("dve", lambda: nc.vector.reduce_sum(s2.t[:], pr.t[:].rearrange("p (a b) -> p a b", b=64), mybir.AxisListType.X), [pr], [s2])
            k.op("act", lambda: nc.scalar.activation(s2.t[:], s2.t[:], AF.Exp), [s2], [s2])
            l1 = k.sb([1, 1], F32, "l1")
            k.op("dve", lambda: nc.vector.tensor_tensor(l1.t[:], s2.t[:, 1:2], s2.t[:, 0:1], ALU.subtract), [s2], [l1])
            k.op("dve", lambda: nc.vector.tensor_scalar(l1.t[:], l1.t[:], -lam_init, None, ALU.add), [l1], [l1])
            k.mm(self.P[0], [(self.ones_f.t[0:1, :], l1.t[:])], [self.ones_f, l1], out_ap=self.P[0].t[:, 0:1])
            neglam = k.sb([128, 1], F32, "neglam")
            k.op("dve", lambda: nc.vector.tensor_copy(neglam.t[:], self.P[0].t[:, 0:1]), [self.P[0]], [neglam])
            sg = k.sb([128, 1], F32, "sg")
            k.dma("sp", sg, sg.t[:], self.a_sub_g, self.a_sub_g.t[:])
            k.op("dve", lambda: nc.vector.tensor_scalar(sg.t[:], sg.t[:], 1.0 - lam_init, None, ALU.mult), [sg], [sg])
        PTf = Trk(self.PT.t[:].bitcast(F32), "PTf")
        Sg = [(self.P[0], self.P[1]), (self.P[2], PTf)]
        Pb = [k.sb([128, 512], BF16, "Pq%d" % i) for i in range(4)]
        sqs = [k.sb([128, 512], F32, "sq%d" % i) for i in range(2)]
        A0s = [k.sb([128, 512], F32, "A0s%d" % i) for i in range(2)]
        pbi = 0
        pending = []
        epi_i = 0

        def flush():
            while pending:
                pending.pop(0)()

        def make_epi(h, t0, n, A0, sq_):
            def run():
                R = self.P[6]
                k.op("act", lambda: nc.scalar.activation(sq_.t[:, :n], A0.t[:, :n], AF.Square), [A0], [sq_])
                k.mm(R, [(self.ones_f.t[:], sq_.t[:, :n])], [self.ones_f, sq_], out_ap=R.t[:, :n])
                k.op("dve", lambda: nc.vector.tensor_scalar(rs.t[:, :n], R.t[:, :n], 1.0 / 128, EPS, ALU.mult, ALU.add), [R], [rs])
                k.op("act", lambda: nc.scalar.activation(rs.t[:, :n], rs.t[:, :n], AF.Sqrt), [rs], [rs])
                k.op("dve", lambda: nc.vector.reciprocal(rs.t[:, :n], rs.t[:, :n]), [rs], [rs])
                ob = aob[h % 2]
                k.op("dve", lambda: nc.vector.scalar_tensor_tensor(ob.t[:, :n], A0.t[:, :n], sg.t[:, 0:1], rs.t[:, :n], ALU.mult, ALU.mult), [A0, sg, rs], [ob])
                k.dma("sp", self.AO, self.AO.t[h][:, t0:t0 + n], ob, ob.t[:, :n])
            return run

        for h in range(NH):
            kvh = h if m == 0 else h // 4
            Qt = QTs[h % 2]
            if m == 0 or h % 4 == 0:
                Kc, Vc = KTs[kvh % 2], Vs[kvh % 2]
                k.dma("sp", Kc, Kc.t[:], self.KT, self.KT.t[kvh], acc=False)
                k.dma("sp", Vc, Vc.t[:], self.VK, self.VK.t[kvh].rearrange("(a p) d -> p a d", p=128), acc=False)
            Kt, Vt = KTs[kvh % 2], Vs[kvh % 2]
            k.dma("sp", Qt, Qt.t[:], self.QT, self.QT.t[h], acc=False)
            for qi, (t0, n, col) in enumerate(BLOCKS):
                if last and col == 1:
                    continue
                kbs = list(range(2)) if col == 1 else list(range(34))
                groups = [kbs[i:i + 2] for i in range(0, len(kbs), 2)]
                for mp in range(nmap):
                    pr0 = mp * dk if m == 0 else 0
                    oz = mp if m == 0 else qi % 2
                    O, Z = Ops[oz], Zps[oz]

                    def emit_S(gi):
                        banks = Sg[gi % 2]
                        items = [(banks[j], banks[j].t[:, :n], Kt.t[pr0:pr0 + dk, kb * 128:(kb + 1) * 128], Qt.t[pr0:pr0 + dk, t0:t0 + n], True, True, True)
                                 for j, kb in enumerate(groups[gi])]
                        k.mmg(items, [Kt, Qt])
                    emit_S(0)
                    ng_ = len(groups)
                    for gi, g in enumerate(groups):
                        if gi + 1 < ng_:
                            emit_S(gi + 1)
                        if gi == 1 and mp == 0:
                            flush()
                        banks = Sg[gi % 2]
                        Ps = []
                        for j, kb in enumerate(g):
                            P_ = Pb[pbi % 4]
                            pbi += 1
                            S = banks[j]
                            k.op("act", lambda: nc.scalar.activation(P_.t[:, :n], S.t[:, :n], AF.Exp, scale=scale), [S], [P_])
                            Ps.append(P_)
                        items = []
                        for j, kb in enumerate(g):
                            f_ = (gi == 0 and j == 0)
                            l_ = (gi == ng_ - 1 and j == len(g) - 1)
                            items.append((O, O.t[:, :n], Vt.t[:, kb, :], Ps[j].t[:, :n], f_, l_, f_))
                        for j, kb in enumerate(g):
                            f_ = (gi == 0 and j == 0)
                            l_ = (gi == ng_ - 1 and j == len(g) - 1)
                            items.append((Z, Z.t[:, :n], self.ones_b.t[:], Ps[j].t[:, :n], f_, l_, f_))
                        k.mmg(items, [Vt, self.ones_b] + Ps)
                    k.op("dve", lambda: nc.vector.reciprocal(rz.t[:, :n], Z.t[:, :n]), [Z], [rz])
                    if m == 0:
                        A_ = Am[mp] if mp == 1 else A0s[epi_i % 2]
                        k.op("dve", lambda: nc.vector.tensor_tensor(A_.t[:, :n], O.t[:, :n], rz.t[:, :n], ALU.mult), [O, rz], [A_])
                    else:
                        ob = aob[pbi % 2]
                        k.op("dve", lambda: nc.vector.tensor_tensor(ob.t[:, :n], O.t[:, :n], rz.t[:, :n], ALU.mult), [O, rz], [ob])
                        k.dma("sp", self.AO, self.AO.t[h][:, t0:t0 + n], ob, ob.t[:, :n])
                if m == 0:
                    A0, A1 = A0s[epi_i % 2], Am[1]
                    k.op("dve", lambda: nc.vector.scalar_tensor_tensor(A0.t[:, :n], A1.t[:, :n], neglam.t[:, 0:1], A0.t[:, :n], ALU.mult, ALU.add), [A0, A1, neglam], [A0])
                    pending.append(make_epi(h, t0, n, A0, sqs[epi_i % 2]))
                    epi_i += 1
        flush()
        k.release(mark)


    def phase_a_na(self, L):
        k, nc = self.k, self.nc
        scale = 128 ** -0.5
        mark = k.mark()
        Kt = k.sb([128, T], BF16, "naK")
        Qt = k.sb([128, T], BF16, "naQ")
        Ve = k.sb([128, 34, 128], BF16, "naVe")
        Vo = k.sb([128, 31, 128], BF16, "naVo")
        EE = k.sb([128, 14, 64], F32, "naEE")
        MK = k.sb([128, 14, 64], F32, "naMK")
        sbS = [k.sb([128, 256], F32, "naS%d" % i) for i in range(2)]
        Pb = [k.sb([128, 6, 64], BF16, "naP%d" % i) for i in range(2)]
        rz = k.sb([128, 512], F32, "narz")
        aob = [k.sb([128, 512], BF16, "naob%d" % i) for i in range(2)]
        k.dma("sp", MK, MK.t[:], self.d_mask, self.d_mask.t[:])
        Sps = [self.P[0], self.P[1], self.P[2]]
        OZ = [(self.P[3], self.P[4]), (self.P[5], self.P[6])]
        for h in range(NH):
            k.dma("sp", Kt, Kt.t[:], self.KT, self.KT.t[h], acc=False)
            k.dma("sp", Qt, Qt.t[:], self.QT, self.QT.t[h], acc=False)
            k.dma("sp", Ve, Ve.t[:], self.VK, self.VK.t[h].rearrange("(a p) d -> p a d", p=128), acc=False)
            k.dma("sp", Vo, Vo.t[:], self.VK, self.VK.t[h, T_CTX + 64:T_CTX + 64 + 31 * 128, :].rearrange("(a p) d -> p a d", p=128), acc=False)
            k.dma("sp", EE, EE.t[:], self.d_ee, self.d_ee.t[h], acc=False)
            k.op("dve", lambda: nc.vector.tensor_tensor(EE.t[:], EE.t[:], MK.t[:], ALU.add), [EE, MK], [EE])

            def keycols(r, slot):
                rs_ = min(max(r - 4, 0), 56)
                if slot < 4:
                    s0 = T_CTX + rs_ * 64 + 128 * slot
                else:
                    s0 = 128 * (slot - 4)
                return s0

            def vblock(r, slot):
                rs_ = min(max(r - 4, 0), 56)
                if slot >= 4:
                    return Ve.t[:, slot - 4, :]
                if rs_ % 2 == 0:
                    return Ve.t[:, 2 + rs_ // 2 + slot, :]
                return Vo.t[:, (rs_ - 1) // 2 + slot, :]

            def smm(r):
                S = Sps[r % 3]
                for slot in range(6):
                    s0 = keycols(r, slot)
                    k.mm(S, [(Kt.t[:, s0:s0 + 128], Qt.t[:, T_CTX + r * 64:T_CTX + (r + 1) * 64])], [Kt, Qt],
                         out_ap=S.t[:, slot * 64:(slot + 1) * 64])
            smm(0)
            for r in range(64):
                if r + 1 < 64:
                    smm(r + 1)
                S = Sps[r % 3]
                rs_ = min(max(r - 4, 0), 56)
                d0 = rs_ - r + 7
                sb_, P_ = sbS[r % 2], Pb[r % 2]
                k.op("dve", lambda: nc.vector.scalar_tensor_tensor(sb_.t[:].rearrange("p (a b) -> p a b", b=64), S.t[:, 0:256].rearrange("p (a b) -> p a b", b=64),
                                                                  scale, EE.t[:, d0:d0 + 7:2, :], ALU.mult, ALU.add), [S, EE], [sb_])
                k.op("act", lambda: nc.scalar.activation(P_.t[:, 0:4, :], sb_.t[:].rearrange("p (a b) -> p a b", b=64), AF.Exp), [sb_], [P_])
                k.op("act", lambda: nc.scalar.activation(P_.t[:, 4:6, :], S.t[:, 256:384].rearrange("p (a b) -> p a b", b=64), AF.Exp, scale=scale), [S], [P_], acc=True)
                O, Z = OZ[(r // 8) % 2]
                rs8 = r % 8
                for slot in range(6):
                    k.mm(O, [(vblock(r, slot), P_.t[:, slot, :])], [Ve, Vo, P_], first=(slot == 0), last=(slot == 5), out_ap=O.t[:, rs8 * 64:(rs8 + 1) * 64])
                for slot in range(6):
                    k.mm(Z, [(self.ones_b.t[:], P_.t[:, slot, :])], [self.ones_b, P_], first=(slot == 0), last=(slot == 5), out_ap=Z.t[:, rs8 * 64:(rs8 + 1) * 64])
                if rs8 == 7:
                    ob = aob[(r // 8) % 2]
                    k.op("dve", lambda: nc.vector.reciprocal(rz.t[:], Z.t[:]), [Z], [rz])
                    k.op("dve", lambda: nc.vector.tensor_tensor(ob.t[:], O.t[:], rz.t[:], ALU.mult), [O, rz], [ob])
                    tq = T_CTX + (r - 7) * 64
                    k.dma("sp", self.AO, self.AO.t[h][:, tq:tq + 512], ob, ob.t[:])
        k.release(mark)


    def inproj_mamba(self, W, rhs, U, n, t0, st):
        k, nc = self.k, self.nc

        def epi(ci, c, ps):
            tmp = st["t1"] if ci % 2 == 0 else st["t2"]
            k.op("act", lambda: nc.scalar.copy(tmp.t[:, :n], ps.t[:, :n]), [ps], [tmp])
            if c < 16:
                k.dma("sp", self.ZT, self.ZT.t[c][:, t0:t0 + n], tmp, tmp.t[:, :n])
            elif c < 40:
                k.dma("sp", self.XBC, self.XBC.t[c - 16][:, t0:t0 + n], tmp, tmp.t[:, :n])
            else:
                k.dma("sp", self.DTR, self.DTR.t[:, t0:t0 + n], tmp, tmp.t[0:64, :n])
        self.gemm(W, list(range(41)), 1, rhs, [U], n, epi, self.P[0:3])

    def phase_a_mamba(self, L):
        k, nc = self.k, self.nc
        NCH = T // 128
        mark = k.mark()
        tri = k.sb([128, 2, 128], F32, "tri")
        k.dma("sp", tri, tri.t[:], self.c_tri, self.c_tri.t.rearrange("a p q -> p a q"))
        identf = k.sb([128, 128], F32, "identf")
        k.op("dve", lambda: nc.vector.tensor_copy(identf.t[:], self.ident.t[:]), [self.ident], [identf])
        identf64 = k.sb([64, 64], F32, "identf64")
        k.op("dve", lambda: nc.vector.tensor_copy(identf64.t[:], self.ident.t[0:64, 0:64]), [self.ident], [identf64])
        hp = k.sb([64, 4], F32, "hp")
        k.dma("sp", hp, hp.t[:], self.c_hp, self.c_hp.t[:])
        DTt = k.sb([128, NCH, 64], F32, "DTt")
        DTAt = k.sb([128, NCH, 64], F32, "DTAt")
        CS = k.sb([128, NCH, 64], F32, "CS")
        Dcol = k.sb([64, 32], F32, "Dcol")
        m2 = k.mark()
        dtr = k.sb([64, T], F32, "dtr")
        ab = k.sb([64, T], F32, "ab")
        na = k.sb([64, 1], F32, "na")
        k.dma("sp", dtr, dtr.t[:], self.DTR, self.DTR.t[:])
        k.op("dve", lambda: nc.vector.tensor_scalar(dtr.t[:], dtr.t[:], hp.t[:, 1:2], None, ALU.add), [dtr, hp], [dtr])
        k.op("act", lambda: nc.scalar.activation(ab.t[:], dtr.t[:], AF.Abs), [dtr], [ab])
        k.op("act", lambda: nc.scalar.activation(ab.t[:], ab.t[:], AF.Exp, scale=-1.0), [ab], [ab])
        k.op("act", lambda: nc.scalar.activation(ab.t[:], ab.t[:], AF.Ln, bias=1.0), [ab], [ab])
        k.op("dve", lambda: nc.vector.tensor_scalar(dtr.t[:], dtr.t[:], 0.0, None, ALU.max), [dtr], [dtr])
        k.op("dve", lambda: nc.vector.tensor_tensor(dtr.t[:], dtr.t[:], ab.t[:], ALU.add), [dtr, ab], [dtr])
        k.op("act", lambda: nc.scalar.activation(na.t[:], hp.t[:, 0:1], AF.Exp), [hp], [na])
        k.op("dve", lambda: nc.vector.tensor_scalar(na.t[:], na.t[:], -1.0, None, ALU.mult), [na], [na])
        k.op("dve", lambda: nc.vector.tensor_scalar(ab.t[:], dtr.t[:], na.t[:, 0:1], None, ALU.mult), [dtr, na], [ab])
        for (src, dst) in ((dtr, DTt), (ab, DTAt)):
            for c0 in range(0, NCH, 4):
                nn = min(4, NCH - c0)
                Pt = self.P[(c0 // 4) % 3]
                for i in range(nn):
                    c = c0 + i
                    k.transpose(Pt, Pt.t[:, i * 64:(i + 1) * 64], src, src.t[:, c * 128:(c + 1) * 128], identf64)
                k.op("dve", lambda: nc.vector.tensor_copy(dst.t[:, c0:c0 + nn, :], Pt.t[:, :nn * 64].rearrange("p (a b) -> p a b", b=64)), [Pt], [dst], acc=True)
        for c0 in range(0, NCH, 4):
            nn = min(4, NCH - c0)
            Pt = self.P[(c0 // 4) % 3]
            for i in range(nn):
                for d in range(2):
                    k.mm(Pt, [(tri.t[:, d, :], DTAt.t[:, c0 + i, d * 32:(d + 1) * 32])], [tri, DTAt], out_ap=Pt.t[:, i * 64 + d * 32:i * 64 + (d + 1) * 32])
            k.op("dve", lambda: nc.vector.tensor_copy(CS.t[:, c0:c0 + nn, :], Pt.t[:, :nn * 64].rearrange("p (a b) -> p a b", b=64)), [Pt], [CS], acc=True)
        sel = k.sb([64, 32], F32, "sel")
        k.op("dve", lambda: nc.vector.tensor_tensor(sel.t[:], identf.t[0:64, 0:32], identf.t[0:64, 32:64], ALU.add), [identf], [sel])
        k.op("dve", lambda: nc.vector.tensor_scalar(sel.t[:], sel.t[:], hp.t[:, 2:3], None, ALU.mult), [sel, hp], [sel])
        k.mm(self.P[3], [(self.ones_f.t[0:64, 0:64], sel.t[:])], [self.ones_f, sel], out_ap=self.P[3].t[0:64, 0:32])
        k.op("dve", lambda: nc.vector.tensor_copy(Dcol.t[:], self.P[3].t[0:64, 0:32]), [self.P[3]], [Dcol])
        k.release(m2)
        m2 = k.mark()
        cw = k.sb([128, 24, 5], F32, "cw")
        cb = k.sb([128, 24], F32, "cb")
        k.dma("sp", cw, cw.t[:], self.c_conv_wT, self.c_conv_wT.t[:])
        k.dma("sp", cb, cb.t[:], self.c_conv_bT, self.c_conv_bT.t[:])
        xins = [k.sb([128, T], F32, "xin%d" % i) for i in range(3)]
        accs = [k.sb([128, T], F32, "acc%d" % i) for i in range(3)]
        obs = [k.sb([128, T], BF16, "cob%d" % i) for i in range(2)]
        for c in range(24):
            xin, acc = xins[c % 3], accs[c % 3]
            k.dma("sp", xin, xin.t[:], self.XBC, self.XBC.t[c], acc=False)
            en, ee_ = ("dve", nc.vector)
            k.op(en, lambda: ee_.tensor_scalar(acc.t[:], xin.t[:], cw.t[:, c, 2:3], cb.t[:, c:c + 1], ALU.mult, ALU.add), [xin, cw, cb], [acc])
            for j in (0, 1, 3, 4):
                s_ = j - 2
                for (a_, b_) in ((0, T_CTX), (T_CTX, T)):
                    lo, hi = max(a_, a_ - s_), min(b_, b_ - s_)
                    k.op(en, lambda: ee_.scalar_tensor_tensor(acc.t[:, lo:hi], xin.t[:, lo + s_:hi + s_], cw.t[:, c, j:j + 1], acc.t[:, lo:hi], ALU.mult, ALU.add),
                         [xin, cw, acc], [acc])
            if c < 16:
                k.op("act", lambda: nc.scalar.activation(xin.t[:], acc.t[:], AF.Silu), [acc], [xin])
                k.dma("sp", self.XS, self.XS.t[c], xin, xin.t[:])
            else:
                ob = obs[c % 2]
                k.op("act", lambda: nc.scalar.activation(ob.t[:], acc.t[:], AF.Silu), [acc], [ob])
                k.dma("sp", self.BCb, self.BCb.t[c - 16], ob, ob.t[:])
        k.release(m2)
        BT = k.sb([128, T], BF16, "BT")
        CT = k.sb([128, T], BF16, "CT")
        Btok = k.sb([128, NCH, 128], BF16, "Btok")
        xb = k.sb([128, 2, T], BF16, "xb")
        xtok = k.sb([128, NCH, 256], BF16, "xtok")
        h32 = k.sb([128, 4, 64], F32, "h32")
        hbf = k.sb([128, 4, 64], BF16, "hbf")
        CBm = k.sb([128, 128], F32, "CBm")
        Dg = k.sb([128, 4, 128], F32, "Dg")
        Dm = k.sb([128, 4, 128], F32, "Dm")
        Er = k.sb([128, 4, 128], F32, "Er")
        G = k.sb([128, 4, 128], BF16, "G")
        Cd = k.sb([128, 4, 128], BF16, "Cd")
        xdt = k.sb([128, 4, 64], BF16, "xdt")
        xw = k.sb([128, 4, 64], BF16, "xw")
        wc = k.sb([128, 4], F32, "wc")
        y0s = [k.sb([64, 4, 128], F32, "y0s%d" % i) for i in range(2)]
        xc = [k.sb([64, 4, 128], F32, "xc%d" % i) for i in range(2)]
        zc = [k.sb([64, 4, 128], F32, "zc%d" % i) for i in range(2)]
        yt = k.sb([64, 4, 128], F32, "yt")
        yob = [k.sb([64, 4, 128], BF16, "yob%d" % i) for i in range(2)]
        Pcb, Pcs2, PY2, PS2 = self.P[0], (self.P[1], self.P[2]), (self.P[3], self.P[4]), (self.P[5], self.P[6])
        v3 = lambda ap: ap.rearrange("p (a b) -> p a b", b=128)
        it = 0
        for hb in range(8):
            g = hb // 2
            if hb % 2 == 0:
                k.dma("sp", BT, BT.t[:], self.BCb, self.BCb.t[g], acc=False)
                k.dma("sp", CT, CT.t[:], self.BCb, self.BCb.t[4 + g], acc=False)
                for c0 in range(0, NCH, 4):
                    nn = min(4, NCH - c0)
                    for i in range(nn):
                        c = c0 + i
                        k.transpose(self.PT, self.PT.t[:, i * 128:(i + 1) * 128], BT, BT.t[:, c * 128:(c + 1) * 128], self.ident)
                    k.op("dve", lambda: nc.vector.tensor_copy(Btok.t[:, c0:c0 + nn, :], v3(self.PT.t[:, :nn * 128])), [self.PT], [Btok], acc=True)
            k.dma("pool", xb, xb.t[:], self.XS, self.XS.t[2 * hb:2 * hb + 2].rearrange("c p t -> p c t"), acc=False)
            for c0 in range(0, NCH, 2):
                for i in range(2):
                    for a in range(2):
                        c = c0 + i
                        k.transpose(self.PT, self.PT.t[:, i * 256 + a * 128:i * 256 + (a + 1) * 128], xb, xb.t[:, a, c * 128:(c + 1) * 128], self.ident)
                k.op("dve", lambda: nc.vector.tensor_copy(xtok.t[:, c0:c0 + 2, :], self.PT.t[:, :512].rearrange("p (a b) -> p a b", b=256)), [self.PT], [xtok], acc=True)
            for d in range(2):
                k.op("dve", lambda: nc.vector.memset(h32.t[:], 0.0), [], [h32])
                k.op("dve", lambda: nc.vector.memset(hbf.t[:], 0.0), [], [hbf])
                order = list(range(NCH)) if d == 0 else [1, 0] + list(range(NCH - 1, 1, -1))
                dh0 = d * 32 + hb * 4
                e = 127 if d == 0 else 0
                for c in order:
                    it += 1
                    tok = c * 128
                    Pcs, PY, PS_ = Pcs2[it % 2], PY2[it % 2], PS2[it % 2]
                    ysl = self.YS.t[hb * 4:(hb + 1) * 4, :, tok:tok + 128].rearrange("h p t -> p h t")
                    if d == 1:
                        y0, xc_, zc_, ob = y0s[it % 2], xc[it % 2], zc[it % 2], yob[it % 2]
                        k.dma("sp", y0, y0.t[:], self.YS, ysl, acc=False)
                        k.dma("sp", xc_, xc_.t[:], self.XS, self.XS.t[2 * hb:2 * hb + 2, :, tok:tok + 128].rearrange("c (two p) t -> p (c two) t", two=2), acc=False)
                        k.dma("sp", zc_, zc_.t[:], self.ZT, self.ZT.t[2 * hb:2 * hb + 2, :, tok:tok + 128].rearrange("c (two p) t -> p (c two) t", two=2), acc=False)
                    k.mm(Pcb, [(BT.t[:, tok:tok + 128], CT.t[:, tok:tok + 128])], [BT, CT], out_ap=Pcb.t[:, 0:128])
                    k.op("dve", lambda: nc.vector.tensor_tensor(CBm.t[:], Pcb.t[:, 0:128], tri.t[:, d, :], ALU.mult), [Pcb, tri], [CBm])
                    k.op("pool", lambda: nc.gpsimd.tensor_tensor(Dg.t[:], tri.t[:, d, :].unsqueeze(1).to_broadcast([128, 4, 128]),
                                                                DTAt.t[:, c, dh0:dh0 + 4].unsqueeze(2).to_broadcast([128, 4, 128]), ALU.mult), [tri, DTAt], [Dg])
                    k.mm(Pcs, [(self.ones_f.t[:], Dg.t[:].rearrange("p a b -> p (a b)"))], [self.ones_f, Dg])
                    k.op("dve", lambda: nc.vector.tensor_tensor(Dm.t[:], v3(Pcs.t[:]), CS.t[:, c, dh0:dh0 + 4].unsqueeze(2).to_broadcast([128, 4, 128]), ALU.subtract), [Pcs, CS], [Dm])
                    k.op("dve", lambda: nc.vector.tensor_scalar(Dm.t[:], Dm.t[:], 0.0, None, ALU.min), [Dm], [Dm])
                    k.op("act", lambda: nc.scalar.activation(Dm.t[:], Dm.t[:], AF.Exp), [Dm], [Dm])
                    k.op("pool", lambda: nc.gpsimd.tensor_tensor(G.t[:], Dm.t[:], CBm.t[:].unsqueeze(1).to_broadcast([128, 4, 128]), ALU.mult), [Dm, CBm], [G])
                    k.op("act", lambda: nc.scalar.activation(Er.t[:], v3(Pcs.t[:]), AF.Exp), [Pcs], [Er])
                    k.op("pool", lambda: nc.gpsimd.tensor_tensor(Cd.t[:], Er.t[:], CT.t[:, tok:tok + 128].unsqueeze(1).to_broadcast([128, 4, 128]), ALU.mult), [Er, CT], [Cd])
                    k.op("dve", lambda: nc.vector.tensor_tensor(xdt.t[:], xtok.t[:, c, :].rearrange("p (a b) -> p a b", b=64),
                                                                DTt.t[:, c, dh0:dh0 + 4].unsqueeze(2).to_broadcast([128, 4, 64]), ALU.mult), [xtok, DTt], [xdt])
                    for j in range(4):
                        k.mm(PY, [(xdt.t[:, j, :], G.t[:, j, :]), (hbf.t[:, j, :], Cd.t[:, j, :])], [xdt, G, hbf, Cd], out_ap=PY.t[0:64, j * 128:(j + 1) * 128])
                    k.op("dve", lambda: nc.vector.tensor_tensor(wc.t[:], v3(Pcs.t[:])[:, :, e], CS.t[:, c, dh0:dh0 + 4], ALU.subtract), [Pcs, CS], [wc])
                    k.op("act", lambda: nc.scalar.activation(wc.t[:], wc.t[:], AF.Exp), [wc], [wc])
                    k.op("pool", lambda: nc.gpsimd.tensor_tensor(xw.t[:], xdt.t[:], wc.t[:].unsqueeze(2).to_broadcast([128, 4, 64]), ALU.mult), [xdt, wc], [xw])
                    k.mm(PS_, [(Btok.t[:, c, :], xw.t[:].rearrange("p a b -> p (a b)"))], [Btok, xw], out_ap=PS_.t[:, 0:256])
                    k.op("pool", lambda: nc.gpsimd.tensor_tensor(h32.t[:], h32.t[:], Er.t[:, :, e:e + 1].to_broadcast([128, 4, 64]), ALU.mult), [h32, Er], [h32])
                    k.op("dve", lambda: nc.vector.tensor_tensor(h32.t[:], h32.t[:], PS_.t[:, 0:256].rearrange("p (a b) -> p a b", b=64), ALU.add), [h32, PS_], [h32])
                    k.op("act", lambda: nc.scalar.copy(hbf.t[:], h32.t[:]), [h32], [hbf])
                    if d == 0:
                        y0 = y0s[it % 2]
                        k.op("act", lambda: nc.scalar.copy(y0.t[:], v3(PY.t[0:64, :])), [PY], [y0])
                        k.dma("sp", self.YS, ysl, y0, y0.t[:])
                    else:
                        k.op("dve", lambda: nc.vector.tensor_tensor(yt.t[:], xc_.t[:], Dcol.t[:, hb * 4:hb * 4 + 4].unsqueeze(2).to_broadcast([64, 4, 128]), ALU.mult), [xc_, Dcol], [yt])
                        k.op("dve", lambda: nc.vector.tensor_tensor(yt.t[:], yt.t[:], v3(PY.t[0:64, :]), ALU.add), [yt, PY], [yt])
                        k.op("dve", lambda: nc.vector.tensor_tensor(yt.t[:], yt.t[:], y0.t[:], ALU.add), [yt, y0], [yt])
                        k.op("act", lambda: nc.scalar.activation(zc_.t[:], zc_.t[:], AF.Silu), [zc_], [zc_])
                        k.op("dve", lambda: nc.vector.tensor_tensor(ob.t[:], yt.t[:], zc_.t[:], ALU.mult), [yt, zc_], [ob])
                        k.dma("sp", self.AO, self.AO.t[2 * hb:2 * hb + 2, :, tok:tok + 128].rearrange("c (two p) t -> p (c two) t", two=2), ob, ob.t[:])
        k.release(mark)

    def prep_weights(self, L):
        m = L % 4
        wc = self.wcache
        if m == 0:
            wc[("in", L)] = self.conv_w_inter("winA", self.a_w_in, self.a_w_sw, 2 * NH, NH)
            wc[("out", L)] = self.conv_w("woA", self.a_w_out, self.a_w_out.t, 1024, D)
        elif m == 1:
            wc[("in", L)] = self.conv_w_inter("winB", self.b_w_in, self.b_w_sw, NH + 2, 2)
            wc[("out", L)] = self.conv_w("woB", self.b_w_out, self.b_w_out.t, 1024, D)
        elif m == 2:
            wc[("in", L)] = self.conv_w("winC", self.c_w_in, self.c_w_in.t, D, 5248)
            wc[("out", L)] = self.conv_w("woC", self.c_w_out, self.c_w_out.t, 2048, D)
        else:
            wc[("in", L)] = self.conv_w("winD", self.d_w_in, self.d_w_in.t, D, 3072)
            wc[("out", L)] = self.conv_w("woD", self.d_w_out, self.d_w_out.t, 1024, D)

    def prep_mlp(self, L):
        wc = self.wcache
        wc[("w1", L)] = self.conv_w("w1_%d" % L, self.mlp_w1, self.mlp_w1.t[L], D, D_FF)
        wc[("w2", L)] = self.conv_w("w2_%d" % L, self.mlp_w2, self.mlp_w2.t[L], D_FF, D)

    def conv_w_inter(self, name, w, wsw, nqk, nv):
        k = self.k
        nch = 2 * nqk + nv
        dst = k.dram(name, [nch, 128, KC, 128], BF16)
        for c in range(nch):
            if c < 2 * nqk:
                src = w if c % 2 == 0 else wsw
                sc = c // 2
            else:
                src = w
                sc = nqk + (c - 2 * nqk)
            k.dma("pool", dst, dst.t[c], src, src.t[:, sc * 128:(sc + 1) * 128].rearrange("(kc p) n -> p kc n", p=128))
        return dst

    def build(self):
        k = self.k
        nl = self.nlayers
        self.mod_init()
        self.conv_ada(0)
        self.prep_weights(0)
        self.compute_mod(0)
        self.phase_b(-1)
        for L in range(nl):
            m = L % 4
            self.inproj_all(L)
            if L == 0:
                self.prep_mlp(0)
                if nl > 1:
                    self.conv_ada(1)
                    self.prep_weights(1)
            if L == 1 and nl > 2:
                self.conv_ada(2)
                self.prep_weights(2)
            if m in (0, 1):
                self.phase_a_dense(L)
            elif m == 2:
                self.phase_a_mamba(L)
            else:
                self.phase_a_na(L)
            self.outproj_all(L)
            if L + 1 < nl:
                self.compute_mod(L + 1)
            if L == 0 and nl > 1:
                self.prep_mlp(1)
            if L == 1:
                if nl > 2:
                    self.prep_mlp(2)
                if nl > 3:
                    self.conv_ada(3)
                    self.prep_weights(3)
            if L == 2 and nl > 3:
                self.prep_mlp(3)
            self.phase_b(L)
        k.fence(self.out)
        if self.dbg and self.nlayers < DEPTH:
            k.fence(self.out_ctx)
        k.close()
        return self.nc


def rope_tables(dim):
    t = np.arange(T_LAT)
    row = (t // GRID_W).astype(np.float32)
    col = (t % GRID_W).astype(np.float32)
    n_pairs = dim // 4
    inv = (10000.0 ** (-np.arange(n_pairs, dtype=np.float32) / n_pairs)).astype(np.float32)
    ang = np.concatenate([row[:, None] * inv, col[:, None] * inv], axis=-1).astype(np.float32)
    cos, sin = np.cos(ang), np.sin(ang)
    tab = np.zeros((2, 128, T), np.float32)
    tab[0, :, :T_CTX] = 1.0
    for p in range(128):
        d = p % dim
        j = d // 2
        tab[0, p, T_CTX:] = cos[:, j]
        tab[1, p, T_CTX:] = -sin[:, j] if d % 2 == 0 else sin[:, j]
    return tab


def swap_pairs_cols(w):
    k, n = w.shape
    return np.ascontiguousarray(w.reshape(k, n // 2, 2)[:, :, ::-1].reshape(k, n))


def fm(v):
    s = v.shape
    a = v.reshape(s[:-1] + (s[-1] // 128, 128))
    return np.ascontiguousarray(np.moveaxis(a, -1, 0))


def make_in_maps(inp, cores):
    f = lambda a: np.ascontiguousarray(np.asarray(a, dtype=np.float32))
    x, c, ctx, c_ctx = f(inp["x"]), f(inp["c"]), f(inp["ctx"]), f(inp["c_ctx"])
    shared = {}
    shared["ada_w"] = f(inp["ada_w"])
    shared["ada_bT"] = fm(f(inp["ada_b"]))
    ln = np.stack([f(inp["ln_g"]), f(inp["ln_b"])], axis=2)
    shared["lnT"] = fm(ln)
    shared["mlp_w1"] = f(inp["mlp_w1"])
    shared["mlp_w2"] = f(inp["mlp_w2"])
    shared["ident"] = np.eye(128, dtype=np.float32).astype(ml_dtypes.bfloat16)
    lam = f(inp["a_lambda"])[0]
    shared["a_lam"] = np.ascontiguousarray(np.concatenate([lam[0], lam[2], lam[1], lam[3]])[None, :])
    shared["a_sub_g"] = np.ascontiguousarray(f(inp["a_sub_g"])[0][:, None])
    shared["ropeA"] = rope_tables(64)
    qg, kg = f(inp["b_qn_g"])[0], f(inp["b_kn_g"])[0]
    sw1 = lambda g: g.reshape(64, 2)[:, ::-1].reshape(128)
    shared["b_g"] = np.ascontiguousarray(np.stack([qg, sw1(qg), kg, sw1(kg)], axis=1))
    shared["ropeB"] = rope_tables(128)
    tri = np.zeros((2, 128, 128), np.float32)
    tri[0] = np.triu(np.ones((128, 128), np.float32))
    tri[1] = np.tril(np.ones((128, 128), np.float32))
    shared["c_tri"] = tri
    kc_ = np.arange(128) % 64
    cq = np.arange(64)
    cstart = np.clip(cq - 8, 0, 48)
    valid = (kc_[:, None] >= cstart[None, :]) & (kc_[:, None] < cstart[None, :] + 16)
    shared["d_mask"] = np.ascontiguousarray(np.broadcast_to(np.where(valid, 0.0, -30000.0).astype(np.float32)[:, None, :], (128, 14, 64)))
    aw, bw, cw, dw = f(inp["a_w_in"])[0], f(inp["b_w_in"])[0], f(inp["c_w_in"])[0], f(inp["d_w_in"])[0]
    awo, bwo, cwo, dwo = f(inp["a_w_out"])[0], f(inp["b_w_out"])[0], f(inp["c_w_out"])[0], f(inp["d_w_out"])[0]
    convw, convb = f(inp["c_conv_w"])[0], f(inp["c_conv_b"])[0]
    alog, dtb, dsk = f(inp["c_A_log"])[0], f(inp["c_dt_bias"])[0], f(inp["c_D"])[0]
    ng = f(inp["c_norm_g"])[0]
    rpb = f(inp["d_rpb"])[0]
    idx = np.clip(np.arange(64)[:, None] - np.arange(64)[None, :] + 15, 0, 30)
    Eh = rpb[:, :, idx]
    ee = np.concatenate([Eh[:, 0:14], Eh[:, 1:15]], axis=2)
    ee = np.ascontiguousarray(ee.transpose(0, 2, 1, 3))
    per_rank = []
    for h in range(2):
        pr = {}
        s1 = slice(1024 * h, 1024 * (h + 1))
        qk = np.concatenate([aw[:, 0:2048][:, s1], aw[:, 2048:4096][:, s1]], axis=1)
        pr["a_w_in"] = np.ascontiguousarray(np.concatenate([qk, aw[:, 4096:6144][:, s1]], axis=1))
        pr["a_w_sw"] = swap_pairs_cols(qk)
        pr["a_w_out"] = np.ascontiguousarray(awo[s1, :])
        s2 = slice(256 * h, 256 * (h + 1))
        qkb = np.concatenate([bw[:, 0:2048][:, s1], bw[:, 2048:2560][:, s2]], axis=1)
        pr["b_w_in"] = np.ascontiguousarray(np.concatenate([qkb, bw[:, 2560:3072][:, s2]], axis=1))
        pr["b_w_sw"] = swap_pairs_cols(qkb)
        pr["b_w_out"] = np.ascontiguousarray(bwo[s1, :])
        sc = slice(2048 * h, 2048 * (h + 1))
        sg = slice(512 * h, 512 * (h + 1))
        sh = slice(32 * h, 32 * (h + 1))
        dtc = np.concatenate([cw[:, 10240:10304][:, sh], cw[:, 10304:10368][:, sh], np.zeros((D, 64), np.float32)], axis=1)
        pr["c_w_in"] = np.ascontiguousarray(np.concatenate([cw[:, 0:4096][:, sc], cw[:, 4096:8192][:, sc], cw[:, 8192:9216][:, sg], cw[:, 9216:10240][:, sg], dtc], axis=1))
        cwo_ = np.concatenate([convw[:, 0:4096][:, sc], convw[:, 4096:5120][:, sg], convw[:, 5120:6144][:, sg]], axis=1)
        pr["c_conv_wT"] = np.ascontiguousarray(np.transpose(fm(cwo_), (0, 2, 1)))
        cbo_ = np.concatenate([convb[0:4096][sc], convb[4096:5120][sg], convb[5120:6144][sg]])
        pr["c_conv_bT"] = fm(cbo_)
        own = lambda v: np.concatenate([v[0][sh], v[1][sh]])
        pr["c_hp"] = np.ascontiguousarray(np.stack([own(alog), own(dtb), own(dsk), np.zeros(64, np.float32)], axis=1))
        pr["c_norm_gT"] = fm(ng[sc])
        pr["c_w_out"] = np.ascontiguousarray(cwo[sc, :])
        pr["d_w_in"] = np.ascontiguousarray(np.concatenate([dw[:, 0:2048][:, s1], dw[:, 2048:4096][:, s1], dw[:, 4096:6144][:, s1]], axis=1))
        pr["d_ee"] = np.ascontiguousarray(ee[8 * h:8 * (h + 1)])
        pr["d_w_out"] = np.ascontiguousarray(dwo[s1, :])
        per_rank.append(pr)
    maps = []
    for core in cores:
        b, h = core // 2, core % 2
        mp = dict(shared)
        mp.update(per_rank[h])
        xa = np.concatenate([ctx[b][128 * h:128 * (h + 1)], x[b][2048 * h:2048 * (h + 1)]], axis=0)
        mp["xT"] = np.ascontiguousarray(xa.T.reshape(KC, 128, TO))
        cc = np.stack([c[b], c_ctx], axis=1)
        mp["cT"] = np.ascontiguousarray(cc.reshape(KC, 128, 2).transpose(1, 0, 2))
        maps.append(mp)
    return maps


_CACHE = {}


def kernel(**inputs):
    if "nc" not in _CACHE:
        _CACHE["nc"] = Prog(4).build()
    nc = _CACHE["nc"]
    cores = list(range(8))
    maps = make_in_maps(inputs, cores)
    res = run_bass_kernel_spmd(nc, maps, core_ids=cores)
    out = np.empty((4, T_LAT, D), np.float32)
    for core, r in zip(cores, res.results):
        b, h = core // 2, core % 2
        o = np.asarray(r["out"]).reshape(D, 2048)
        out[b, 2048 * h:2048 * (h + 1), :] = o.T
    return out
```

```python
import math
import numpy as np
import ml_dtypes
import concourse.bass as bass
import concourse.mybir as mybir
from concourse.bass_utils import run_bass_kernel_spmd

F32 = mybir.dt.float32
BF16 = mybir.dt.bfloat16
AF = mybir.ActivationFunctionType
ALU = mybir.AluOpType

D = 2048
KC = 16
T_CTX = 256
T_LAT = 4096
T = T_CTX + T_LAT
DEPTH = 4
D_FF = 8192
EPS = 1e-6
DN_ALPHA = (2.0 * DEPTH) ** 0.25
GRID_W = 64
C_DI = 4096
NWB = 10
TO = 2176
NH = 8
RG = [[0, 1], [2, 3], [4, 5], [6, 7]]


class Trk:
    __slots__ = ("t", "w", "r", "name")

    def __init__(self, t, name=""):
        self.t = t
        self.w = []
        self.r = []
        self.name = name


class Kern:
    ENG = ("pe", "act", "dve", "pool", "sp")

    def __init__(self, n_dma_slots=(24, 12)):
        self.nc = bass.Bass("TRN2", target_bir_lowering=False)
        nc = self.nc
        self.e = {"pe": nc.tensor, "act": nc.scalar, "dve": nc.vector, "pool": nc.gpsimd, "sp": nc.sync}
        self._ctx = []
        self.sem = {}
        self.cnt = {}
        for k in ("pe", "act", "dve", "pool"):
            self.sem[k] = self._enter(nc.semaphore("s_" + k))
            self.cnt[k] = 0
        self.dq = {}
        for q, n in zip(("sp", "pool"), n_dma_slots):
            slots = []
            for i in range(n):
                key = "d_%s_%d" % (q, i)
                self.sem[key] = self._enter(nc.semaphore(key))
                self.cnt[key] = 0
                slots.append(key)
            self.dq[q] = [slots, 0]
        self.seen = {k: {} for k in self.ENG}
        self.uid = 0

    def _enter(self, cm):
        v = cm.__enter__()
        self._ctx.append(cm)
        return v

    def close(self):
        for cm in reversed(self._ctx):
            cm.__exit__(None, None, None)
        self._ctx = []

    def sb(self, shape, dtype, name=None):
        self.uid += 1
        name = (name or "t") + "_%d" % self.uid
        t = self._enter(self.nc.sbuf_tensor(name, list(shape), dtype))
        return Trk(t, name)

    def ps(self, shape=(128, 512), dtype=F32, name=None):
        self.uid += 1
        name = (name or "p") + "_%d" % self.uid
        t = self._enter(self.nc.psum_tensor(name, list(shape), dtype))
        return Trk(t, name)

    def dram(self, name, shape, dtype, kind="Internal"):
        t = self.nc.dram_tensor(name, list(shape), dtype, kind=kind).ap()
        return Trk(t, name)

    def _wait(self, eng, evs):
        seen = self.seen[eng]
        need = {}
        for (k, v) in evs:
            if seen.get(k, 0) >= v:
                continue
            if need.get(k, 0) < v:
                need[k] = v
        for k, v in need.items():
            self.e[eng].wait_ge(self.sem[k], v)
            seen[k] = v

    def _deps(self, eng, reads, writes, acc):
        evs = []
        for o in reads:
            evs.extend(o.w)
        for o in writes:
            if not acc:
                evs.extend(o.w)
            evs.extend(o.r)
        if eng == "pe":
            evs = [ev for ev in evs if ev[0] != "pe"]
        return evs

    @staticmethod
    def _compact(evs):
        m = {}
        for k, v in evs:
            if m.get(k, 0) < v:
                m[k] = v
        return list(m.items())

    def _commit(self, ev, reads, writes, acc):
        for o in reads:
            o.r.append(ev)
            if len(o.r) > 16:
                o.r = self._compact(o.r)
        for o in writes:
            if acc:
                o.w.append(ev)
                if len(o.w) > 16:
                    o.w = self._compact(o.w)
            else:
                o.w = [ev]
                o.r = []

    def op(self, eng, fn, reads=(), writes=(), acc=False):
        self._wait(eng, self._deps(eng, reads, writes, acc))
        ins = fn()
        self.cnt[eng] += 1
        ins.then_inc(self.sem[eng], 1)
        ev = (eng, self.cnt[eng])
        self._commit(ev, reads, writes, acc)
        return ev

    def mm(self, out, pairs, reads, first=True, last=True, out_ap=None):
        self._wait("pe", self._deps("pe", reads, [out], not first))
        oap = out_ap if out_ap is not None else out.t[:]
        n = len(pairs)
        ins = None
        for i, (l, r) in enumerate(pairs):
            ins = self.nc.tensor.matmul(oap, l, r, start=(first and i == 0), stop=(last and i == n - 1))
        self.cnt["pe"] += 1
        ins.then_inc(self.sem["pe"], 1)
        ev = ("pe", self.cnt["pe"])
        self._commit(ev, reads, [out], not first)
        return ev

    def mmg(self, items, reads):
        evs = []
        for o in reads:
            evs.extend(o.w)
        for it in items:
            evs.extend(it[0].r)
        evs = [ev for ev in evs if ev[0] != "pe"]
        self._wait("pe", evs)
        ins = None
        for it in items:
            ins = self.nc.tensor.matmul(it[1], it[2], it[3], start=it[4], stop=it[5])
        self.cnt["pe"] += 1
        ins.then_inc(self.sem["pe"], 1)
        ev = ("pe", self.cnt["pe"])
        for o in reads:
            o.r.append(ev)
            if len(o.r) > 16:
                o.r = self._compact(o.r)
        for it in items:
            o = it[0]
            if it[6]:
                o.w = [ev]
                o.r = []
            else:
                o.w.append(ev)
                if len(o.w) > 16:
                    o.w = self._compact(o.w)
        return ev

    def transpose(self, out, out_ap, in_, in_ap, ident):
        self._wait("pe", self._deps("pe", [in_, ident], [out], True))
        ins = self.nc.tensor.transpose(out_ap, in_ap, ident.t[:])
        self.cnt["pe"] += 1
        ins.then_inc(self.sem["pe"], 1)
        ev = ("pe", self.cnt["pe"])
        self._commit(ev, [in_, ident], [out], True)
        return ev

    def dma(self, q, out, out_ap, in_, in_ap, acc=True, **kw):
        slots, idx = self.dq[q]
        key = slots[idx % len(slots)]
        self.dq[q][1] = idx + 1
        evs = self._deps(q, [in_], [out], acc)
        if self.cnt[key] > 0:
            evs.append((key, self.cnt[key]))
        self._wait(q, evs)
        ins = self.e[q].dma_start(out=out_ap, in_=in_ap, **kw)
        self.cnt[key] += 16
        ins.then_inc(self.sem[key], 16)
        ev = (key, self.cnt[key])
        self._commit(ev, [in_], [out], acc)
        return ev

    def collective(self, kind, op, in_, in_ap, out, out_ap):
        self.uid += 1
        key = "cc%d" % self.uid
        self.sem[key] = self._enter(self.nc.semaphore(key))
        self.cnt[key] = 0
        self._wait("pool", self._deps("pool", [in_], [out], False))
        ins = self.nc.gpsimd.collective_compute(kind, op, replica_groups=RG, ins=[in_ap], outs=[out_ap])
        ins.then_inc(self.sem[key], 1)
        self.cnt[key] = 1
        ev = (key, 1)
        self._commit(ev, [in_], [out], False)
        return ev

    def fence(self, obj, eng="sp"):
        self._wait(eng, list(obj.w))

    def barrier(self):
        evs = [(key, v) for key, v in self.cnt.items() if v > 0]
        for eng in self.ENG:
            self._wait(eng, [ev for ev in evs if not (eng == "pe" and ev[0] == "pe")])

    def mark(self):
        return len(self._ctx)

    def release(self, mark):
        self.barrier()
        while len(self._ctx) > mark:
            self._ctx.pop().__exit__(None, None, None)


BLOCKS = [(0, 256, 1)] + [(T_CTX + i * 512, 512, 0) for i in range(8)]
OWN_BLOCKS = [(0, 128, 1)] + [(128 + i * 512, 512, 0) for i in range(4)]
ALL_BLOCKS = [(0, 0, 128, 0, 1), (1, 0, 128, 128, 1)] + \
    [(r, 128 + i * 512, 512, T_CTX + r * 2048 + i * 512, 0) for r in range(2) for i in range(4)]


class Prog:
    def __init__(self, nlayers=4, dbg=False):
        self.nlayers = nlayers
        self.k = k = Kern()
        self.nc = nc = k.nc
        self.dbg = dbg
        ein = lambda name, shape, dt=F32: k.dram(name, shape, dt, kind="ExternalInput")
        self.xT = ein("xT", [KC, 128, TO])
        self.cT = ein("cT", [128, KC, 2])
        self.ada_w = ein("ada_w", [DEPTH, D, 6 * D])
        self.ada_bT = ein("ada_bT", [128, DEPTH, 96])
        self.lnT = ein("lnT", [128, DEPTH, 2, 2, KC])
        self.mlp_w1 = ein("mlp_w1", [DEPTH, D, D_FF])
        self.mlp_w2 = ein("mlp_w2", [DEPTH, D_FF, D])
        self.ident_in = ein("ident", [128, 128], BF16)
        self.a_w_in = ein("a_w_in", [D, 3072])
        self.a_w_sw = ein("a_w_sw", [D, 2048])
        self.a_lam = ein("a_lam", [1, 256])
        self.a_sub_g = ein("a_sub_g", [128, 1])
        self.a_w_out = ein("a_w_out", [1024, D])
        self.ropeA = ein("ropeA", [2, 128, T])
        self.b_w_in = ein("b_w_in", [D, 1536])
        self.b_w_sw = ein("b_w_sw", [D, 1280])
        self.b_g = ein("b_g", [128, 4])
        self.b_w_out = ein("b_w_out", [1024, D])
        self.ropeB = ein("ropeB", [2, 128, T])
        self.c_w_in = ein("c_w_in", [D, 5248])
        self.c_conv_wT = ein("c_conv_wT", [128, 24, 5])
        self.c_conv_bT = ein("c_conv_bT", [128, 24])
        self.c_hp = ein("c_hp", [64, 4])
        self.c_norm_gT = ein("c_norm_gT", [128, 16])
        self.c_w_out = ein("c_w_out", [2048, D])
        self.c_tri = ein("c_tri", [2, 128, 128])
        self.d_w_in = ein("d_w_in", [D, 3072])
        self.d_ee = ein("d_ee", [8, 128, 14, 64])
        self.d_mask = ein("d_mask", [128, 14, 64])
        self.d_w_out = ein("d_w_out", [1024, D])
        self.out = k.dram("out", [KC, 128, 2048], F32, kind="ExternalOutput")
        if dbg:
            self.out_ctx = k.dram("out_ctx", [KC, 128, 128], F32, kind="ExternalOutput")
        self.XT = k.dram("XTs", [KC, 128, TO], F32)
        self.UGi = [k.dram("UGi%d" % j, [KC * 128, n // 2], F32) for j, (l0, n, col) in enumerate(OWN_BLOCKS)]
        self.UG = [k.dram("UG%d" % j, [2 * KC * 128, n // 2], F32) for j, (l0, n, col) in enumerate(OWN_BLOCKS)]
        self.YP = [k.dram("YP%d" % j, [2 * 2 * 128, TO], F32) for j in range(8)] + [k.dram("YP8", [2 * 128, TO], F32)]
        self.YR = [k.dram("YR%d" % j, [2 * 128, TO], F32) for j in range(8)] + [k.dram("YR8", [128, TO], F32)]
        self.AO = k.dram("AOs", [16, 128, T], BF16)
        self.QT = k.dram("QTs", [NH, 128, T], BF16)
        self.KT = k.dram("KTs", [NH, 128, T], BF16)
        self.VK = k.dram("VKs", [NH, T, 128], BF16)
        self.ZT = k.dram("ZTs", [16, 128, T], F32)
        self.XBC = k.dram("XBCs", [24, 128, T], F32)
        self.DTR = k.dram("DTRs", [64, T], F32)
        self.XS = k.dram("XSs", [16, 128, T], F32)
        self.BCb = k.dram("BCbs", [8, 128, T], BF16)
        self.YS = k.dram("YSs", [32, 64, T], F32)
        self.P = [k.ps([128, 512], F32, "P%d" % i) for i in range(7)]
        self.PT = k.ps([128, 1024], BF16, "PT")
        self.ident = k.sb([128, 128], BF16, "ident")
        self.ones_f = k.sb([128, 128], F32, "ones_f")
        self.ones_b = k.sb([128, 128], BF16, "ones_b")
        self.mod = k.sb([128, DEPTH, 96, 2], F32, "mod")
        self.ln = k.sb([128, DEPTH, 2, 2, KC], F32, "ln")
        self.wb = [k.sb([128, 16, 128], BF16, "wb%d" % i) for i in range(NWB)]
        self.wbi = 0
        k.dma("sp", self.ident, self.ident.t[:], self.ident_in, self.ident_in.t[:])
        k.dma("sp", self.ln, self.ln.t[:], self.lnT, self.lnT.t[:])
        k.op("dve", lambda: nc.vector.memset(self.ones_f.t[:], 1.0), [], [self.ones_f])
        k.op("dve", lambda: nc.vector.memset(self.ones_b.t[:], 1.0), [], [self.ones_b])
        self.wcache = {}
        self.Xc = [Trk(None, "Xc%d" % i) for i in range(KC)]

    def conv_w(self, name, src, src_ap, K, ncols):
        k = self.k
        nch = ncols // 128
        kc = K // 128
        dst = k.dram(name, [nch, 128, kc, 128], BF16)
        for c in range(nch):
            for k0 in range(0, kc, 16):
                k1 = min(kc, k0 + 16)
                k.dma("pool", dst, dst.t[c][:, k0:k1, :], src,
                      src_ap[k0 * 128:k1 * 128, c * 128:(c + 1) * 128].rearrange("(kc p) n -> p kc n", p=128))
        return dst

    def gemm(self, wblk, chunks, kgroups, rhs_ap, rhs_trks, n, epi, psums, kper=16):
        k = self.k
        seq = [(ci, c, kg) for ci, c in enumerate(chunks) for kg in range(kgroups)]
        bufs = {}
        issued = 0
        PF = 8
        ps = None
        for j in range(len(seq)):
            while issued < min(len(seq), j + PF):
                ci, c, kg = seq[issued]
                wb = self.wb[self.wbi % NWB]
                self.wbi += 1
                k.dma("sp", wb, wb.t[:, :kper, :], wblk, wblk.t[c][:, kg * kper:(kg + 1) * kper, :], acc=False)
                bufs[issued] = wb
                issued += 1
            ci, c, kg = seq[j]
            if kg == 0:
                ps = psums[ci % len(psums)]
            wb = bufs.pop(j)
            pairs = [(wb.t[:, i, :], rhs_ap(kg * kper + i)) for i in range(kper)]
            k.mm(ps, pairs, [wb] + list(rhs_trks), first=(kg == 0), last=(kg == kgroups - 1), out_ap=ps.t[:, :n])
            if kg == kgroups - 1:
                epi(ci, c, ps)

    def mod_init(self):
        k, nc = self.k, self.nc
        self.sc = k.sb([128, KC, 2], BF16, "sc")
        self.adb = k.sb([128, DEPTH, 96], F32, "adb")
        mark = k.mark()
        cs = k.sb([128, KC, 2], F32, "cs")
        k.dma("sp", cs, cs.t[:], self.cT, self.cT.t[:])
        k.dma("sp", self.adb, self.adb.t[:], self.ada_bT, self.ada_bT.t[:])
        k.op("act", lambda: nc.scalar.activation(self.sc.t[:], cs.t[:], AF.Silu), [cs], [self.sc])
        k.release(mark)

    def conv_ada(self, L):
        self.wcache[("ada", L)] = self.conv_w("adaB%d" % L, self.ada_w, self.ada_w.t[L], D, 6 * D)

    def compute_mod(self, L):
        k, nc = self.k, self.nc
        wblk = self.wcache[("ada", L)]
        sc, adb = self.sc, self.adb

        def epi(ci, c, ps):
            k.op("dve", lambda: nc.vector.tensor_scalar(self.mod.t[:, L, c, :], ps.t[:, 0:2], adb.t[:, L, c:c + 1], None, ALU.add),
                 [ps, adb], [self.mod], acc=True)
        self.gemm(wblk, list(range(96)), 1, lambda kc: sc.t[:, kc, :], [sc], 2, epi, self.P[0:3])
        for j in (1, 4):
            k.op("dve", lambda: nc.vector.tensor_scalar(self.mod.t[:, L, j * 16:(j + 1) * 16, :], self.mod.t[:, L, j * 16:(j + 1) * 16, :], 1.0, None, ALU.add),
                 [self.mod], [self.mod])

    def mv(self, L, j, dc, col):
        return self.mod.t[:, L, j * 16 + dc, col:col + 1]

    def layer_norm(self, X, n, L, which, tmp, st):
        k, nc = self.k, self.nc
        S1, S2 = self.P[3], self.P[4]
        for dc in range(KC):
            k.mm(S1, [(self.ones_f.t[:], X.t[:, dc, :n])], [self.ones_f, X], first=(dc == 0), last=(dc == KC - 1), out_ap=S1.t[:, :n])
            k.op("act", lambda: nc.scalar.activation(tmp.t[:, :n], X.t[:, dc, :n], AF.Square), [X], [tmp])
            k.mm(S2, [(self.ones_f.t[:], tmp.t[:, :n])], [self.ones_f, tmp], first=(dc == 0), last=(dc == KC - 1), out_ap=S2.t[:, :n])
        mean, rstd = st["mean"], st["rstd"]
        k.op("act", lambda: nc.scalar.mul(mean.t[:, :n], S1.t[:, :n], 1.0 / D), [S1], [mean])
        k.op("dve", lambda: nc.vector.tensor_tensor(rstd.t[:, :n], mean.t[:, :n], mean.t[:, :n], ALU.mult), [mean], [rstd])
        k.op("dve", lambda: nc.vector.scalar_tensor_tensor(rstd.t[:, :n], S2.t[:, :n], 1.0 / D, rstd.t[:, :n], ALU.mult, ALU.subtract), [S2, rstd], [rstd])
        k.op("dve", lambda: nc.vector.tensor_scalar(rstd.t[:, :n], rstd.t[:, :n], EPS, None, ALU.add), [rstd], [rstd])
        k.op("act", lambda: nc.scalar.activation(rstd.t[:, :n], rstd.t[:, :n], AF.Sqrt), [rstd], [rstd])
        k.op("dve", lambda: nc.vector.reciprocal(rstd.t[:, :n], rstd.t[:, :n]), [rstd], [rstd])
        for dc in range(KC):
            self.Xc[dc].w = list(X.w)
            self.Xc[dc].r = list(X.r)
        for dc in range(KC):
            en, ee_ = ("dve", nc.vector)
            Xc = self.Xc[dc]
            k.op(en, lambda: ee_.tensor_tensor(X.t[:, dc, :n], X.t[:, dc, :n], mean.t[:, :n], ALU.subtract), [Xc, mean], [Xc])
            k.op(en, lambda: ee_.tensor_tensor(X.t[:, dc, :n], X.t[:, dc, :n], rstd.t[:, :n], ALU.mult), [Xc, rstd], [Xc])
            k.op(en, lambda: ee_.tensor_scalar(X.t[:, dc, :n], X.t[:, dc, :n], self.ln.t[:, L, which, 0, dc:dc + 1], self.ln.t[:, L, which, 1, dc:dc + 1], ALU.mult, ALU.add),
                 [Xc, self.ln], [Xc])
        X.w = []
        X.r = []
        for dc in range(KC):
            X.w.extend(self.Xc[dc].w)
            self.Xc[dc].w = []
            self.Xc[dc].r = []

    def modulate(self, U, X, n, L, js, jb, col):
        k, nc = self.k, self.nc
        for dc in range(KC):
            k.op("act", lambda: nc.scalar.activation(U.t[:, dc, :n], X.t[:, dc, :n], AF.Identity, bias=self.mv(L, jb, dc, col), scale=self.mv(L, js, dc, col)),
                 [X, self.mod], [U])

    def store_v_tok(self, ps, n, t0, head, st):
        k, nc = self.k, self.nc
        vt, vtok = st["vt"], st["vtok"][head % 2]
        k.op("act", lambda: nc.scalar.copy(vt.t[:, :n], ps.t[:, :n]), [ps], [vt])
        npc = n // 128
        for pc in range(npc):
            k.transpose(self.PT, self.PT.t[:, pc * 128:(pc + 1) * 128], vt, vt.t[:, pc * 128:(pc + 1) * 128], self.ident)
        k.op("dve", lambda: nc.vector.tensor_copy(vtok.t[:, :npc, :], self.PT.t[:, :npc * 128].rearrange("p (a b) -> p a b", b=128)), [self.PT], [vtok])
        k.dma("sp", self.VK, self.VK.t[head, t0:t0 + n, :].rearrange("(a p) d -> p a d", p=128), vtok, vtok.t[:, :npc, :])

    def inproj(self, L, U, n, t0, st):
        k, nc = self.k, self.nc
        m = L % 4
        W = self.wcache[("in", L)]
        rhs = lambda kc: U.t[:, kc, :n]
        if m == 0 or m == 1:
            rope = self.ropeA if m == 0 else self.ropeB
            cs = st["rope"]
            k.dma("sp", cs, cs.t[:, :, :n], rope, rope.t[:, :, t0:t0 + n].rearrange("a p t -> p a t"), acc=False)
            nq = NH
            nk = NH if m == 0 else 2
            nqk = nq + nk
            hold = {}

            def epi(ci, c, ps):
                if ci < 2 * nqk:
                    idx, sw = ci // 2, ci % 2
                    if sw == 0:
                        hold["p"] = ps
                        return
                    p1, p2 = hold["p"], ps
                    isq = idx < nq
                    dst = self.QT if isq else self.KT
                    dch = idx if isq else idx - nq
                    t1, t2, ob = st["t1"], st["t2"], st["ob"][idx % 2]
                    if m == 0:
                        k.op("dve", lambda: nc.vector.tensor_tensor(t1.t[:, :n], p1.t[:, :n], cs.t[:, 0, :n], ALU.mult), [p1, cs], [t1])
                        k.op("dve", lambda: nc.vector.tensor_tensor(t2.t[:, :n], p2.t[:, :n], cs.t[:, 1, :n], ALU.mult), [p2, cs], [t2])
                        k.op("dve", lambda: nc.vector.tensor_tensor(ob.t[:, :n], t1.t[:, :n], t2.t[:, :n], ALU.add), [t1, t2], [ob])
                    else:
                        g0 = 0 if isq else 2
                        R = self.P[5]
                        k.op("act", lambda: nc.scalar.activation(t1.t[:, :n], p1.t[:, :n], AF.Square), [p1], [t1])
                        k.mm(R, [(self.ones_f.t[:], t1.t[:, :n])], [self.ones_f, t1], out_ap=R.t[:, :n])
                        rs = st["rs"]
                        k.op("dve", lambda: nc.vector.tensor_scalar(rs.t[:, :n], R.t[:, :n], 1.0 / 128, EPS, ALU.mult, ALU.add), [R], [rs])
                        k.op("act", lambda: nc.scalar.activation(rs.t[:, :n], rs.t[:, :n], AF.Sqrt), [rs], [rs])
                        k.op("dve", lambda: nc.vector.reciprocal(rs.t[:, :n], rs.t[:, :n]), [rs], [rs])
                        k.op("dve", lambda: nc.vector.scalar_tensor_tensor(t1.t[:, :n], p1.t[:, :n], st["bg"].t[:, g0:g0 + 1], cs.t[:, 0, :n], ALU.mult, ALU.mult), [p1, cs, st["bg"]], [t1])
                        k.op("dve", lambda: nc.vector.scalar_tensor_tensor(t2.t[:, :n], p2.t[:, :n], st["bg"].t[:, g0 + 1:g0 + 2], cs.t[:, 1, :n], ALU.mult, ALU.mult), [p2, cs, st["bg"]], [t2])
                        k.op("dve", lambda: nc.vector.tensor_tensor(t1.t[:, :n], t1.t[:, :n], t2.t[:, :n], ALU.add), [t1, t2], [t1])
                        k.op("dve", lambda: nc.vector.tensor_tensor(ob.t[:, :n], t1.t[:, :n], rs.t[:, :n], ALU.mult), [t1, rs], [ob])
                    k.dma("sp", dst, dst.t[dch][:, t0:t0 + n], ob, ob.t[:, :n])
                else:
                    self.store_v_tok(ps, n, t0, ci - 2 * nqk, st)
            nch = 2 * nqk + (NH if m == 0 else 2)
            self.gemm(W, list(range(nch)), 1, rhs, [U], n, epi, self.P[0:3])
        elif m == 3:
            def epi(ci, c, ps):
                if ci < 2 * NH:
                    dst = self.QT if ci < NH else self.KT
                    ob = st["ob"][ci % 2]
                    k.op("act", lambda: nc.scalar.copy(ob.t[:, :n], ps.t[:, :n]), [ps], [ob])
                    k.dma("sp", dst, dst.t[ci % NH][:, t0:t0 + n], ob, ob.t[:, :n])
                else:
                    self.store_v_tok(ps, n, t0, ci - 2 * NH, st)
            self.gemm(W, list(range(3 * NH)), 1, rhs, [U], n, epi, self.P[0:3])
        else:
            self.inproj_mamba(W, rhs, U, n, t0, st)

    def make_st(self):
        k = self.k
        st = {
            "mean": k.sb([128, 512], F32), "rstd": k.sb([128, 512], F32),
            "t1": k.sb([128, 512], F32), "t2": k.sb([128, 512], F32), "rs": k.sb([128, 512], F32),
            "ob": [k.sb([128, 512], BF16), k.sb([128, 512], BF16)],
            "rope": k.sb([128, 2, 512], F32), "vt": k.sb([128, 512], BF16),
            "vtok": [k.sb([128, 4, 128], BF16), k.sb([128, 4, 128], BF16)],
            "bg": k.sb([128, 4], F32), "ng": k.sb([128, 16], F32),
        }
        k.dma("sp", st["bg"], st["bg"].t[:], self.b_g, self.b_g.t[:])
        k.dma("sp", st["ng"], st["ng"].t[:], self.c_norm_gT, self.c_norm_gT.t[:])
        return st

    def inproj_all(self, L):
        k, nc = self.k, self.nc
        mark = k.mark()
        st = self.make_st()
        Us = [k.sb([128, KC, 512], BF16, "Ua%d" % i) for i in range(2)]
        for bi, (r, l0, n, t0, col) in enumerate(ALL_BLOCKS):
            U = Us[bi % 2]
            j = [x[0] for x in OWN_BLOCKS].index(l0)
            UGv = self.UG[j].t.bitcast(BF16).rearrange("(r c p) t -> r p c t", r=2, p=128)
            k.dma("sp", U, U.t[:, :, :n], self.UG[j], UGv[r], acc=False)
            self.inproj(L, U, n, t0, st)
        k.release(mark)

    def outproj_all(self, L):
        k, nc = self.k, self.nc
        m = L % 4
        mark = k.mark()
        kco = 16 if m == 2 else NH
        st = self.make_st()
        As = [k.sb([128, kco, 512], BF16, "Aa%d" % i) for i in range(2)]
        yo = [k.sb([128, 512], F32, "yo%d" % i) for i in range(3)]
        Wo = self.wcache[("out", L)]
        YPv = [self.YP[j].t.rearrange("(r c p) t -> r c p t", r=2, p=128) for j in range(9)]
        yi = 0

        def run_block(bi, blk, chunks):
            nonlocal yi
            (r, l0, n, t0, col) = blk
            A = As[bi % 2]
            k.dma("sp", A, A.t[:, :, :n], self.AO, self.AO.t[0:kco, :, t0:t0 + n].rearrange("c p t -> p c t"), acc=False)
            if m == 2:
                R = self.P[5]
                for c in range(16):
                    tq = st["t1"] if c % 2 == 0 else st["t2"]
                    k.op("act", lambda: nc.scalar.activation(tq.t[:, :n], A.t[:, c, :n], AF.Square), [A], [tq])
                    k.mm(R, [(self.ones_f.t[:], tq.t[:, :n])], [self.ones_f, tq], first=(c == 0), last=(c == 15), out_ap=R.t[:, :n])
                y = yo[yi % 3]
                yi += 1
                k.op("act", lambda: nc.scalar.copy(y.t[:, :n], R.t[:, :n]), [R], [y])
                k.dma("sp", self.YP[8], YPv[8][r][0][:, l0:l0 + n], y, y.t[:, :n])
                for c in range(16):
                    k.op("dve", lambda: nc.vector.tensor_scalar(A.t[:, c, :n], A.t[:, c, :n], st["ng"].t[:, c:c + 1], None, ALU.mult), [A, st["ng"]], [A])

            def epi(ci, c, ps):
                nonlocal yi
                y = yo[yi % 3]
                yi += 1
                k.op("act", lambda: nc.scalar.copy(y.t[:, :n], ps.t[:, :n]), [ps], [y])
                k.dma("sp", self.YP[c // 2], YPv[c // 2][r][c % 2][:, l0:l0 + n], y, y.t[:, :n])
            self.gemm(Wo, chunks, 1, lambda kc: A.t[:, kc, :n], [A], n, epi, self.P[0:3], kper=kco)

        if m == 2:
            for bi, blk in enumerate(ALL_BLOCKS):
                run_block(bi, blk, list(range(16)))
            for j in range(9):
                k.collective("ReduceScatter", ALU.add, self.YP[j], self.YP[j].t[:], self.YR[j], self.YR[j].t[:])
        else:
            AR = k.sb([128, NH, T], BF16, "AOres")
            for c in range(NH):
                k.dma("sp", AR, AR.t[:, c, :], self.AO, self.AO.t[c], acc=(c > 0))
            wq = {}

            def loadw(j):
                ws = []
                for ci in range(2):
                    wb = self.wb[self.wbi % NWB]
                    self.wbi += 1
                    k.dma("sp", wb, wb.t[:, :NH, :], Wo, Wo.t[2 * j + ci][:, 0:NH, :], acc=False)
                    ws.append(wb)
                wq[j] = ws
            loadw(0)
            pi = 0
            for j in range(8):
                if j + 1 < 8:
                    loadw(j + 1)
                ws = wq.pop(j)
                for (r, l0, n, t0, col) in ALL_BLOCKS:
                    for ci in range(2):
                        ps = self.P[pi % 3]
                        pi += 1
                        k.mm(ps, [(ws[ci].t[:, i, :], AR.t[:, i, t0:t0 + n]) for i in range(NH)], [ws[ci], AR], out_ap=ps.t[:, :n])
                        y = yo[yi % 3]
                        yi += 1
                        k.op("act", lambda: nc.scalar.copy(y.t[:, :n], ps.t[:, :n]), [ps], [y])
                        k.dma("sp", self.YP[j], YPv[j][r][ci][:, l0:l0 + n], y, y.t[:, :n])
                k.collective("ReduceScatter", ALU.add, self.YP[j], self.YP[j].t[:], self.YR[j], self.YR[j].t[:])
        k.release(mark)

    def phase_b(self, L):
        k, nc = self.k, self.nc
        last = (L == self.nlayers - 1)
        mark = k.mark()
        X = k.sb([128, KC, 512], F32, "X")
        U = k.sb([128, KC, 512], BF16, "U")
        st = {
            "mean": k.sb([128, 512], F32), "rstd": k.sb([128, 512], F32),
            "t1": k.sb([128, 512], F32), "t2": k.sb([128, 512], F32), "rs": k.sb([128, 512], F32),
        }
        ys = [k.sb([128, 512], F32, "ys%d" % i) for i in range(3)]
        YRv = [self.YR[j].t.rearrange("(c p) t -> c p t", p=128) for j in range(9)]
        if L >= 0:
            H = k.sb([128, 64, 512], BF16, "H")
            W1 = self.wcache[("w1", L)]
            W2 = self.wcache[("w2", L)]
        for (l0, n, col) in OWN_BLOCKS:
            if last and col == 1 and self.nlayers == DEPTH:
                continue
            src = self.XT if L >= 0 else self.xT
            k.dma("sp", X, X.t[:, :, :n], src, src.t[:, :, l0:l0 + n].rearrange("c p t -> p c t"), acc=False)
            if L >= 0:
                k.op("act", lambda: nc.scalar.mul(X.t[:, :, :n], X.t[:, :, :n], DN_ALPHA), [X], [X])
                if L % 4 == 2:
                    rs = st["rs"]
                    k.dma("sp", rs, rs.t[:, :n], self.YR[8], YRv[8][0][:, l0:l0 + n], acc=False)
                    k.op("dve", lambda: nc.vector.tensor_scalar(rs.t[:, :n], rs.t[:, :n], 1.0 / C_DI, EPS, ALU.mult, ALU.add), [rs], [rs])
                    k.op("act", lambda: nc.scalar.activation(rs.t[:, :n], rs.t[:, :n], AF.Sqrt), [rs], [rs])
                    k.op("dve", lambda: nc.vector.reciprocal(rs.t[:, :n], rs.t[:, :n]), [rs], [rs])
                for c in range(KC):
                    y = ys[c % 3]
                    k.dma("sp", y, y.t[:, :n], self.YR[c // 2], YRv[c // 2][c % 2][:, l0:l0 + n], acc=False)
                    if L % 4 == 2:
                        k.op("dve", lambda: nc.vector.tensor_tensor(y.t[:, :n], y.t[:, :n], st["rs"].t[:, :n], ALU.mult), [y, st["rs"]], [y])
                    k.op("dve", lambda: nc.vector.scalar_tensor_tensor(X.t[:, c, :n], y.t[:, :n], self.mv(L, 2, c, col), X.t[:, c, :n], ALU.mult, ALU.add),
                         [y, X, self.mod], [X])
                self.layer_norm(X, n, L, 0, st["t1"], st)
                self.modulate(U, X, n, L, 4, 3, col)
                k.op("act", lambda: nc.scalar.mul(X.t[:, :, :n], X.t[:, :, :n], DN_ALPHA), [X], [X])

                def epi_1(ci, c, ps):
                    tmp = st["t1"] if ci % 2 == 0 else st["t2"]
                    k.op("act", lambda: nc.scalar.activation(tmp.t[:, :n], ps.t[:, :n], AF.Relu), [ps], [tmp])
                    k.op("dve", lambda: nc.vector.tensor_tensor(H.t[:, c, :n], tmp.t[:, :n], tmp.t[:, :n], ALU.mult), [tmp], [H], acc=True)
                self.gemm(W1, list(range(64)), 1, lambda kc: U.t[:, kc, :n], [U], n, epi_1, self.P[0:3])

                def epi_2(ci, c, ps):
                    k.op("dve", lambda: nc.vector.scalar_tensor_tensor(X.t[:, c, :n], ps.t[:, :n], self.mv(L, 5, c, col), X.t[:, c, :n], ALU.mult, ALU.add),
                         [ps, X, self.mod], [X])
                self.gemm(W2, list(range(16)), 4, lambda kc: H.t[:, kc, :n], [H], n, epi_2, self.P[0:3])
                self.layer_norm(X, n, L, 1, st["t1"], st)
            if last:
                if col == 0:
                    k.dma("sp", self.out, self.out.t[:, :, l0 - 128:l0 - 128 + n].rearrange("c p t -> p c t"), X, X.t[:, :, :n])
                else:
                    k.dma("sp", self.out_ctx, self.out_ctx.t[:, :, 0:n].rearrange("c p t -> p c t"), X, X.t[:, :, :n])
            else:
                k.dma("sp", self.XT, self.XT.t[:, :, l0:l0 + n].rearrange("c p t -> p c t"), X, X.t[:, :, :n])
                self.modulate(U, X, n, L + 1, 1, 0, col)
                j = [x[0] for x in OWN_BLOCKS].index(l0)
                UGiv = self.UGi[j].t.bitcast(BF16).rearrange("(c p) t -> p c t", p=128)
                k.dma("sp", self.UGi[j], UGiv, U, U.t[:, :, :n])
                k.collective("AllGather", ALU.bypass, self.UGi[j], self.UGi[j].t[:], self.UG[j], self.UG[j].t[:])
        k.release(mark)

    def phase_a_dense(self, L):
        k, nc = self.k, self.nc
        m = L % 4
        last = (L == self.nlayers - 1) and self.nlayers == DEPTH
        nmap = 2 if m == 0 else 1
        dk = 64 if m == 0 else 128
        scale = dk ** -0.5
        mark = k.mark()
        KTs = [k.sb([128, T], BF16, "KTs%d" % i) for i in range(2)]
        QTs = [k.sb([128, T], BF16, "QTs%d" % i) for i in range(2)]
        Vs = [k.sb([128, 34, 128], BF16, "Vs%d" % i) for i in range(2)]
        Pb = [k.sb([128, 512], BF16, "Pb%d" % i) for i in range(3)]
        rz = k.sb([128, 512], F32, "rz")
        Am = [k.sb([128, 512], F32, "Am%d" % i) for i in range(2)]
        sq = k.sb([128, 512], F32, "sq")
        rs = k.sb([128, 512], F32, "rs")
        aob = [k.sb([128, 512], BF16, "aob%d" % i) for i in range(2)]
        Sps = [self.P[0], self.P[1], self.P[2]]
        Ops = [self.P[3], self.P[5]]
        Zps = [self.P[4], self.P[6]]
        if m == 0:
            lam_init = 0.8 - 0.6 * math.exp(-0.3 * L)
            lv = k.sb([1, 256], F32, "lv")
            k.dma("sp", lv, lv.t[:], self.a_lam, self.a_lam.t[:])
            pr = k.sb([1, 128], F32, "pr")
            k.op("dve", lambda: nc.vector.tensor_tensor(pr.t[:], lv.t[:, 0:128], lv.t[:, 128:256], ALU.mult), [lv], [pr])
            s2 = k.sb([1, 2], F32, "s2")
            k.op("dve", lambda: nc.vector.reduce_sum(s2.t[:], pr.t[:].rearrange("p (a b) -> p a b", b=64), mybir.AxisListType.X), [pr], [s2])
            k.op("act", lambda: nc.scalar.activation(s2.t[:], s2.t[:], AF.Exp), [s2], [s2])
            l1 = k.sb([1, 1], F32, "l1")
            k.op("dve", lambda: nc.vector.tensor_tensor(l1.t[:], s2.t[:, 1:2], s2.t[:, 0:1], ALU.subtract), [s2], [l1])
            k.op("dve", lambda: nc.vector.tensor_scalar(l1.t[:], l1.t[:], -lam_init, None, ALU.add), [l1], [l1])
            k.mm(self.P[0], [(self.ones_f.t[0:1, :], l1.t[:])], [self.ones_f, l1], out_ap=self.P[0].t[:, 0:1])
            neglam = k.sb([128, 1], F32, "neglam")
            k.op("dve", lambda: nc.vector.tensor_copy(neglam.t[:], self.P[0].t[:, 0:1]), [self.P[0]], [neglam])
            sg = k.sb([128, 1], F32, "sg")
            k.dma("sp", sg, sg.t[:], self.a_sub_g, self.a_sub_g.t[:])
            k.op("dve", lambda: nc.vector.tensor_scalar(sg.t[:], sg.t[:], 1.0 - lam_init, None, ALU.mult), [sg], [sg])
        PTf = Trk(self.PT.t[:].bitcast(F32), "PTf")
        Sg = [(self.P[0], self.P[1]), (self.P[2], PTf)]
        Pb = [k.sb([128, 512], BF16, "Pq%d" % i) for i in range(4)]
        sqs = [k.sb([128, 512], F32, "sq%d" % i) for i in range(2)]
        A0s = [k.sb([128, 512], F32, "A0s%d" % i) for i in range(2)]
        pbi = 0
        pending = []
        epi_i = 0

        def flush():
            while pending:
                pending.pop(0)()

        def make_epi(h, t0, n, A0, sq_):
            def run():
                R = self.P[6]
                k.op("act", lambda: nc.scalar.activation(sq_.t[:, :n], A0.t[:, :n], AF.Square), [A0], [sq_])
                k.mm(R, [(self.ones_f.t[:], sq_.t[:, :n])], [self.ones_f, sq_], out_ap=R.t[:, :n])
                k.op("dve", lambda: nc.vector.tensor_scalar(rs.t[:, :n], R.t[:, :n], 1.0 / 128, EPS, ALU.mult, ALU.add), [R], [rs])
                k.op("act", lambda: nc.scalar.activation(rs.t[:, :n], rs.t[:, :n], AF.Sqrt), [rs], [rs])
                k.op("dve", lambda: nc.vector.reciprocal(rs.t[:, :n], rs.t[:, :n]), [rs], [rs])
                ob = aob[h % 2]
                k.op("dve", lambda: nc.vector.scalar_tensor_tensor(ob.t[:, :n], A0.t[:, :n], sg.t[:, 0:1], rs.t[:, :n], ALU.mult, ALU.mult), [A0, sg, rs], [ob])
                k.dma("sp", self.AO, self.AO.t[h][:, t0:t0 + n], ob, ob.t[:, :n])
            return run

        for h in range(NH):
            kvh = h if m == 0 else h // 4
            Qt = QTs[h % 2]
            if m == 0 or h % 4 == 0:
                Kc, Vc = KTs[kvh % 2], Vs[kvh % 2]
                k.dma("sp", Kc, Kc.t[:], self.KT, self.KT.t[kvh], acc=False)
                k.dma("sp", Vc, Vc.t[:], self.VK, self.VK.t[kvh].rearrange("(a p) d -> p a d", p=128), acc=False)
            Kt, Vt = KTs[kvh % 2], Vs[kvh % 2]
            k.dma("sp", Qt, Qt.t[:], self.QT, self.QT.t[h], acc=False)
            for qi, (t0, n, col) in enumerate(BLOCKS):
                if last and col == 1:
                    continue
                kbs = list(range(2)) if col == 1 else list(range(34))
                groups = [kbs[i:i + 2] for i in range(0, len(kbs), 2)]
                for mp in range(nmap):
                    pr0 = mp * dk if m == 0 else 0
                    oz = mp if m == 0 else qi % 2
                    O, Z = Ops[oz], Zps[oz]

                    def emit_S(gi):
                        banks = Sg[gi % 2]
                        items = [(banks[j], banks[j].t[:, :n], Kt.t[pr0:pr0 + dk, kb * 128:(kb + 1) * 128], Qt.t[pr0:pr0 + dk, t0:t0 + n], True, True, True)
                                 for j, kb in enumerate(groups[gi])]
                        k.mmg(items, [Kt, Qt])
                    emit_S(0)
                    ng_ = len(groups)
                    for gi, g in enumerate(groups):
                        if gi + 1 < ng_:
                            emit_S(gi + 1)
                        if gi == 1 and mp == 0:
                            flush()
                        banks = Sg[gi % 2]
                        Ps = []
                        for j, kb in enumerate(g):
                            P_ = Pb[pbi % 4]
                            pbi += 1
                            S = banks[j]
                            k.op("act", lambda: nc.scalar.activation(P_.t[:, :n], S.t[:, :n], AF.Exp, scale=scale), [S], [P_])
                            Ps.append(P_)
                        items = []
                        for j, kb in enumerate(g):
                            f_ = (gi == 0 and j == 0)
                            l_ = (gi == ng_ - 1 and j == len(g) - 1)
                            items.append((O, O.t[:, :n], Vt.t[:, kb, :], Ps[j].t[:, :n], f_, l_, f_))
                        for j, kb in enumerate(g):
                            f_ = (gi == 0 and j == 0)
                            l_ = (gi == ng_ - 1 and j == len(g) - 1)
                            items.append((Z, Z.t[:, :n], self.ones_b.t[:], Ps[j].t[:, :n], f_, l_, f_))
                        k.mmg(items, [Vt, self.ones_b] + Ps)
                    k.op("dve", lambda: nc.vector.reciprocal(rz.t[:, :n], Z.t[:, :n]), [Z], [rz])
                    if m == 0:
                        A_ = Am[mp] if mp == 1 else A0s[epi_i % 2]
                        k.op("dve", lambda: nc.vector.tensor_tensor(A_.t[:, :n], O.t[:, :n], rz.t[:, :n], ALU.mult), [O, rz], [A_])
                    else:
                        ob = aob[pbi % 2]
                        k.op("dve", lambda: nc.vector.tensor_tensor(ob.t[:, :n], O.t[:, :n], rz.t[:, :n], ALU.mult), [O, rz], [ob])
                        k.dma("sp", self.AO, self.AO.t[h][:, t0:t0 + n], ob, ob.t[:, :n])
                if m == 0:
                    A0, A1 = A0s[epi_i % 2], Am[1]
                    k.op("dve", lambda: nc.vector.scalar_tensor_tensor(A0.t[:, :n], A1.t[:, :n], neglam.t[:, 0:1], A0.t[:, :n], ALU.mult, ALU.add), [A0, A1, neglam], [A0])
                    pending.append(make_epi(h, t0, n, A0, sqs[epi_i % 2]))
                    epi_i += 1
        flush()
        k.release(mark)


    def phase_a_na(self, L):
        k, nc = self.k, self.nc
        scale = 128 ** -0.5
        mark = k.mark()
        Kt = k.sb([128, T], BF16, "naK")
        Qt = k.sb([128, T], BF16, "naQ")
        Ve = k.sb([128, 34, 128], BF16, "naVe")
        Vo = k.sb([128, 31, 128], BF16, "naVo")
        EE = k.sb([128, 14, 64], F32, "naEE")
        MK = k.sb([128, 14, 64], F32, "naMK")
        sbS = [k.sb([128, 256], F32, "naS%d" % i) for i in range(2)]
        Pb = [k.sb([128, 6, 64], BF16, "naP%d" % i) for i in range(2)]
        rz = k.sb([128, 512], F32, "narz")
        aob = [k.sb([128, 512], BF16, "naob%d" % i) for i in range(2)]
        k.dma("sp", MK, MK.t[:], self.d_mask, self.d_mask.t[:])
        Sps = [self.P[0], self.P[1], self.P[2]]
        OZ = [(self.P[3], self.P[4]), (self.P[5], self.P[6])]
        for h in range(NH):
            k.dma("sp", Kt, Kt.t[:], self.KT, self.KT.t[h], acc=False)
            k.dma("sp", Qt, Qt.t[:], self.QT, self.QT.t[h], acc=False)
            k.dma("sp", Ve, Ve.t[:], self.VK, self.VK.t[h].rearrange("(a p) d -> p a d", p=128), acc=False)
            k.dma("sp", Vo, Vo.t[:], self.VK, self.VK.t[h, T_CTX + 64:T_CTX + 64 + 31 * 128, :].rearrange("(a p) d -> p a d", p=128), acc=False)
            k.dma("sp", EE, EE.t[:], self.d_ee, self.d_ee.t[h], acc=False)
            k.op("dve", lambda: nc.vector.tensor_tensor(EE.t[:], EE.t[:], MK.t[:], ALU.add), [EE, MK], [EE])

            def keycols(r, slot):
                rs_ = min(max(r - 4, 0), 56)
                if slot < 4:
                    s0 = T_CTX + rs_ * 64 + 128 * slot
                else:
                    s0 = 128 * (slot - 4)
                return s0

            def vblock(r, slot):
                rs_ = min(max(r - 4, 0), 56)
                if slot >= 4:
                    return Ve.t[:, slot - 4, :]
                if rs_ % 2 == 0:
                    return Ve.t[:, 2 + rs_ // 2 + slot, :]
                return Vo.t[:, (rs_ - 1) // 2 + slot, :]

            def smm(r):
                S = Sps[r % 3]
                for slot in range(6):
                    s0 = keycols(r, slot)
                    k.mm(S, [(Kt.t[:, s0:s0 + 128], Qt.t[:, T_CTX + r * 64:T_CTX + (r + 1) * 64])], [Kt, Qt],
                         out_ap=S.t[:, slot * 64:(slot + 1) * 64])
            smm(0)
            for r in range(64):
                if r + 1 < 64:
                    smm(r + 1)
                S = Sps[r % 3]
                rs_ = min(max(r - 4, 0), 56)
                d0 = rs_ - r + 7
                sb_, P_ = sbS[r % 2], Pb[r % 2]
                k.op("dve", lambda: nc.vector.scalar_tensor_tensor(sb_.t[:].rearrange("p (a b) -> p a b", b=64), S.t[:, 0:256].rearrange("p (a b) -> p a b", b=64),
                                                                  scale, EE.t[:, d0:d0 + 7:2, :], ALU.mult, ALU.add), [S, EE], [sb_])
                k.op("act", lambda: nc.scalar.activation(P_.t[:, 0:4, :], sb_.t[:].rearrange("p (a b) -> p a b", b=64), AF.Exp), [sb_], [P_])
                k.op("act", lambda: nc.scalar.activation(P_.t[:, 4:6, :], S.t[:, 256:384].rearrange("p (a b) -> p a b", b=64), AF.Exp, scale=scale), [S], [P_], acc=True)
                O, Z = OZ[(r // 8) % 2]
                rs8 = r % 8
                for slot in range(6):
                    k.mm(O, [(vblock(r, slot), P_.t[:, slot, :])], [Ve, Vo, P_], first=(slot == 0), last=(slot == 5), out_ap=O.t[:, rs8 * 64:(rs8 + 1) * 64])
                for slot in range(6):
                    k.mm(Z, [(self.ones_b.t[:], P_.t[:, slot, :])], [self.ones_b, P_], first=(slot == 0), last=(slot == 5), out_ap=Z.t[:, rs8 * 64:(rs8 + 1) * 64])
                if rs8 == 7:
                    ob = aob[(r // 8) % 2]
                    k.op("dve", lambda: nc.vector.reciprocal(rz.t[:], Z.t[:]), [Z], [rz])
                    k.op("dve", lambda: nc.vector.tensor_tensor(ob.t[:], O.t[:], rz.t[:], ALU.mult), [O, rz], [ob])
                    tq = T_CTX + (r - 7) * 64
                    k.dma("sp", self.AO, self.AO.t[h][:, tq:tq + 512], ob, ob.t[:])
        k.release(mark)


    def inproj_mamba(self, W, rhs, U, n, t0, st):
        k, nc = self.k, self.nc

        def epi(ci, c, ps):
            tmp = st["t1"] if ci % 2 == 0 else st["t2"]
            k.op("act", lambda: nc.scalar.copy(tmp.t[:, :n], ps.t[:, :n]), [ps], [tmp])
            if c < 16:
                k.dma("sp", self.ZT, self.ZT.t[c][:, t0:t0 + n], tmp, tmp.t[:, :n])
            elif c < 40:
                k.dma("sp", self.XBC, self.XBC.t[c - 16][:, t0:t0 + n], tmp, tmp.t[:, :n])
            else:
                k.dma("sp", self.DTR, self.DTR.t[:, t0:t0 + n], tmp, tmp.t[0:64, :n])
        self.gemm(W, list(range(41)), 1, rhs, [U], n, epi, self.P[0:3])

    def phase_a_mamba(self, L):
        k, nc = self.k, self.nc
        NCH = T // 128
        mark = k.mark()
        tri = k.sb([128, 2, 128], F32, "tri")
        k.dma("sp", tri, tri.t[:], self.c_tri, self.c_tri.t.rearrange("a p q -> p a q"))
        identf = k.sb([128, 128], F32, "identf")
        k.op("dve", lambda: nc.vector.tensor_copy(identf.t[:], self.ident.t[:]), [self.ident], [identf])
        identf64 = k.sb([64, 64], F32, "identf64")
        k.op("dve", lambda: nc.vector.tensor_copy(identf64.t[:], self.ident.t[0:64, 0:64]), [self.ident], [identf64])
        hp = k.sb([64, 4], F32, "hp")
        k.dma("sp", hp, hp.t[:], self.c_hp, self.c_hp.t[:])
        DTt = k.sb([128, NCH, 64], F32, "DTt")
        DTAt = k.sb([128, NCH, 64], F32, "DTAt")
        CS = k.sb([128, NCH, 64], F32, "CS")
        Dcol = k.sb([64, 32], F32, "Dcol")
        m2 = k.mark()
        dtr = k.sb([64, T], F32, "dtr")
        ab = k.sb([64, T], F32, "ab")
        na = k.sb([64, 1], F32, "na")
        k.dma("sp", dtr, dtr.t[:], self.DTR, self.DTR.t[:])
        k.op("dve", lambda: nc.vector.tensor_scalar(dtr.t[:], dtr.t[:], hp.t[:, 1:2], None, ALU.add), [dtr, hp], [dtr])
        k.op("act", lambda: nc.scalar.activation(ab.t[:], dtr.t[:], AF.Abs), [dtr], [ab])
        k.op("act", lambda: nc.scalar.activation(ab.t[:], ab.t[:], AF.Exp, scale=-1.0), [ab], [ab])
        k.op("act", lambda: nc.scalar.activation(ab.t[:], ab.t[:], AF.Ln, bias=1.0), [ab], [ab])
        k.op("dve", lambda: nc.vector.tensor_scalar(dtr.t[:], dtr.t[:], 0.0, None, ALU.max), [dtr], [dtr])
        k.op("dve", lambda: nc.vector.tensor_tensor(dtr.t[:], dtr.t[:], ab.t[:], ALU.add), [dtr, ab], [dtr])
        k.op("act", lambda: nc.scalar.activation(na.t[:], hp.t[:, 0:1], AF.Exp), [hp], [na])
        k.op("dve", lambda: nc.vector.tensor_scalar(na.t[:], na.t[:], -1.0, None, ALU.mult), [na], [na])
        k.op("dve", lambda: nc.vector.tensor_scalar(ab.t[:], dtr.t[:], na.t[:, 0:1], None, ALU.mult), [dtr, na], [ab])
        for (src, dst) in ((dtr, DTt), (ab, DTAt)):
            for c0 in range(0, NCH, 4):
                nn = min(4, NCH - c0)
                Pt = self.P[(c0 // 4) % 3]
                for i in range(nn):
                    c = c0 + i
                    k.transpose(Pt, Pt.t[:, i * 64:(i + 1) * 64], src, src.t[:, c * 128:(c + 1) * 128], identf64)
                k.op("dve", lambda: nc.vector.tensor_copy(dst.t[:, c0:c0 + nn, :], Pt.t[:, :nn * 64].rearrange("p (a b) -> p a b", b=64)), [Pt], [dst], acc=True)
        for c0 in range(0, NCH, 4):
            nn = min(4, NCH - c0)
            Pt = self.P[(c0 // 4) % 3]
            for i in range(nn):
                for d in range(2):
                    k.mm(Pt, [(tri.t[:, d, :], DTAt.t[:, c0 + i, d * 32:(d + 1) * 32])], [tri, DTAt], out_ap=Pt.t[:, i * 64 + d * 32:i * 64 + (d + 1) * 32])
            k.op("dve", lambda: nc.vector.tensor_copy(CS.t[:, c0:c0 + nn, :], Pt.t[:, :nn * 64].rearrange("p (a b) -> p a b", b=64)), [Pt], [CS], acc=True)
        sel = k.sb([64, 32], F32, "sel")
        k.op("dve", lambda: nc.vector.tensor_tensor(sel.t[:], identf.t[0:64, 0:32], identf.t[0:64, 32:64], ALU.add), [identf], [sel])
        k.op("dve", lambda: nc.vector.tensor_scalar(sel.t[:], sel.t[:], hp.t[:, 2:3], None, ALU.mult), [sel, hp], [sel])
        k.mm(self.P[3], [(self.ones_f.t[0:64, 0:64], sel.t[:])], [self.ones_f, sel], out_ap=self.P[3].t[0:64, 0:32])
        k.op("dve", lambda: nc.vector.tensor_copy(Dcol.t[:], self.P[3].t[0:64, 0:32]), [self.P[3]], [Dcol])
        k.release(m2)
        m2 = k.mark()
        cw = k.sb([128, 24, 5], F32, "cw")
        cb = k.sb([128, 24], F32, "cb")
        k.dma("sp", cw, cw.t[:], self.c_conv_wT, self.c_conv_wT.t[:])
        k.dma("sp", cb, cb.t[:], self.c_conv_bT, self.c_conv_bT.t[:])
        xins = [k.sb([128, T], F32, "xin%d" % i) for i in range(3)]
        accs = [k.sb([128, T], F32, "acc%d" % i) for i in range(3)]
        obs = [k.sb([128, T], BF16, "cob%d" % i) for i in range(2)]
        for c in range(24):
            xin, acc = xins[c % 3], accs[c % 3]
            k.dma("sp", xin, xin.t[:], self.XBC, self.XBC.t[c], acc=False)
            en, ee_ = ("dve", nc.vector)
            k.op(en, lambda: ee_.tensor_scalar(acc.t[:], xin.t[:], cw.t[:, c, 2:3], cb.t[:, c:c + 1], ALU.mult, ALU.add), [xin, cw, cb], [acc])
            for j in (0, 1, 3, 4):
                s_ = j - 2
                for (a_, b_) in ((0, T_CTX), (T_CTX, T)):
                    lo, hi = max(a_, a_ - s_), min(b_, b_ - s_)
                    k.op(en, lambda: ee_.scalar_tensor_tensor(acc.t[:, lo:hi], xin.t[:, lo + s_:hi + s_], cw.t[:, c, j:j + 1], acc.t[:, lo:hi], ALU.mult, ALU.add),
                         [xin, cw, acc], [acc])
            if c < 16:
                k.op("act", lambda: nc.scalar.activation(xin.t[:], acc.t[:], AF.Silu), [acc], [xin])
                k.dma("sp", self.XS, self.XS.t[c], xin, xin.t[:])
            else:
                ob = obs[c % 2]
                k.op("act", lambda: nc.scalar.activation(ob.t[:], acc.t[:], AF.Silu), [acc], [ob])
                k.dma("sp", self.BCb, self.BCb.t[c - 16], ob, ob.t[:])
        k.release(m2)
        BT = k.sb([128, T], BF16, "BT")
        CT = k.sb([128, T], BF16, "CT")
        Btok = k.sb([128, NCH, 128], BF16, "Btok")
        xb = k.sb([128, 2, T], BF16, "xb")
        xtok = k.sb([128, NCH, 256], BF16, "xtok")
        h32 = k.sb([128, 4, 64], F32, "h32")
        hbf = k.sb([128, 4, 64], BF16, "hbf")
        CBm = k.sb([128, 128], F32, "CBm")
        Dg = k.sb([128, 4, 128], F32, "Dg")
        Dm = k.sb([128, 4, 128], F32, "Dm")
        Er = k.sb([128, 4, 128], F32, "Er")
        G = k.sb([128, 4, 128], BF16, "G")
        Cd = k.sb([128, 4, 128], BF16, "Cd")
        xdt = k.sb([128, 4, 64], BF16, "xdt")
        xw = k.sb([128, 4, 64], BF16, "xw")
        wc = k.sb([128, 4], F32, "wc")
        y0s = [k.sb([64, 4, 128], F32, "y0s%d" % i) for i in range(2)]
        xc = [k.sb([64, 4, 128], F32, "xc%d" % i) for i in range(2)]
        zc = [k.sb([64, 4, 128], F32, "zc%d" % i) for i in range(2)]
        yt = k.sb([64, 4, 128], F32, "yt")
        yob = [k.sb([64, 4, 128], BF16, "yob%d" % i) for i in range(2)]
        Pcb, Pcs2, PY2, PS2 = self.P[0], (self.P[1], self.P[2]), (self.P[3], self.P[4]), (self.P[5], self.P[6])
        v3 = lambda ap: ap.rearrange("p (a b) -> p a b", b=128)
        it = 0
        for hb in range(8):
            g = hb // 2
            if hb % 2 == 0:
                k.dma("sp", BT, BT.t[:], self.BCb, self.BCb.t[g], acc=False)
                k.dma("sp", CT, CT.t[:], self.BCb, self.BCb.t[4 + g], acc=False)
                for c0 in range(0, NCH, 4):
                    nn = min(4, NCH - c0)
                    for i in range(nn):
                        c = c0 + i
                        k.transpose(self.PT, self.PT.t[:, i * 128:(i + 1) * 128], BT, BT.t[:, c * 128:(c + 1) * 128], self.ident)
                    k.op("dve", lambda: nc.vector.tensor_copy(Btok.t[:, c0:c0 + nn, :], v3(self.PT.t[:, :nn * 128])), [self.PT], [Btok], acc=True)
            k.dma("pool", xb, xb.t[:], self.XS, self.XS.t[2 * hb:2 * hb + 2].rearrange("c p t -> p c t"), acc=False)
            for c0 in range(0, NCH, 2):
                for i in range(2):
                    for a in range(2):
                        c = c0 + i
                        k.transpose(self.PT, self.PT.t[:, i * 256 + a * 128:i * 256 + (a + 1) * 128], xb, xb.t[:, a, c * 128:(c + 1) * 128], self.ident)
                k.op("dve", lambda: nc.vector.tensor_copy(xtok.t[:, c0:c0 + 2, :], self.PT.t[:, :512].rearrange("p (a b) -> p a b", b=256)), [self.PT], [xtok], acc=True)
            for d in range(2):
                k.op("dve", lambda: nc.vector.memset(h32.t[:], 0.0), [], [h32])
                k.op("dve", lambda: nc.vector.memset(hbf.t[:], 0.0), [], [hbf])
                order = list(range(NCH)) if d == 0 else [1, 0] + list(range(NCH - 1, 1, -1))
                dh0 = d * 32 + hb * 4
                e = 127 if d == 0 else 0
                for c in order:
                    it += 1
                    tok = c * 128
                    Pcs, PY, PS_ = Pcs2[it % 2], PY2[it % 2], PS2[it % 2]
                    ysl = self.YS.t[hb * 4:(hb + 1) * 4, :, tok:tok + 128].rearrange("h p t -> p h t")
                    if d == 1:
                        y0, xc_, zc_, ob = y0s[it % 2], xc[it % 2], zc[it % 2], yob[it % 2]
                        k.dma("sp", y0, y0.t[:], self.YS, ysl, acc=False)
                        k.dma("sp", xc_, xc_.t[:], self.XS, self.XS.t[2 * hb:2 * hb + 2, :, tok:tok + 128].rearrange("c (two p) t -> p (c two) t", two=2), acc=False)
                        k.dma("sp", zc_, zc_.t[:], self.ZT, self.ZT.t[2 * hb:2 * hb + 2, :, tok:tok + 128].rearrange("c (two p) t -> p (c two) t", two=2), acc=False)
                    k.mm(Pcb, [(BT.t[:, tok:tok + 128], CT.t[:, tok:tok + 128])], [BT, CT], out_ap=Pcb.t[:, 0:128])
                    k.op("dve", lambda: nc.vector.tensor_tensor(CBm.t[:], Pcb.t[:, 0:128], tri.t[:, d, :], ALU.mult), [Pcb, tri], [CBm])
                    k.op("pool", lambda: nc.gpsimd.tensor_tensor(Dg.t[:], tri.t[:, d, :].unsqueeze(1).to_broadcast([128, 4, 128]),
                                                                DTAt.t[:, c, dh0:dh0 + 4].unsqueeze(2).to_broadcast([128, 4, 128]), ALU.mult), [tri, DTAt], [Dg])
                    k.mm(Pcs, [(self.ones_f.t[:], Dg.t[:].rearrange("p a b -> p (a b)"))], [self.ones_f, Dg])
                    k.op("dve", lambda: nc.vector.tensor_tensor(Dm.t[:], v3(Pcs.t[:]), CS.t[:, c, dh0:dh0 + 4].unsqueeze(2).to_broadcast([128, 4, 128]), ALU.subtract), [Pcs, CS], [Dm])
                    k.op("dve", lambda: nc.vector.tensor_scalar(Dm.t[:], Dm.t[:], 0.0, None, ALU.min), [Dm], [Dm])
                    k.op("act", lambda: nc.scalar.activation(Dm.t[:], Dm.t[:], AF.Exp), [Dm], [Dm])
                    k.op("pool", lambda: nc.gpsimd.tensor_tensor(G.t[:], Dm.t[:], CBm.t[:].unsqueeze(1).to_broadcast([128, 4, 128]), ALU.mult), [Dm, CBm], [G])
                    k.op("act", lambda: nc.scalar.activation(Er.t[:], v3(Pcs.t[:]), AF.Exp), [Pcs], [Er])
                    k.op("pool", lambda: nc.gpsimd.tensor_tensor(Cd.t[:], Er.t[:], CT.t[:, tok:tok + 128].unsqueeze(1).to_broadcast([128, 4, 128]), ALU.mult), [Er, CT], [Cd])
                    k.op("dve", lambda: nc.vector.tensor_tensor(xdt.t[:], xtok.t[:, c, :].rearrange("p (a b) -> p a b", b=64),
                                                                DTt.t[:, c, dh0:dh0 + 4].unsqueeze(2).to_broadcast([128, 4, 64]), ALU.mult), [xtok, DTt], [xdt])
                    for j in range(4):
                        k.mm(PY, [(xdt.t[:, j, :], G.t[:, j, :]), (hbf.t[:, j, :], Cd.t[:, j, :])], [xdt, G, hbf, Cd], out_ap=PY.t[0:64, j * 128:(j + 1) * 128])
                    k.op("dve", lambda: nc.vector.tensor_tensor(wc.t[:], v3(Pcs.t[:])[:, :, e], CS.t[:, c, dh0:dh0 + 4], ALU.subtract), [Pcs, CS], [wc])
                    k.op("act", lambda: nc.scalar.activation(wc.t[:], wc.t[:], AF.Exp), [wc], [wc])
                    k.op("pool", lambda: nc.gpsimd.tensor_tensor(xw.t[:], xdt.t[:], wc.t[:].unsqueeze(2).to_broadcast([128, 4, 64]), ALU.mult), [xdt, wc], [xw])
                    k.mm(PS_, [(Btok.t[:, c, :], xw.t[:].rearrange("p a b -> p (a b)"))], [Btok, xw], out_ap=PS_.t[:, 0:256])
                    k.op("pool", lambda: nc.gpsimd.tensor_tensor(h32.t[:], h32.t[:], Er.t[:, :, e:e + 1].to_broadcast([128, 4, 64]), ALU.mult), [h32, Er], [h32])
                    k.op("dve", lambda: nc.vector.tensor_tensor(h32.t[:], h32.t[:], PS_.t[:, 0:256].rearrange("p (a b) -> p a b", b=64), ALU.add), [h32, PS_], [h32])
                    k.op("act", lambda: nc.scalar.copy(hbf.t[:], h32.t[:]), [h32], [hbf])
                    if d == 0:
                        y0 = y0s[it % 2]
                        k.op("act", lambda: nc.scalar.copy(y0.t[:], v3(PY.t[0:64, :])), [PY], [y0])
                        k.dma("sp", self.YS, ysl, y0, y0.t[:])
                    else:
                        k.op("dve", lambda: nc.vector.tensor_tensor(yt.t[:], xc_.t[:], Dcol.t[:, hb * 4:hb * 4 + 4].unsqueeze(2).to_broadcast([64, 4, 128]), ALU.mult), [xc_, Dcol], [yt])
                        k.op("dve", lambda: nc.vector.tensor_tensor(yt.t[:], yt.t[:], v3(PY.t[0:64, :]), ALU.add), [yt, PY], [yt])
                        k.op("dve", lambda: nc.vector.tensor_tensor(yt.t[:], yt.t[:], y0.t[:], ALU.add), [yt, y0], [yt])
                        k.op("act", lambda: nc.scalar.activation(zc_.t[:], zc_.t[:], AF.Silu), [zc_], [zc_])
                        k.op("dve", lambda: nc.vector.tensor_tensor(ob.t[:], yt.t[:], zc_.t[:], ALU.mult), [yt, zc_], [ob])
                        k.dma("sp", self.AO, self.AO.t[2 * hb:2 * hb + 2, :, tok:tok + 128].rearrange("c (two p) t -> p (c two) t", two=2), ob, ob.t[:])
        k.release(mark)

    def prep_weights(self, L):
        m = L % 4
        wc = self.wcache
        if m == 0:
            wc[("in", L)] = self.conv_w_inter("winA", self.a_w_in, self.a_w_sw, 2 * NH, NH)
            wc[("out", L)] = self.conv_w("woA", self.a_w_out, self.a_w_out.t, 1024, D)
        elif m == 1:
            wc[("in", L)] = self.conv_w_inter("winB", self.b_w_in, self.b_w_sw, NH + 2, 2)
            wc[("out", L)] = self.conv_w("woB", self.b_w_out, self.b_w_out.t, 1024, D)
        elif m == 2:
            wc[("in", L)] = self.conv_w("winC", self.c_w_in, self.c_w_in.t, D, 5248)
            wc[("out", L)] = self.conv_w("woC", self.c_w_out, self.c_w_out.t, 2048, D)
        else:
            wc[("in", L)] = self.conv_w("winD", self.d_w_in, self.d_w_in.t, D, 3072)
            wc[("out", L)] = self.conv_w("woD", self.d_w_out, self.d_w_out.t, 1024, D)
        wc[("w1", L)] = self.conv_w("w1_%d" % L, self.mlp_w1, self.mlp_w1.t[L], D, D_FF)
        wc[("w2", L)] = self.conv_w("w2_%d" % L, self.mlp_w2, self.mlp_w2.t[L], D_FF, D)

    def conv_w_inter(self, name, w, wsw, nqk, nv):
        k = self.k
        nch = 2 * nqk + nv
        dst = k.dram(name, [nch, 128, KC, 128], BF16)
        for c in range(nch):
            if c < 2 * nqk:
                src = w if c % 2 == 0 else wsw
                sc = c // 2
            else:
                src = w
                sc = nqk + (c - 2 * nqk)
            k.dma("pool", dst, dst.t[c], src, src.t[:, sc * 128:(sc + 1) * 128].rearrange("(kc p) n -> p kc n", p=128))
        return dst

    def build(self):
        k = self.k
        self.mod_init()
        self.conv_ada(0)
        self.prep_weights(0)
        self.compute_mod(0)
        self.phase_b(-1)
        for L in range(self.nlayers):
            m = L % 4
            self.inproj_all(L)
            if L == 0 and self.nlayers > 1:
                self.conv_ada(1)
                self.prep_weights(1)
            if L == 1 and self.nlayers > 2:
                self.conv_ada(2)
                self.prep_weights(2)
            if m in (0, 1):
                self.phase_a_dense(L)
            elif m == 2:
                self.phase_a_mamba(L)
            else:
                self.phase_a_na(L)
            self.outproj_all(L)
            if L + 1 < self.nlayers:
                self.compute_mod(L + 1)
            if L == 1 and self.nlayers > 3:
                self.conv_ada(3)
                self.prep_weights(3)
            self.phase_b(L)
        k.fence(self.out)
        if self.dbg and self.nlayers < DEPTH:
            k.fence(self.out_ctx)
        k.close()
        return self.nc


def rope_tables(dim):
    t = np.arange(T_LAT)
    row = (t // GRID_W).astype(np.float32)
    col = (t % GRID_W).astype(np.float32)
    n_pairs = dim // 4
    inv = (10000.0 ** (-np.arange(n_pairs, dtype=np.float32) / n_pairs)).astype(np.float32)
    ang = np.concatenate([row[:, None] * inv, col[:, None] * inv], axis=-1).astype(np.float32)
    cos, sin = np.cos(ang), np.sin(ang)
    tab = np.zeros((2, 128, T), np.float32)
    tab[0, :, :T_CTX] = 1.0
    for p in range(128):
        d = p % dim
        j = d // 2
        tab[0, p, T_CTX:] = cos[:, j]
        tab[1, p, T_CTX:] = -sin[:, j] if d % 2 == 0 else sin[:, j]
    return tab


def swap_pairs_cols(w):
    k, n = w.shape
    return np.ascontiguousarray(w.reshape(k, n // 2, 2)[:, :, ::-1].reshape(k, n))


def fm(v):
    s = v.shape
    a = v.reshape(s[:-1] + (s[-1] // 128, 128))
    return np.ascontiguousarray(np.moveaxis(a, -1, 0))


def make_in_maps(inp, cores):
    f = lambda a: np.ascontiguousarray(np.asarray(a, dtype=np.float32))
    x, c, ctx, c_ctx = f(inp["x"]), f(inp["c"]), f(inp["ctx"]), f(inp["c_ctx"])
    shared = {}
    shared["ada_w"] = f(inp["ada_w"])
    shared["ada_bT"] = fm(f(inp["ada_b"]))
    ln = np.stack([f(inp["ln_g"]), f(inp["ln_b"])], axis=2)
    shared["lnT"] = fm(ln)
    shared["mlp_w1"] = f(inp["mlp_w1"])
    shared["mlp_w2"] = f(inp["mlp_w2"])
    shared["ident"] = np.eye(128, dtype=np.float32).astype(ml_dtypes.bfloat16)
    lam = f(inp["a_lambda"])[0]
    shared["a_lam"] = np.ascontiguousarray(np.concatenate([lam[0], lam[2], lam[1], lam[3]])[None, :])
    shared["a_sub_g"] = np.ascontiguousarray(f(inp["a_sub_g"])[0][:, None])
    shared["ropeA"] = rope_tables(64)
    qg, kg = f(inp["b_qn_g"])[0], f(inp["b_kn_g"])[0]
    sw1 = lambda g: g.reshape(64, 2)[:, ::-1].reshape(128)
    shared["b_g"] = np.ascontiguousarray(np.stack([qg, sw1(qg), kg, sw1(kg)], axis=1))
    shared["ropeB"] = rope_tables(128)
    tri = np.zeros((2, 128, 128), np.float32)
    tri[0] = np.triu(np.ones((128, 128), np.float32))
    tri[1] = np.tril(np.ones((128, 128), np.float32))
    shared["c_tri"] = tri
    kc_ = np.arange(128) % 64
    cq = np.arange(64)
    cstart = np.clip(cq - 8, 0, 48)
    valid = (kc_[:, None] >= cstart[None, :]) & (kc_[:, None] < cstart[None, :] + 16)
    shared["d_mask"] = np.ascontiguousarray(np.broadcast_to(np.where(valid, 0.0, -30000.0).astype(np.float32)[:, None, :], (128, 14, 64)))
    aw, bw, cw, dw = f(inp["a_w_in"])[0], f(inp["b_w_in"])[0], f(inp["c_w_in"])[0], f(inp["d_w_in"])[0]
    awo, bwo, cwo, dwo = f(inp["a_w_out"])[0], f(inp["b_w_out"])[0], f(inp["c_w_out"])[0], f(inp["d_w_out"])[0]
    convw, convb = f(inp["c_conv_w"])[0], f(inp["c_conv_b"])[0]
    alog, dtb, dsk = f(inp["c_A_log"])[0], f(inp["c_dt_bias"])[0], f(inp["c_D"])[0]
    ng = f(inp["c_norm_g"])[0]
    rpb = f(inp["d_rpb"])[0]
    idx = np.clip(np.arange(64)[:, None] - np.arange(64)[None, :] + 15, 0, 30)
    Eh = rpb[:, :, idx]
    ee = np.concatenate([Eh[:, 0:14], Eh[:, 1:15]], axis=2)
    ee = np.ascontiguousarray(ee.transpose(0, 2, 1, 3))
    per_rank = []
    for h in range(2):
        pr = {}
        s1 = slice(1024 * h, 1024 * (h + 1))
        qk = np.concatenate([aw[:, 0:2048][:, s1], aw[:, 2048:4096][:, s1]], axis=1)
        pr["a_w_in"] = np.ascontiguousarray(np.concatenate([qk, aw[:, 4096:6144][:, s1]], axis=1))
        pr["a_w_sw"] = swap_pairs_cols(qk)
        pr["a_w_out"] = np.ascontiguousarray(awo[s1, :])
        s2 = slice(256 * h, 256 * (h + 1))
        qkb = np.concatenate([bw[:, 0:2048][:, s1], bw[:, 2048:2560][:, s2]], axis=1)
        pr["b_w_in"] = np.ascontiguousarray(np.concatenate([qkb, bw[:, 2560:3072][:, s2]], axis=1))
        pr["b_w_sw"] = swap_pairs_cols(qkb)
        pr["b_w_out"] = np.ascontiguousarray(bwo[s1, :])
        sc = slice(2048 * h, 2048 * (h + 1))
        sg = slice(512 * h, 512 * (h + 1))
        sh = slice(32 * h, 32 * (h + 1))
        dtc = np.concatenate([cw[:, 10240:10304][:, sh], cw[:, 10304:10368][:, sh], np.zeros((D, 64), np.float32)], axis=1)
        pr["c_w_in"] = np.ascontiguousarray(np.concatenate([cw[:, 0:4096][:, sc], cw[:, 4096:8192][:, sc], cw[:, 8192:9216][:, sg], cw[:, 9216:10240][:, sg], dtc], axis=1))
        cwo_ = np.concatenate([convw[:, 0:4096][:, sc], convw[:, 4096:5120][:, sg], convw[:, 5120:6144][:, sg]], axis=1)
        pr["c_conv_wT"] = np.ascontiguousarray(np.transpose(fm(cwo_), (0, 2, 1)))
        cbo_ = np.concatenate([convb[0:4096][sc], convb[4096:5120][sg], convb[5120:6144][sg]])
        pr["c_conv_bT"] = fm(cbo_)
        own = lambda v: np.concatenate([v[0][sh], v[1][sh]])
        pr["c_hp"] = np.ascontiguousarray(np.stack([own(alog), own(dtb), own(dsk), np.zeros(64, np.float32)], axis=1))
        pr["c_norm_gT"] = fm(ng[sc])
        pr["c_w_out"] = np.ascontiguousarray(cwo[sc, :])
        pr["d_w_in"] = np.ascontiguousarray(np.concatenate([dw[:, 0:2048][:, s1], dw[:, 2048:4096][:, s1], dw[:, 4096:6144][:, s1]], axis=1))
        pr["d_ee"] = np.ascontiguousarray(ee[8 * h:8 * (h + 1)])
        pr["d_w_out"] = np.ascontiguousarray(dwo[s1, :])
        per_rank.append(pr)
    maps = []
    for core in cores:
        b, h = core // 2, core % 2
        mp = dict(shared)
        mp.update(per_rank[h])
        xa = np.concatenate([ctx[b][128 * h:128 * (h + 1)], x[b][2048 * h:2048 * (h + 1)]], axis=0)
        mp["xT"] = np.ascontiguousarray(xa.T.reshape(KC, 128, TO))
        cc = np.stack([c[b], c_ctx], axis=1)
        mp["cT"] = np.ascontiguousarray(cc.reshape(KC, 128, 2).transpose(1, 0, 2))
        maps.append(mp)
    return maps


_CACHE = {}


def kernel(**inputs):
    if "nc" not in _CACHE:
        _CACHE["nc"] = Prog(4).build()
    nc = _CACHE["nc"]
    cores = list(range(8))
    maps = make_in_maps(inputs, cores)
    res = run_bass_kernel_spmd(nc, maps, core_ids=cores)
    out = np.empty((4, T_LAT, D), np.float32)
    for core, r in zip(cores, res.results):
        b, h = core // 2, core % 2
        o = np.asarray(r["out"]).reshape(D, 2048)
        out[b, 2048 * h:2048 * (h + 1), :] = o.T
    return out
```

```python
import math
import numpy as np
import ml_dtypes
import concourse.bass as bass
import concourse.mybir as mybir
from concourse.bass_utils import run_bass_kernel_spmd

F32 = mybir.dt.float32
BF16 = mybir.dt.bfloat16
AF = mybir.ActivationFunctionType
ALU = mybir.AluOpType

D = 2048
KC = 16
T_CTX = 256
T_LAT = 4096
T = T_CTX + T_LAT
DEPTH = 4
D_FF = 8192
EPS = 1e-6
DN_ALPHA = (2.0 * DEPTH) ** 0.25
GRID_W = 64
C_DI = 4096
NWB = 10
TO = 2176
NH = 8
RG = [[0, 1], [2, 3], [4, 5], [6, 7]]


class Trk:
    __slots__ = ("t", "w", "r", "name")

    def __init__(self, t, name=""):
        self.t = t
        self.w = []
        self.r = []
        self.name = name


class Kern:
    ENG = ("pe", "act", "dve", "pool", "sp")

    def __init__(self, n_dma_slots=(14, 8)):
        self.nc = bass.Bass("TRN2", target_bir_lowering=False)
        nc = self.nc
        self.e = {"pe": nc.tensor, "act": nc.scalar, "dve": nc.vector, "pool": nc.gpsimd, "sp": nc.sync}
        self._ctx = []
        self.sem = {}
        self.cnt = {}
        for k in ("pe", "act", "dve", "pool"):
            self.sem[k] = self._enter(nc.semaphore("s_" + k))
            self.cnt[k] = 0
        self.dq = {}
        for q, n in zip(("sp", "pool"), n_dma_slots):
            slots = []
            for i in range(n):
                key = "d_%s_%d" % (q, i)
                self.sem[key] = self._enter(nc.semaphore(key))
                self.cnt[key] = 0
                slots.append(key)
            self.dq[q] = [slots, 0]
        self.seen = {k: {} for k in self.ENG}
        self.uid = 0

    def _enter(self, cm):
        v = cm.__enter__()
        self._ctx.append(cm)
        return v

    def close(self):
        for cm in reversed(self._ctx):
            cm.__exit__(None, None, None)
        self._ctx = []

    def sb(self, shape, dtype, name=None):
        self.uid += 1
        name = (name or "t") + "_%d" % self.uid
        t = self._enter(self.nc.sbuf_tensor(name, list(shape), dtype))
        return Trk(t, name)

    def ps(self, shape=(128, 512), dtype=F32, name=None):
        self.uid += 1
        name = (name or "p") + "_%d" % self.uid
        t = self._enter(self.nc.psum_tensor(name, list(shape), dtype))
        return Trk(t, name)

    def dram(self, name, shape, dtype, kind="Internal"):
        t = self.nc.dram_tensor(name, list(shape), dtype, kind=kind).ap()
        return Trk(t, name)

    def _wait(self, eng, evs):
        seen = self.seen[eng]
        need = {}
        for (k, v) in evs:
            if seen.get(k, 0) >= v:
                continue
            if need.get(k, 0) < v:
                need[k] = v
        for k, v in need.items():
            self.e[eng].wait_ge(self.sem[k], v)
            seen[k] = v

    def _deps(self, eng, reads, writes, acc):
        evs = []
        for o in reads:
            evs.extend(o.w)
        for o in writes:
            if not acc:
                evs.extend(o.w)
            evs.extend(o.r)
        if eng == "pe":
            evs = [ev for ev in evs if ev[0] != "pe"]
        return evs

    @staticmethod
    def _compact(evs):
        m = {}
        for k, v in evs:
            if m.get(k, 0) < v:
                m[k] = v
        return list(m.items())

    def _commit(self, ev, reads, writes, acc):
        for o in reads:
            o.r.append(ev)
            if len(o.r) > 16:
                o.r = self._compact(o.r)
        for o in writes:
            if acc:
                o.w.append(ev)
                if len(o.w) > 16:
                    o.w = self._compact(o.w)
            else:
                o.w = [ev]
                o.r = []

    def op(self, eng, fn, reads=(), writes=(), acc=False):
        self._wait(eng, self._deps(eng, reads, writes, acc))
        ins = fn()
        self.cnt[eng] += 1
        ins.then_inc(self.sem[eng], 1)
        ev = (eng, self.cnt[eng])
        self._commit(ev, reads, writes, acc)
        return ev

    def mm(self, out, pairs, reads, first=True, last=True, out_ap=None):
        self._wait("pe", self._deps("pe", reads, [out], not first))
        oap = out_ap if out_ap is not None else out.t[:]
        n = len(pairs)
        ins = None
        for i, (l, r) in enumerate(pairs):
            ins = self.nc.tensor.matmul(oap, l, r, start=(first and i == 0), stop=(last and i == n - 1))
        self.cnt["pe"] += 1
        ins.then_inc(self.sem["pe"], 1)
        ev = ("pe", self.cnt["pe"])
        self._commit(ev, reads, [out], not first)
        return ev

    def mmg(self, items, reads):
        evs = []
        for o in reads:
            evs.extend(o.w)
        for it in items:
            evs.extend(it[0].r)
        evs = [ev for ev in evs if ev[0] != "pe"]
        self._wait("pe", evs)
        ins = None
        for it in items:
            ins = self.nc.tensor.matmul(it[1], it[2], it[3], start=it[4], stop=it[5])
        self.cnt["pe"] += 1
        ins.then_inc(self.sem["pe"], 1)
        ev = ("pe", self.cnt["pe"])
        for o in reads:
            o.r.append(ev)
            if len(o.r) > 16:
                o.r = self._compact(o.r)
        for it in items:
            o = it[0]
            if it[6]:
                o.w = [ev]
                o.r = []
            else:
                o.w.append(ev)
                if len(o.w) > 16:
                    o.w = self._compact(o.w)
        return ev

    def transpose(self, out, out_ap, in_, in_ap, ident):
        self._wait("pe", self._deps("pe", [in_, ident], [out], True))
        ins = self.nc.tensor.transpose(out_ap, in_ap, ident.t[:])
        self.cnt["pe"] += 1
        ins.then_inc(self.sem["pe"], 1)
        ev = ("pe", self.cnt["pe"])
        self._commit(ev, [in_, ident], [out], True)
        return ev

    def dma(self, q, out, out_ap, in_, in_ap, acc=True, **kw):
        slots, idx = self.dq[q]
        key = slots[idx % len(slots)]
        self.dq[q][1] = idx + 1
        evs = self._deps(q, [in_], [out], acc)
        if self.cnt[key] > 0:
            evs.append((key, self.cnt[key]))
        self._wait(q, evs)
        ins = self.e[q].dma_start(out=out_ap, in_=in_ap, **kw)
        self.cnt[key] += 16
        ins.then_inc(self.sem[key], 16)
        ev = (key, self.cnt[key])
        self._commit(ev, [in_], [out], acc)
        return ev

    def collective(self, kind, op, in_, in_ap, out, out_ap):
        self.uid += 1
        key = "cc%d" % self.uid
        self.sem[key] = self._enter(self.nc.semaphore(key))
        self.cnt[key] = 0
        self._wait("pool", self._deps("pool", [in_], [out], False))
        ins = self.nc.gpsimd.collective_compute(kind, op, replica_groups=RG, ins=[in_ap], outs=[out_ap])
        ins.then_inc(self.sem[key], 1)
        self.cnt[key] = 1
        ev = (key, 1)
        self._commit(ev, [in_], [out], False)
        return ev

    def fence(self, obj, eng="sp"):
        self._wait(eng, list(obj.w))

    def barrier(self):
        evs = [(key, v) for key, v in self.cnt.items() if v > 0]
        for eng in self.ENG:
            self._wait(eng, [ev for ev in evs if not (eng == "pe" and ev[0] == "pe")])

    def mark(self):
        return len(self._ctx)

    def release(self, mark):
        self.barrier()
        while len(self._ctx) > mark:
            self._ctx.pop().__exit__(None, None, None)


BLOCKS = [(0, 256, 1)] + [(T_CTX + i * 512, 512, 0) for i in range(8)]
OWN_BLOCKS = [(0, 128, 1)] + [(128 + i * 512, 512, 0) for i in range(4)]
ALL_BLOCKS = [(0, 0, 128, 0, 1), (1, 0, 128, 128, 1)] + \
    [(r, 128 + i * 512, 512, T_CTX + r * 2048 + i * 512, 0) for r in range(2) for i in range(4)]


class Prog:
    def __init__(self, nlayers=4, dbg=False):
        self.nlayers = nlayers
        self.k = k = Kern()
        self.nc = nc = k.nc
        self.dbg = dbg
        ein = lambda name, shape, dt=F32: k.dram(name, shape, dt, kind="ExternalInput")
        self.xT = ein("xT", [KC, 128, TO])
        self.cT = ein("cT", [128, KC, 2])
        self.ada_w = ein("ada_w", [DEPTH, D, 6 * D])
        self.ada_bT = ein("ada_bT", [128, DEPTH, 96])
        self.lnT = ein("lnT", [128, DEPTH, 2, 2, KC])
        self.mlp_w1 = ein("mlp_w1", [DEPTH, D, D_FF])
        self.mlp_w2 = ein("mlp_w2", [DEPTH, D_FF, D])
        self.ident_in = ein("ident", [128, 128], BF16)
        self.a_w_in = ein("a_w_in", [D, 3072])
        self.a_w_sw = ein("a_w_sw", [D, 2048])
        self.a_lam = ein("a_lam", [1, 256])
        self.a_sub_g = ein("a_sub_g", [128, 1])
        self.a_w_out = ein("a_w_out", [1024, D])
        self.ropeA = ein("ropeA", [2, 128, T])
        self.b_w_in = ein("b_w_in", [D, 1536])
        self.b_w_sw = ein("b_w_sw", [D, 1280])
        self.b_g = ein("b_g", [128, 4])
        self.b_w_out = ein("b_w_out", [1024, D])
        self.ropeB = ein("ropeB", [2, 128, T])
        self.c_w_in = ein("c_w_in", [D, 5248])
        self.c_conv_wT = ein("c_conv_wT", [128, 24, 5])
        self.c_conv_bT = ein("c_conv_bT", [128, 24])
        self.c_hp = ein("c_hp", [64, 4])
        self.c_norm_gT = ein("c_norm_gT", [128, 16])
        self.c_w_out = ein("c_w_out", [2048, D])
        self.c_tri = ein("c_tri", [2, 128, 128])
        self.d_w_in = ein("d_w_in", [D, 3072])
        self.d_ee = ein("d_ee", [8, 128, 14, 64])
        self.d_mask = ein("d_mask", [128, 14, 64])
        self.d_w_out = ein("d_w_out", [1024, D])
        self.out = k.dram("out", [KC, 128, 2048], F32, kind="ExternalOutput")
        if dbg:
            self.out_ctx = k.dram("out_ctx", [KC, 128, 128], F32, kind="ExternalOutput")
        self.XT = k.dram("XTs", [KC, 128, TO], F32)
        self.UGi = [k.dram("UGi%d" % j, [KC * 128, n // 2], F32) for j, (l0, n, col) in enumerate(OWN_BLOCKS)]
        self.UG = [k.dram("UG%d" % j, [2 * KC * 128, n // 2], F32) for j, (l0, n, col) in enumerate(OWN_BLOCKS)]
        self.YP = [k.dram("YP%d" % j, [2 * 2 * 128, TO], F32) for j in range(8)] + [k.dram("YP8", [2 * 128, TO], F32)]
        self.YR = [k.dram("YR%d" % j, [2 * 128, TO], F32) for j in range(8)] + [k.dram("YR8", [128, TO], F32)]
        self.AO = k.dram("AOs", [16, 128, T], BF16)
        self.QT = k.dram("QTs", [NH, 128, T], BF16)
        self.KT = k.dram("KTs", [NH, 128, T], BF16)
        self.VK = k.dram("VKs", [NH, T, 128], BF16)
        self.ZT = k.dram("ZTs", [16, 128, T], F32)
        self.XBC = k.dram("XBCs", [24, 128, T], F32)
        self.DTR = k.dram("DTRs", [64, T], F32)
        self.XS = k.dram("XSs", [16, 128, T], F32)
        self.BCb = k.dram("BCbs", [8, 128, T], BF16)
        self.YS = k.dram("YSs", [32, 64, T], F32)
        self.P = [k.ps([128, 512], F32, "P%d" % i) for i in range(7)]
        self.PT = k.ps([128, 1024], BF16, "PT")
        self.ident = k.sb([128, 128], BF16, "ident")
        self.ones_f = k.sb([128, 128], F32, "ones_f")
        self.ones_b = k.sb([128, 128], BF16, "ones_b")
        self.mod = k.sb([128, DEPTH, 96, 2], F32, "mod")
        self.ln = k.sb([128, DEPTH, 2, 2, KC], F32, "ln")
        self.wb = [k.sb([128, 16, 128], BF16, "wb%d" % i) for i in range(NWB)]
        self.wbi = 0
        k.dma("sp", self.ident, self.ident.t[:], self.ident_in, self.ident_in.t[:])
        k.dma("sp", self.ln, self.ln.t[:], self.lnT, self.lnT.t[:])
        k.op("dve", lambda: nc.vector.memset(self.ones_f.t[:], 1.0), [], [self.ones_f])
        k.op("dve", lambda: nc.vector.memset(self.ones_b.t[:], 1.0), [], [self.ones_b])
        self.wcache = {}
        self.Xc = [Trk(None, "Xc%d" % i) for i in range(KC)]

    def conv_w(self, name, src, src_ap, K, ncols):
        k = self.k
        nch = ncols // 128
        kc = K // 128
        dst = k.dram(name, [nch, 128, kc, 128], BF16)
        for c in range(nch):
            for k0 in range(0, kc, 16):
                k1 = min(kc, k0 + 16)
                k.dma("pool", dst, dst.t[c][:, k0:k1, :], src,
                      src_ap[k0 * 128:k1 * 128, c * 128:(c + 1) * 128].rearrange("(kc p) n -> p kc n", p=128))
        return dst

    def gemm(self, wblk, chunks, kgroups, rhs_ap, rhs_trks, n, epi, psums, kper=16):
        k = self.k
        seq = [(ci, c, kg) for ci, c in enumerate(chunks) for kg in range(kgroups)]
        bufs = {}
        issued = 0
        PF = 8
        ps = None
        for j in range(len(seq)):
            while issued < min(len(seq), j + PF):
                ci, c, kg = seq[issued]
                wb = self.wb[self.wbi % NWB]
                self.wbi += 1
                k.dma("sp", wb, wb.t[:, :kper, :], wblk, wblk.t[c][:, kg * kper:(kg + 1) * kper, :], acc=False)
                bufs[issued] = wb
                issued += 1
            ci, c, kg = seq[j]
            if kg == 0:
                ps = psums[ci % len(psums)]
            wb = bufs.pop(j)
            pairs = [(wb.t[:, i, :], rhs_ap(kg * kper + i)) for i in range(kper)]
            k.mm(ps, pairs, [wb] + list(rhs_trks), first=(kg == 0), last=(kg == kgroups - 1), out_ap=ps.t[:, :n])
            if kg == kgroups - 1:
                epi(ci, c, ps)

    def mod_init(self):
        k, nc = self.k, self.nc
        self.sc = k.sb([128, KC, 2], BF16, "sc")
        self.adb = k.sb([128, DEPTH, 96], F32, "adb")
        mark = k.mark()
        cs = k.sb([128, KC, 2], F32, "cs")
        k.dma("sp", cs, cs.t[:], self.cT, self.cT.t[:])
        k.dma("sp", self.adb, self.adb.t[:], self.ada_bT, self.ada_bT.t[:])
        k.op("act", lambda: nc.scalar.activation(self.sc.t[:], cs.t[:], AF.Silu), [cs], [self.sc])
        k.release(mark)

    def conv_ada(self, L):
        self.wcache[("ada", L)] = self.conv_w("adaB%d" % L, self.ada_w, self.ada_w.t[L], D, 6 * D)

    def compute_mod(self, L):
        k, nc = self.k, self.nc
        wblk = self.wcache[("ada", L)]
        sc, adb = self.sc, self.adb

        def epi(ci, c, ps):
            k.op("dve", lambda: nc.vector.tensor_scalar(self.mod.t[:, L, c, :], ps.t[:, 0:2], adb.t[:, L, c:c + 1], None, ALU.add),
                 [ps, adb], [self.mod], acc=True)
        self.gemm(wblk, list(range(96)), 1, lambda kc: sc.t[:, kc, :], [sc], 2, epi, self.P[0:3])
        for j in (1, 4):
            k.op("dve", lambda: nc.vector.tensor_scalar(self.mod.t[:, L, j * 16:(j + 1) * 16, :], self.mod.t[:, L, j * 16:(j + 1) * 16, :], 1.0, None, ALU.add),
                 [self.mod], [self.mod])

    def mv(self, L, j, dc, col):
        return self.mod.t[:, L, j * 16 + dc, col:col + 1]

    def layer_norm(self, X, n, L, which, tmp, st):
        k, nc = self.k, self.nc
        S1, S2 = self.P[3], self.P[4]
        for dc in range(KC):
            xb_, sqb = st["lb"][dc % 2], st["lb"][2 + dc % 2]
            k.op("act", lambda: nc.scalar.copy(xb_.t[:, :n], X.t[:, dc, :n]), [X], [xb_])
            k.mm(S1, [(self.ones_b.t[:], xb_.t[:, :n])], [self.ones_b, xb_], first=(dc == 0), last=(dc == KC - 1), out_ap=S1.t[:, :n])
            k.op("act", lambda: nc.scalar.activation(sqb.t[:, :n], X.t[:, dc, :n], AF.Square), [X], [sqb])
            k.mm(S2, [(self.ones_b.t[:], sqb.t[:, :n])], [self.ones_b, sqb], first=(dc == 0), last=(dc == KC - 1), out_ap=S2.t[:, :n])
        mean, rstd = st["mean"], st["rstd"]
        k.op("act", lambda: nc.scalar.mul(mean.t[:, :n], S1.t[:, :n], 1.0 / D), [S1], [mean])
        k.op("dve", lambda: nc.vector.tensor_tensor(rstd.t[:, :n], mean.t[:, :n], mean.t[:, :n], ALU.mult), [mean], [rstd])
        k.op("dve", lambda: nc.vector.scalar_tensor_tensor(rstd.t[:, :n], S2.t[:, :n], 1.0 / D, rstd.t[:, :n], ALU.mult, ALU.subtract), [S2, rstd], [rstd])
        k.op("dve", lambda: nc.vector.tensor_scalar(rstd.t[:, :n], rstd.t[:, :n], EPS, None, ALU.add), [rstd], [rstd])
        k.op("act", lambda: nc.scalar.activation(rstd.t[:, :n], rstd.t[:, :n], AF.Sqrt), [rstd], [rstd])
        k.op("dve", lambda: nc.vector.reciprocal(rstd.t[:, :n], rstd.t[:, :n]), [rstd], [rstd])
        for dc in range(KC):
            self.Xc[dc].w = list(X.w)
            self.Xc[dc].r = list(X.r)
        for dc in range(KC):
            en, ee_ = ("dve", nc.vector)
            Xc = self.Xc[dc]
            k.op(en, lambda: ee_.tensor_tensor(X.t[:, dc, :n], X.t[:, dc, :n], mean.t[:, :n], ALU.subtract), [Xc, mean], [Xc])
            k.op(en, lambda: ee_.tensor_tensor(X.t[:, dc, :n], X.t[:, dc, :n], rstd.t[:, :n], ALU.mult), [Xc, rstd], [Xc])
            k.op(en, lambda: ee_.tensor_scalar(X.t[:, dc, :n], X.t[:, dc, :n], self.ln.t[:, L, which, 0, dc:dc + 1], self.ln.t[:, L, which, 1, dc:dc + 1], ALU.mult, ALU.add),
                 [Xc, self.ln], [Xc])
        X.w = []
        X.r = []
        for dc in range(KC):
            X.w.extend(self.Xc[dc].w)
            self.Xc[dc].w = []
            self.Xc[dc].r = []

    def modulate(self, U, X, n, L, js, jb, col):
        k, nc = self.k, self.nc
        for dc in range(KC):
            k.op("act", lambda: nc.scalar.activation(U.t[:, dc, :n], X.t[:, dc, :n], AF.Identity, bias=self.mv(L, jb, dc, col), scale=self.mv(L, js, dc, col)),
                 [X, self.mod], [U])

    def store_v_tok(self, ps, n, t0, head, st):
        k, nc = self.k, self.nc
        vt, vtok = st["vt"], st["vtok"][head % 2]
        k.op("act", lambda: nc.scalar.copy(vt.t[:, :n], ps.t[:, :n]), [ps], [vt])
        npc = n // 128
        for pc in range(npc):
            k.transpose(self.PT, self.PT.t[:, pc * 128:(pc + 1) * 128], vt, vt.t[:, pc * 128:(pc + 1) * 128], self.ident)
        k.op("dve", lambda: nc.vector.tensor_copy(vtok.t[:, :npc, :], self.PT.t[:, :npc * 128].rearrange("p (a b) -> p a b", b=128)), [self.PT], [vtok])
        k.dma("sp", self.VK, self.VK.t[head, t0:t0 + n, :].rearrange("(a p) d -> p a d", p=128), vtok, vtok.t[:, :npc, :])

    def inproj(self, L, U, n, t0, st):
        k, nc = self.k, self.nc
        m = L % 4
        W = self.wcache[("in", L)]
        rhs = lambda kc: U.t[:, kc, :n]
        if m == 0 or m == 1:
            rope = self.ropeA if m == 0 else self.ropeB
            cs = st["rope"]
            k.dma("sp", cs, cs.t[:, :, :n], rope, rope.t[:, :, t0:t0 + n].rearrange("a p t -> p a t"), acc=False)
            nq = NH
            nk = NH if m == 0 else 2
            nqk = nq + nk
            hold = {}

            def epi(ci, c, ps):
                if ci < 2 * nqk:
                    idx, sw = ci // 2, ci % 2
                    if sw == 0:
                        hold["p"] = ps
                        return
                    p1, p2 = hold["p"], ps
                    isq = idx < nq
                    dst = self.QT if isq else self.KT
                    dch = idx if isq else idx - nq
                    t1, t2, ob = st["t1"], st["t2"], st["ob"][idx % 2]
                    if m == 0:
                        k.op("dve", lambda: nc.vector.tensor_tensor(t1.t[:, :n], p1.t[:, :n], cs.t[:, 0, :n], ALU.mult), [p1, cs], [t1])
                        k.op("dve", lambda: nc.vector.tensor_tensor(t2.t[:, :n], p2.t[:, :n], cs.t[:, 1, :n], ALU.mult), [p2, cs], [t2])
                        k.op("dve", lambda: nc.vector.tensor_tensor(ob.t[:, :n], t1.t[:, :n], t2.t[:, :n], ALU.add), [t1, t2], [ob])
                    else:
                        g0 = 0 if isq else 2
                        R = self.P[5]
                        k.op("act", lambda: nc.scalar.activation(t1.t[:, :n], p1.t[:, :n], AF.Square), [p1], [t1])
                        k.mm(R, [(self.ones_f.t[:], t1.t[:, :n])], [self.ones_f, t1], out_ap=R.t[:, :n])
                        rs = st["rs"]
                        k.op("dve", lambda: nc.vector.tensor_scalar(rs.t[:, :n], R.t[:, :n], 1.0 / 128, EPS, ALU.mult, ALU.add), [R], [rs])
                        k.op("act", lambda: nc.scalar.activation(rs.t[:, :n], rs.t[:, :n], AF.Sqrt), [rs], [rs])
                        k.op("dve", lambda: nc.vector.reciprocal(rs.t[:, :n], rs.t[:, :n]), [rs], [rs])
                        k.op("dve", lambda: nc.vector.scalar_tensor_tensor(t1.t[:, :n], p1.t[:, :n], st["bg"].t[:, g0:g0 + 1], cs.t[:, 0, :n], ALU.mult, ALU.mult), [p1, cs, st["bg"]], [t1])
                        k.op("dve", lambda: nc.vector.scalar_tensor_tensor(t2.t[:, :n], p2.t[:, :n], st["bg"].t[:, g0 + 1:g0 + 2], cs.t[:, 1, :n], ALU.mult, ALU.mult), [p2, cs, st["bg"]], [t2])
                        k.op("dve", lambda: nc.vector.tensor_tensor(t1.t[:, :n], t1.t[:, :n], t2.t[:, :n], ALU.add), [t1, t2], [t1])
                        k.op("dve", lambda: nc.vector.tensor_tensor(ob.t[:, :n], t1.t[:, :n], rs.t[:, :n], ALU.mult), [t1, rs], [ob])
                    k.dma("sp", dst, dst.t[dch][:, t0:t0 + n], ob, ob.t[:, :n])
                else:
                    self.store_v_tok(ps, n, t0, ci - 2 * nqk, st)
            nch = 2 * nqk + (NH if m == 0 else 2)
            self.gemm(W, list(range(nch)), 1, rhs, [U], n, epi, self.P[0:3])
        elif m == 3:
            def epi(ci, c, ps):
                if ci < 2 * NH:
                    dst = self.QT if ci < NH else self.KT
                    ob = st["ob"][ci % 2]
                    k.op("act", lambda: nc.scalar.copy(ob.t[:, :n], ps.t[:, :n]), [ps], [ob])
                    k.dma("sp", dst, dst.t[ci % NH][:, t0:t0 + n], ob, ob.t[:, :n])
                else:
                    self.store_v_tok(ps, n, t0, ci - 2 * NH, st)
            self.gemm(W, list(range(3 * NH)), 1, rhs, [U], n, epi, self.P[0:3])
        else:
            self.inproj_mamba(W, rhs, U, n, t0, st)

    def make_st(self):
        k = self.k
        st = {
            "mean": k.sb([128, 512], F32), "rstd": k.sb([128, 512], F32),
            "t1": k.sb([128, 512], F32), "t2": k.sb([128, 512], F32), "rs": k.sb([128, 512], F32),
            "ob": [k.sb([128, 512], BF16), k.sb([128, 512], BF16)],
            "rope": k.sb([128, 2, 512], F32), "vt": k.sb([128, 512], BF16),
            "vtok": [k.sb([128, 4, 128], BF16), k.sb([128, 4, 128], BF16)],
            "bg": k.sb([128, 4], F32), "ng": k.sb([128, 16], F32),
        }
        k.dma("sp", st["bg"], st["bg"].t[:], self.b_g, self.b_g.t[:])
        k.dma("sp", st["ng"], st["ng"].t[:], self.c_norm_gT, self.c_norm_gT.t[:])
        return st

    def inproj_all(self, L):
        k, nc = self.k, self.nc
        mark = k.mark()
        st = self.make_st()
        Us = [k.sb([128, KC, 512], BF16, "Ua%d" % i) for i in range(2)]
        for bi, (r, l0, n, t0, col) in enumerate(ALL_BLOCKS):
            U = Us[bi % 2]
            j = [x[0] for x in OWN_BLOCKS].index(l0)
            UGv = self.UG[j].t.bitcast(BF16).rearrange("(r c p) t -> r p c t", r=2, p=128)
            k.dma("sp", U, U.t[:, :, :n], self.UG[j], UGv[r], acc=False)
            self.inproj(L, U, n, t0, st)
        k.release(mark)

    def outproj_all(self, L):
        k, nc = self.k, self.nc
        m = L % 4
        mark = k.mark()
        kco = 16 if m == 2 else NH
        st = self.make_st()
        As = [k.sb([128, kco, 512], BF16, "Aa%d" % i) for i in range(2)]
        yo = [k.sb([128, 512], F32, "yo%d" % i) for i in range(3)]
        Wo = self.wcache[("out", L)]
        YPv = [self.YP[j].t.rearrange("(r c p) t -> r c p t", r=2, p=128) for j in range(9)]
        yi = 0

        def run_block(bi, blk, chunks):
            nonlocal yi
            (r, l0, n, t0, col) = blk
            A = As[bi % 2]
            k.dma("sp", A, A.t[:, :, :n], self.AO, self.AO.t[0:kco, :, t0:t0 + n].rearrange("c p t -> p c t"), acc=False)
            if m == 2:
                R = self.P[5]
                for c in range(16):
                    tq = st["t1"] if c % 2 == 0 else st["t2"]
                    k.op("act", lambda: nc.scalar.activation(tq.t[:, :n], A.t[:, c, :n], AF.Square), [A], [tq])
                    k.mm(R, [(self.ones_f.t[:], tq.t[:, :n])], [self.ones_f, tq], first=(c == 0), last=(c == 15), out_ap=R.t[:, :n])
                y = yo[yi % 3]
                yi += 1
                k.op("act", lambda: nc.scalar.copy(y.t[:, :n], R.t[:, :n]), [R], [y])
                k.dma("sp", self.YP[8], YPv[8][r][0][:, l0:l0 + n], y, y.t[:, :n])
                for c in range(16):
                    k.op("dve", lambda: nc.vector.tensor_scalar(A.t[:, c, :n], A.t[:, c, :n], st["ng"].t[:, c:c + 1], None, ALU.mult), [A, st["ng"]], [A])

            def epi(ci, c, ps):
                nonlocal yi
                y = yo[yi % 3]
                yi += 1
                k.op("act", lambda: nc.scalar.copy(y.t[:, :n], ps.t[:, :n]), [ps], [y])
                k.dma("sp", self.YP[c // 2], YPv[c // 2][r][c % 2][:, l0:l0 + n], y, y.t[:, :n])
            self.gemm(Wo, chunks, 1, lambda kc: A.t[:, kc, :n], [A], n, epi, self.P[0:3], kper=kco)

        if m == 2:
            for bi, blk in enumerate(ALL_BLOCKS):
                run_block(bi, blk, list(range(16)))
            for j in range(9):
                k.collective("ReduceScatter", ALU.add, self.YP[j], self.YP[j].t[:], self.YR[j], self.YR[j].t[:])
        else:
            AR = k.sb([128, NH, T], BF16, "AOres")
            for c in range(NH):
                k.dma("sp", AR, AR.t[:, c, :], self.AO, self.AO.t[c], acc=(c > 0))
            wq = {}

            def loadw(j):
                ws = []
                for ci in range(2):
                    wb = self.wb[self.wbi % NWB]
                    self.wbi += 1
                    k.dma("sp", wb, wb.t[:, :NH, :], Wo, Wo.t[2 * j + ci][:, 0:NH, :], acc=False)
                    ws.append(wb)
                wq[j] = ws
            loadw(0)
            pi = 0
            for j in range(8):
                if j + 1 < 8:
                    loadw(j + 1)
                ws = wq.pop(j)
                for (r, l0, n, t0, col) in ALL_BLOCKS:
                    for ci in range(2):
                        ps = self.P[pi % 3]
                        pi += 1
                        k.mm(ps, [(ws[ci].t[:, i, :], AR.t[:, i, t0:t0 + n]) for i in range(NH)], [ws[ci], AR], out_ap=ps.t[:, :n])
                        y = yo[yi % 3]
                        yi += 1
                        k.op("act", lambda: nc.scalar.copy(y.t[:, :n], ps.t[:, :n]), [ps], [y])
                        k.dma("sp", self.YP[j], YPv[j][r][ci][:, l0:l0 + n], y, y.t[:, :n])
                k.collective("ReduceScatter", ALU.add, self.YP[j], self.YP[j].t[:], self.YR[j], self.YR[j].t[:])
        k.release(mark)

    def phase_b(self, L):
        k, nc = self.k, self.nc
        last = (L == self.nlayers - 1)
        mark = k.mark()
        X = k.sb([128, KC, 512], F32, "X")
        U = k.sb([128, KC, 512], BF16, "U")
        st = {
            "mean": k.sb([128, 512], F32), "rstd": k.sb([128, 512], F32),
            "t1": k.sb([128, 512], F32), "t2": k.sb([128, 512], F32), "rs": k.sb([128, 512], F32),
            "lb": [k.sb([128, 512], BF16, "lb%d" % i) for i in range(4)],
        }
        ys = [k.sb([128, 512], F32, "ys%d" % i) for i in range(3)]
        YRv = [self.YR[j].t.rearrange("(c p) t -> c p t", p=128) for j in range(9)]
        if L >= 0:
            H = k.sb([128, 64, 512], BF16, "H")
            W1 = self.wcache[("w1", L)]
            W2 = self.wcache[("w2", L)]
        for (l0, n, col) in OWN_BLOCKS:
            if last and col == 1 and self.nlayers == DEPTH:
                continue
            src = self.XT if L >= 0 else self.xT
            k.dma("sp", X, X.t[:, :, :n], src, src.t[:, :, l0:l0 + n].rearrange("c p t -> p c t"), acc=False)
            if L >= 0:
                k.op("act", lambda: nc.scalar.mul(X.t[:, :, :n], X.t[:, :, :n], DN_ALPHA), [X], [X])
                if L % 4 == 2:
                    rs = st["rs"]
                    k.dma("sp", rs, rs.t[:, :n], self.YR[8], YRv[8][0][:, l0:l0 + n], acc=False)
                    k.op("dve", lambda: nc.vector.tensor_scalar(rs.t[:, :n], rs.t[:, :n], 1.0 / C_DI, EPS, ALU.mult, ALU.add), [rs], [rs])
                    k.op("act", lambda: nc.scalar.activation(rs.t[:, :n], rs.t[:, :n], AF.Sqrt), [rs], [rs])
                    k.op("dve", lambda: nc.vector.reciprocal(rs.t[:, :n], rs.t[:, :n]), [rs], [rs])
                for c in range(KC):
                    y = ys[c % 3]
                    k.dma("sp", y, y.t[:, :n], self.YR[c // 2], YRv[c // 2][c % 2][:, l0:l0 + n], acc=False)
                    if L % 4 == 2:
                        k.op("dve", lambda: nc.vector.tensor_tensor(y.t[:, :n], y.t[:, :n], st["rs"].t[:, :n], ALU.mult), [y, st["rs"]], [y])
                    k.op("dve", lambda: nc.vector.scalar_tensor_tensor(X.t[:, c, :n], y.t[:, :n], self.mv(L, 2, c, col), X.t[:, c, :n], ALU.mult, ALU.add),
                         [y, X, self.mod], [X])
                self.layer_norm(X, n, L, 0, st["t1"], st)
                self.modulate(U, X, n, L, 4, 3, col)
                k.op("act", lambda: nc.scalar.mul(X.t[:, :, :n], X.t[:, :, :n], DN_ALPHA), [X], [X])

                def epi_1(ci, c, ps):
                    tmp = st["t1"] if ci % 2 == 0 else st["t2"]
                    k.op("act", lambda: nc.scalar.activation(tmp.t[:, :n], ps.t[:, :n], AF.Relu), [ps], [tmp])
                    k.op("dve", lambda: nc.vector.tensor_tensor(H.t[:, c, :n], tmp.t[:, :n], tmp.t[:, :n], ALU.mult), [tmp], [H], acc=True)
                self.gemm(W1, list(range(64)), 1, lambda kc: U.t[:, kc, :n], [U], n, epi_1, self.P[0:3])

                def epi_2(ci, c, ps):
                    k.op("dve", lambda: nc.vector.scalar_tensor_tensor(X.t[:, c, :n], ps.t[:, :n], self.mv(L, 5, c, col), X.t[:, c, :n], ALU.mult, ALU.add),
                         [ps, X, self.mod], [X])
                self.gemm(W2, list(range(16)), 4, lambda kc: H.t[:, kc, :n], [H], n, epi_2, self.P[0:3])
                self.layer_norm(X, n, L, 1, st["t1"], st)
            if last:
                if col == 0:
                    k.dma("sp", self.out, self.out.t[:, :, l0 - 128:l0 - 128 + n].rearrange("c p t -> p c t"), X, X.t[:, :, :n])
                else:
                    k.dma("sp", self.out_ctx, self.out_ctx.t[:, :, 0:n].rearrange("c p t -> p c t"), X, X.t[:, :, :n])
            else:
                k.dma("sp", self.XT, self.XT.t[:, :, l0:l0 + n].rearrange("c p t -> p c t"), X, X.t[:, :, :n])
                self.modulate(U, X, n, L + 1, 1, 0, col)
                j = [x[0] for x in OWN_BLOCKS].index(l0)
                UGiv = self.UGi[j].t.bitcast(BF16).rearrange("(c p) t -> p c t", p=128)
                k.dma("sp", self.UGi[j], UGiv, U, U.t[:, :, :n])
                k.collective("AllGather", ALU.bypass, self.UGi[j], self.UGi[j].t[:], self.UG[j], self.UG[j].t[:])
        k.release(mark)

    def phase_a_dense(self, L):
        k, nc = self.k, self.nc
        m = L % 4
        last = (L == self.nlayers - 1) and self.nlayers == DEPTH
        nmap = 2 if m == 0 else 1
        dk = 64 if m == 0 else 128
        scale = dk ** -0.5
        mark = k.mark()
        KTs = [k.sb([128, T], BF16, "KTs%d" % i) for i in range(2)]
        QTs = [k.sb([128, T], BF16, "QTs%d" % i) for i in range(2)]
        Vs = [k.sb([128, 34, 128], BF16, "Vs%d" % i) for i in range(2)]
        Pb = [k.sb([128, 512], BF16, "Pb%d" % i) for i in range(3)]
        rz = k.sb([128, 512], F32, "rz")
        Am = [k.sb([128, 512], F32, "Am%d" % i) for i in range(2)]
        sq = k.sb([128, 512], F32, "sq")
        rs = k.sb([128, 512], F32, "rs")
        aob = [k.sb([128, 512], BF16, "aob%d" % i) for i in range(2)]
        Sps = [self.P[0], self.P[1], self.P[2]]
        Ops = [self.P[3], self.P[5]]
        Zps = [self.P[4], self.P[6]]
        if m == 0:
            lam_init = 0.8 - 0.6 * math.exp(-0.3 * L)
            lv = k.sb([1, 256], F32, "lv")
            k.dma("sp", lv, lv.t[:], self.a_lam, self.a_lam.t[:])
            pr = k.sb([1, 128], F32, "pr")
            k.op("dve", lambda: nc.vector.tensor_tensor(pr.t[:], lv.t[:, 0:128], lv.t[:, 128:256], ALU.mult), [lv], [pr])
            s2 = k.sb([1, 2], F32, "s2")
            k.op("dve", lambda: nc.vector.reduce_sum(s2.t[:], pr.t[:].rearrange("p (a b) -> p a b", b=64), mybir.AxisListType.X), [pr], [s2])
            k.op("act", lambda: nc.scalar.activation(s2.t[:], s2.t[:], AF.Exp), [s2], [s2])
            l1 = k.sb([1, 1], F32, "l1")
            k.op("dve", lambda: nc.vector.tensor_tensor(l1.t[:], s2.t[:, 1:2], s2.t[:, 0:1], ALU.subtract), [s2], [l1])
            k.op("dve", lambda: nc.vector.tensor_scalar(l1.t[:], l1.t[:], -lam_init, None, ALU.add), [l1], [l1])
            k.mm(self.P[0], [(self.ones_f.t[0:1, :], l1.t[:])], [self.ones_f, l1], out_ap=self.P[0].t[:, 0:1])
            neglam = k.sb([128, 1], F32, "neglam")
            k.op("dve", lambda: nc.vector.tensor_copy(neglam.t[:], self.P[0].t[:, 0:1]), [self.P[0]], [neglam])
            sg = k.sb([128, 1], F32, "sg")
            k.dma("sp", sg, sg.t[:], self.a_sub_g, self.a_sub_g.t[:])
            k.op("dve", lambda: nc.vector.tensor_scalar(sg.t[:], sg.t[:], 1.0 - lam_init, None, ALU.mult), [sg], [sg])
        PTf = Trk(self.PT.t[:].bitcast(F32), "PTf")
        Sg = [(self.P[0], self.P[1]), (self.P[2], PTf)]
        Pb = [k.sb([128, 512], BF16, "Pq%d" % i) for i in range(4)]
        sqs = [k.sb([128, 512], F32, "sq%d" % i) for i in range(2)]
        A0s = [k.sb([128, 512], F32, "A0s%d" % i) for i in range(2)]
        pbi = 0
        pending = []
        epi_i = 0

        def flush():
            while pending:
                pending.pop(0)()

        def make_epi(h, t0, n, A0, sq_):
            def run():
                R = self.P[6]
                k.op("act", lambda: nc.scalar.activation(sq_.t[:, :n], A0.t[:, :n], AF.Square), [A0], [sq_])
                k.mm(R, [(self.ones_f.t[:], sq_.t[:, :n])], [self.ones_f, sq_], out_ap=R.t[:, :n])
                k.op("dve", lambda: nc.vector.tensor_scalar(rs.t[:, :n], R.t[:, :n], 1.0 / 128, EPS, ALU.mult, ALU.add), [R], [rs])
                k.op("act", lambda: nc.scalar.activation(rs.t[:, :n], rs.t[:, :n], AF.Sqrt), [rs], [rs])
                k.op("dve", lambda: nc.vector.reciprocal(rs.t[:, :n], rs.t[:, :n]), [rs], [rs])
                ob = aob[h % 2]
                k.op("dve", lambda: nc.vector.scalar_tensor_tensor(ob.t[:, :n], A0.t[:, :n], sg.t[:, 0:1], rs.t[:, :n], ALU.mult, ALU.mult), [A0, sg, rs], [ob])
                k.dma("sp", self.AO, self.AO.t[h][:, t0:t0 + n], ob, ob.t[:, :n])
            return run

        for h in range(NH):
            kvh = h if m == 0 else h // 4
            Qt = QTs[h % 2]
            if m == 0 or h % 4 == 0:
                Kc, Vc = KTs[kvh % 2], Vs[kvh % 2]
                k.dma("sp", Kc, Kc.t[:], self.KT, self.KT.t[kvh], acc=False)
                k.dma("sp", Vc, Vc.t[:], self.VK, self.VK.t[kvh].rearrange("(a p) d -> p a d", p=128), acc=False)
            Kt, Vt = KTs[kvh % 2], Vs[kvh % 2]
            k.dma("sp", Qt, Qt.t[:], self.QT, self.QT.t[h], acc=False)
            for qi, (t0, n, col) in enumerate(BLOCKS):
                if last and col == 1:
                    continue
                kbs = list(range(2)) if col == 1 else list(range(34))
                groups = [kbs[i:i + 2] for i in range(0, len(kbs), 2)]
                for mp in range(nmap):
                    pr0 = mp * dk if m == 0 else 0
                    oz = mp if m == 0 else qi % 2
                    O, Z = Ops[oz], Zps[oz]

                    def emit_S(gi):
                        banks = Sg[gi % 2]
                        items = [(banks[j], banks[j].t[:, :n], Kt.t[pr0:pr0 + dk, kb * 128:(kb + 1) * 128], Qt.t[pr0:pr0 + dk, t0:t0 + n], True, True, True)
                                 for j, kb in enumerate(groups[gi])]
                        k.mmg(items, [Kt, Qt])
                    emit_S(0)
                    ng_ = len(groups)
                    for gi, g in enumerate(groups):
                        if gi + 1 < ng_:
                            emit_S(gi + 1)
                        if gi == 1 and mp == 0:
                            flush()
                        banks = Sg[gi % 2]
                        Ps = []
                        for j, kb in enumerate(g):
                            P_ = Pb[pbi % 4]
                            pbi += 1
                            S = banks[j]
                            k.op("act", lambda: nc.scalar.activation(P_.t[:, :n], S.t[:, :n], AF.Exp, scale=scale), [S], [P_])
                            Ps.append(P_)
                        items = []
                        for j, kb in enumerate(g):
                            f_ = (gi == 0 and j == 0)
                            l_ = (gi == ng_ - 1 and j == len(g) - 1)
                            items.append((O, O.t[:, :n], Vt.t[:, kb, :], Ps[j].t[:, :n], f_, l_, f_))
                        for j, kb in enumerate(g):
                            f_ = (gi == 0 and j == 0)
                            l_ = (gi == ng_ - 1 and j == len(g) - 1)
                            items.append((Z, Z.t[:, :n], self.ones_b.t[:], Ps[j].t[:, :n], f_, l_, f_))
                        k.mmg(items, [Vt, self.ones_b] + Ps)
                    k.op("dve", lambda: nc.vector.reciprocal(rz.t[:, :n], Z.t[:, :n]), [Z], [rz])
                    if m == 0:
                        A_ = Am[mp] if mp == 1 else A0s[epi_i % 2]
                        k.op("dve", lambda: nc.vector.tensor_tensor(A_.t[:, :n], O.t[:, :n], rz.t[:, :n], ALU.mult), [O, rz], [A_])
                    else:
                        ob = aob[pbi % 2]
                        k.op("dve", lambda: nc.vector.tensor_tensor(ob.t[:, :n], O.t[:, :n], rz.t[:, :n], ALU.mult), [O, rz], [ob])
                        k.dma("sp", self.AO, self.AO.t[h][:, t0:t0 + n], ob, ob.t[:, :n])
                if m == 0:
                    A0, A1 = A0s[epi_i % 2], Am[1]
                    k.op("dve", lambda: nc.vector.scalar_tensor_tensor(A0.t[:, :n], A1.t[:, :n], neglam.t[:, 0:1], A0.t[:, :n], ALU.mult, ALU.add), [A0, A1, neglam], [A0])
                    pending.append(make_epi(h, t0, n, A0, sqs[epi_i % 2]))
                    epi_i += 1
        flush()
        k.release(mark)


    def phase_a_na(self, L):
        k, nc = self.k, self.nc
        scale = 128 ** -0.5
        mark = k.mark()
        Kt = k.sb([128, T], BF16, "naK")
        Qt = k.sb([128, T], BF16, "naQ")
        Ve = k.sb([128, 34, 128], BF16, "naVe")
        Vo = k.sb([128, 31, 128], BF16, "naVo")
        EE = k.sb([128, 14, 64], F32, "naEE")
        MK = k.sb([128, 14, 64], F32, "naMK")
        sbS = [k.sb([128, 256], F32, "naS%d" % i) for i in range(2)]
        Pb = [k.sb([128, 6, 64], BF16, "naP%d" % i) for i in range(2)]
        rz = k.sb([128, 512], F32, "narz")
        aob = [k.sb([128, 512], BF16, "naob%d" % i) for i in range(2)]
        k.dma("sp", MK, MK.t[:], self.d_mask, self.d_mask.t[:])
        Sps = [self.P[0], self.P[1], self.P[2]]
        OZ = [(self.P[3], self.P[4]), (self.P[5], self.P[6])]
        for h in range(NH):
            k.dma("sp", Kt, Kt.t[:], self.KT, self.KT.t[h], acc=False)
            k.dma("sp", Qt, Qt.t[:], self.QT, self.QT.t[h], acc=False)
            k.dma("sp", Ve, Ve.t[:], self.VK, self.VK.t[h].rearrange("(a p) d -> p a d", p=128), acc=False)
            k.dma("sp", Vo, Vo.t[:], self.VK, self.VK.t[h, T_CTX + 64:T_CTX + 64 + 31 * 128, :].rearrange("(a p) d -> p a d", p=128), acc=False)
            k.dma("sp", EE, EE.t[:], self.d_ee, self.d_ee.t[h], acc=False)
            k.op("dve", lambda: nc.vector.tensor_tensor(EE.t[:], EE.t[:], MK.t[:], ALU.add), [EE, MK], [EE])

            def keycols(r, slot):
                rs_ = min(max(r - 4, 0), 56)
                if slot < 4:
                    s0 = T_CTX + rs_ * 64 + 128 * slot
                else:
                    s0 = 128 * (slot - 4)
                return s0

            def vblock(r, slot):
                rs_ = min(max(r - 4, 0), 56)
                if slot >= 4:
                    return Ve.t[:, slot - 4, :]
                if rs_ % 2 == 0:
                    return Ve.t[:, 2 + rs_ // 2 + slot, :]
                return Vo.t[:, (rs_ - 1) // 2 + slot, :]

            def smm(r):
                S = Sps[r % 3]
                for slot in range(6):
                    s0 = keycols(r, slot)
                    k.mm(S, [(Kt.t[:, s0:s0 + 128], Qt.t[:, T_CTX + r * 64:T_CTX + (r + 1) * 64])], [Kt, Qt],
                         out_ap=S.t[:, slot * 64:(slot + 1) * 64])
            smm(0)
            for r in range(64):
                if r + 1 < 64:
                    smm(r + 1)
                S = Sps[r % 3]
                rs_ = min(max(r - 4, 0), 56)
                d0 = rs_ - r + 7
                sb_, P_ = sbS[r % 2], Pb[r % 2]
                k.op("dve", lambda: nc.vector.scalar_tensor_tensor(sb_.t[:].rearrange("p (a b) -> p a b", b=64), S.t[:, 0:256].rearrange("p (a b) -> p a b", b=64),
                                                                  scale, EE.t[:, d0:d0 + 7:2, :], ALU.mult, ALU.add), [S, EE], [sb_])
                k.op("act", lambda: nc.scalar.activation(P_.t[:, 0:4, :], sb_.t[:].rearrange("p (a b) -> p a b", b=64), AF.Exp), [sb_], [P_])
                k.op("act", lambda: nc.scalar.activation(P_.t[:, 4:6, :], S.t[:, 256:384].rearrange("p (a b) -> p a b", b=64), AF.Exp, scale=scale), [S], [P_], acc=True)
                O, Z = OZ[(r // 8) % 2]
                rs8 = r % 8
                for slot in range(6):
                    k.mm(O, [(vblock(r, slot), P_.t[:, slot, :])], [Ve, Vo, P_], first=(slot == 0), last=(slot == 5), out_ap=O.t[:, rs8 * 64:(rs8 + 1) * 64])
                for slot in range(6):
                    k.mm(Z, [(self.ones_b.t[:], P_.t[:, slot, :])], [self.ones_b, P_], first=(slot == 0), last=(slot == 5), out_ap=Z.t[:, rs8 * 64:(rs8 + 1) * 64])
                if rs8 == 7:
                    ob = aob[(r // 8) % 2]
                    k.op("dve", lambda: nc.vector.reciprocal(rz.t[:], Z.t[:]), [Z], [rz])
                    k.op("dve", lambda: nc.vector.tensor_tensor(ob.t[:], O.t[:], rz.t[:], ALU.mult), [O, rz], [ob])
                    tq = T_CTX + (r - 7) * 64
                    k.dma("sp", self.AO, self.AO.t[h][:, tq:tq + 512], ob, ob.t[:])
        k.release(mark)


    def inproj_mamba(self, W, rhs, U, n, t0, st):
        k, nc = self.k, self.nc

        def epi(ci, c, ps):
            tmp = st["t1"] if ci % 2 == 0 else st["t2"]
            k.op("act", lambda: nc.scalar.copy(tmp.t[:, :n], ps.t[:, :n]), [ps], [tmp])
            if c < 16:
                k.dma("sp", self.ZT, self.ZT.t[c][:, t0:t0 + n], tmp, tmp.t[:, :n])
            elif c < 40:
                k.dma("sp", self.XBC, self.XBC.t[c - 16][:, t0:t0 + n], tmp, tmp.t[:, :n])
            else:
                k.dma("sp", self.DTR, self.DTR.t[:, t0:t0 + n], tmp, tmp.t[0:64, :n])
        self.gemm(W, list(range(41)), 1, rhs, [U], n, epi, self.P[0:3])

    def phase_a_mamba(self, L):
        k, nc = self.k, self.nc
        NCH = T // 128
        mark = k.mark()
        tri = k.sb([128, 2, 128], F32, "tri")
        k.dma("sp", tri, tri.t[:], self.c_tri, self.c_tri.t.rearrange("a p q -> p a q"))
        identf = k.sb([128, 128], F32, "identf")
        k.op("dve", lambda: nc.vector.tensor_copy(identf.t[:], self.ident.t[:]), [self.ident], [identf])
        identf64 = k.sb([64, 64], F32, "identf64")
        k.op("dve", lambda: nc.vector.tensor_copy(identf64.t[:], self.ident.t[0:64, 0:64]), [self.ident], [identf64])
        hp = k.sb([64, 4], F32, "hp")
        k.dma("sp", hp, hp.t[:], self.c_hp, self.c_hp.t[:])
        DTt = k.sb([128, NCH, 64], F32, "DTt")
        DTAt = k.sb([128, NCH, 64], F32, "DTAt")
        CS = k.sb([128, NCH, 64], F32, "CS")
        Dcol = k.sb([64, 32], F32, "Dcol")
        m2 = k.mark()
        dtr = k.sb([64, T], F32, "dtr")
        ab = k.sb([64, T], F32, "ab")
        na = k.sb([64, 1], F32, "na")
        k.dma("sp", dtr, dtr.t[:], self.DTR, self.DTR.t[:])
        k.op("dve", lambda: nc.vector.tensor_scalar(dtr.t[:], dtr.t[:], hp.t[:, 1:2], None, ALU.add), [dtr, hp], [dtr])
        k.op("act", lambda: nc.scalar.activation(ab.t[:], dtr.t[:], AF.Abs), [dtr], [ab])
        k.op("act", lambda: nc.scalar.activation(ab.t[:], ab.t[:], AF.Exp, scale=-1.0), [ab], [ab])
        k.op("act", lambda: nc.scalar.activation(ab.t[:], ab.t[:], AF.Ln, bias=1.0), [ab], [ab])
        k.op("dve", lambda: nc.vector.tensor_scalar(dtr.t[:], dtr.t[:], 0.0, None, ALU.max), [dtr], [dtr])
        k.op("dve", lambda: nc.vector.tensor_tensor(dtr.t[:], dtr.t[:], ab.t[:], ALU.add), [dtr, ab], [dtr])
        k.op("act", lambda: nc.scalar.activation(na.t[:], hp.t[:, 0:1], AF.Exp), [hp], [na])
        k.op("dve", lambda: nc.vector.tensor_scalar(na.t[:], na.t[:], -1.0, None, ALU.mult), [na], [na])
        k.op("dve", lambda: nc.vector.tensor_scalar(ab.t[:], dtr.t[:], na.t[:, 0:1], None, ALU.mult), [dtr, na], [ab])
        for (src, dst) in ((dtr, DTt), (ab, DTAt)):
            for c0 in range(0, NCH, 4):
                nn = min(4, NCH - c0)
                Pt = self.P[(c0 // 4) % 3]
                for i in range(nn):
                    c = c0 + i
                    k.transpose(Pt, Pt.t[:, i * 64:(i + 1) * 64], src, src.t[:, c * 128:(c + 1) * 128], identf64)
                k.op("dve", lambda: nc.vector.tensor_copy(dst.t[:, c0:c0 + nn, :], Pt.t[:, :nn * 64].rearrange("p (a b) -> p a b", b=64)), [Pt], [dst], acc=True)
        for c0 in range(0, NCH, 4):
            nn = min(4, NCH - c0)
            Pt = self.P[(c0 // 4) % 3]
            for i in range(nn):
                for d in range(2):
                    k.mm(Pt, [(tri.t[:, d, :], DTAt.t[:, c0 + i, d * 32:(d + 1) * 32])], [tri, DTAt], out_ap=Pt.t[:, i * 64 + d * 32:i * 64 + (d + 1) * 32])
            k.op("dve", lambda: nc.vector.tensor_copy(CS.t[:, c0:c0 + nn, :], Pt.t[:, :nn * 64].rearrange("p (a b) -> p a b", b=64)), [Pt], [CS], acc=True)
        sel = k.sb([64, 32], F32, "sel")
        k.op("dve", lambda: nc.vector.tensor_tensor(sel.t[:], identf.t[0:64, 0:32], identf.t[0:64, 32:64], ALU.add), [identf], [sel])
        k.op("dve", lambda: nc.vector.tensor_scalar(sel.t[:], sel.t[:], hp.t[:, 2:3], None, ALU.mult), [sel, hp], [sel])
        k.mm(self.P[3], [(self.ones_f.t[0:64, 0:64], sel.t[:])], [self.ones_f, sel], out_ap=self.P[3].t[0:64, 0:32])
        k.op("dve", lambda: nc.vector.tensor_copy(Dcol.t[:], self.P[3].t[0:64, 0:32]), [self.P[3]], [Dcol])
        k.release(m2)
        m2 = k.mark()
        cw = k.sb([128, 24, 5], F32, "cw")
        cb = k.sb([128, 24], F32, "cb")
        k.dma("sp", cw, cw.t[:], self.c_conv_wT, self.c_conv_wT.t[:])
        k.dma("sp", cb, cb.t[:], self.c_conv_bT, self.c_conv_bT.t[:])
        xins = [k.sb([128, T], F32, "xin%d" % i) for i in range(3)]
        accs = [k.sb([128, T], F32, "acc%d" % i) for i in range(3)]
        obs = [k.sb([128, T], BF16, "cob%d" % i) for i in range(2)]
        for c in range(24):
            xin, acc = xins[c % 3], accs[c % 3]
            k.dma("sp", xin, xin.t[:], self.XBC, self.XBC.t[c], acc=False)
            en, ee_ = ("dve", nc.vector)
            k.op(en, lambda: ee_.tensor_scalar(acc.t[:], xin.t[:], cw.t[:, c, 2:3], cb.t[:, c:c + 1], ALU.mult, ALU.add), [xin, cw, cb], [acc])
            for j in (0, 1, 3, 4):
                s_ = j - 2
                for (a_, b_) in ((0, T_CTX), (T_CTX, T)):
                    lo, hi = max(a_, a_ - s_), min(b_, b_ - s_)
                    k.op(en, lambda: ee_.scalar_tensor_tensor(acc.t[:, lo:hi], xin.t[:, lo + s_:hi + s_], cw.t[:, c, j:j + 1], acc.t[:, lo:hi], ALU.mult, ALU.add),
                         [xin, cw, acc], [acc])
            if c < 16:
                k.op("act", lambda: nc.scalar.activation(xin.t[:], acc.t[:], AF.Silu), [acc], [xin])
                k.dma("sp", self.XS, self.XS.t[c], xin, xin.t[:])
            else:
                ob = obs[c % 2]
                k.op("act", lambda: nc.scalar.activation(ob.t[:], acc.t[:], AF.Silu), [acc], [ob])
                k.dma("sp", self.BCb, self.BCb.t[c - 16], ob, ob.t[:])
        k.release(m2)
        BT = k.sb([128, T], BF16, "BT")
        CT = k.sb([128, T], BF16, "CT")
        Btok = k.sb([128, NCH, 128], BF16, "Btok")
        xb = k.sb([128, 2, T], BF16, "xb")
        xtok = k.sb([128, NCH, 256], BF16, "xtok")
        h32 = k.sb([128, 4, 64], F32, "h32")
        hbf = k.sb([128, 4, 64], BF16, "hbf")
        CBm = k.sb([128, 128], F32, "CBm")
        Dg = k.sb([128, 4, 128], F32, "Dg")
        Dm = k.sb([128, 4, 128], F32, "Dm")
        Er = k.sb([128, 4, 128], F32, "Er")
        G = k.sb([128, 4, 128], BF16, "G")
        Cd = k.sb([128, 4, 128], BF16, "Cd")
        xdt = k.sb([128, 4, 64], BF16, "xdt")
        xw = k.sb([128, 4, 64], BF16, "xw")
        wc = k.sb([128, 4], F32, "wc")
        y0s = [k.sb([64, 4, 128], F32, "y0s%d" % i) for i in range(2)]
        xc = [k.sb([64, 4, 128], F32, "xc%d" % i) for i in range(2)]
        zc = [k.sb([64, 4, 128], F32, "zc%d" % i) for i in range(2)]
        yt = k.sb([64, 4, 128], F32, "yt")
        yob = [k.sb([64, 4, 128], BF16, "yob%d" % i) for i in range(2)]
        Pcb, Pcs2, PY2, PS2 = self.P[0], (self.P[1], self.P[2]), (self.P[3], self.P[4]), (self.P[5], self.P[6])
        v3 = lambda ap: ap.rearrange("p (a b) -> p a b", b=128)
        it = 0
        for hb in range(8):
            g = hb // 2
            if hb % 2 == 0:
                k.dma("sp", BT, BT.t[:], self.BCb, self.BCb.t[g], acc=False)
                k.dma("sp", CT, CT.t[:], self.BCb, self.BCb.t[4 + g], acc=False)
                for c0 in range(0, NCH, 4):
                    nn = min(4, NCH - c0)
                    for i in range(nn):
                        c = c0 + i
                        k.transpose(self.PT, self.PT.t[:, i * 128:(i + 1) * 128], BT, BT.t[:, c * 128:(c + 1) * 128], self.ident)
                    k.op("dve", lambda: nc.vector.tensor_copy(Btok.t[:, c0:c0 + nn, :], v3(self.PT.t[:, :nn * 128])), [self.PT], [Btok], acc=True)
            k.dma("pool", xb, xb.t[:], self.XS, self.XS.t[2 * hb:2 * hb + 2].rearrange("c p t -> p c t"), acc=False)
            for c0 in range(0, NCH, 2):
                for i in range(2):
                    for a in range(2):
                        c = c0 + i
                        k.transpose(self.PT, self.PT.t[:, i * 256 + a * 128:i * 256 + (a + 1) * 128], xb, xb.t[:, a, c * 128:(c + 1) * 128], self.ident)
                k.op("dve", lambda: nc.vector.tensor_copy(xtok.t[:, c0:c0 + 2, :], self.PT.t[:, :512].rearrange("p (a b) -> p a b", b=256)), [self.PT], [xtok], acc=True)
            for d in range(2):
                k.op("dve", lambda: nc.vector.memset(h32.t[:], 0.0), [], [h32])
                k.op("dve", lambda: nc.vector.memset(hbf.t[:], 0.0), [], [hbf])
                order = list(range(NCH)) if d == 0 else [1, 0] + list(range(NCH - 1, 1, -1))
                dh0 = d * 32 + hb * 4
                e = 127 if d == 0 else 0
                for c in order:
                    it += 1
                    tok = c * 128
                    Pcs, PY, PS_ = Pcs2[it % 2], PY2[it % 2], PS2[it % 2]
                    ysl = self.YS.t[hb * 4:(hb + 1) * 4, :, tok:tok + 128].rearrange("h p t -> p h t")
                    if d == 1:
                        y0, xc_, zc_, ob = y0s[it % 2], xc[it % 2], zc[it % 2], yob[it % 2]
                        k.dma("sp", y0, y0.t[:], self.YS, ysl, acc=False)
                        k.dma("sp", xc_, xc_.t[:], self.XS, self.XS.t[2 * hb:2 * hb + 2, :, tok:tok + 128].rearrange("c (two p) t -> p (c two) t", two=2), acc=False)
                        k.dma("sp", zc_, zc_.t[:], self.ZT, self.ZT.t[2 * hb:2 * hb + 2, :, tok:tok + 128].rearrange("c (two p) t -> p (c two) t", two=2), acc=False)
                    k.mm(Pcb, [(BT.t[:, tok:tok + 128], CT.t[:, tok:tok + 128])], [BT, CT], out_ap=Pcb.t[:, 0:128])
                    k.op("dve", lambda: nc.vector.tensor_tensor(CBm.t[:], Pcb.t[:, 0:128], tri.t[:, d, :], ALU.mult), [Pcb, tri], [CBm])
                    k.op("pool", lambda: nc.gpsimd.tensor_tensor(Dg.t[:], tri.t[:, d, :].unsqueeze(1).to_broadcast([128, 4, 128]),
                                                                DTAt.t[:, c, dh0:dh0 + 4].unsqueeze(2).to_broadcast([128, 4, 128]), ALU.mult), [tri, DTAt], [Dg])
                    k.mm(Pcs, [(self.ones_f.t[:], Dg.t[:].rearrange("p a b -> p (a b)"))], [self.ones_f, Dg])
                    k.op("dve", lambda: nc.vector.tensor_tensor(Dm.t[:], v3(Pcs.t[:]), CS.t[:, c, dh0:dh0 + 4].unsqueeze(2).to_broadcast([128, 4, 128]), ALU.subtract), [Pcs, CS], [Dm])
                    k.op("dve", lambda: nc.vector.tensor_scalar(Dm.t[:], Dm.t[:], 0.0, None, ALU.min), [Dm], [Dm])
                    k.op("act", lambda: nc.scalar.activation(Dm.t[:], Dm.t[:], AF.Exp), [Dm], [Dm])
                    k.op("pool", lambda: nc.gpsimd.tensor_tensor(G.t[:], Dm.t[:], CBm.t[:].unsqueeze(1).to_broadcast([128, 4, 128]), ALU.mult), [Dm, CBm], [G])
                    k.op("act", lambda: nc.scalar.activation(Er.t[:], v3(Pcs.t[:]), AF.Exp), [Pcs], [Er])
                    k.op("pool", lambda: nc.gpsimd.tensor_tensor(Cd.t[:], Er.t[:], CT.t[:, tok:tok + 128].unsqueeze(1).to_broadcast([128, 4, 128]), ALU.mult), [Er, CT], [Cd])
                    k.op("dve", lambda: nc.vector.tensor_tensor(xdt.t[:], xtok.t[:, c, :].rearrange("p (a b) -> p a b", b=64),
                                                                DTt.t[:, c, dh0:dh0 + 4].unsqueeze(2).to_broadcast([128, 4, 64]), ALU.mult), [xtok, DTt], [xdt])
                    for j in range(4):
                        k.mm(PY, [(xdt.t[:, j, :], G.t[:, j, :]), (hbf.t[:, j, :], Cd.t[:, j, :])], [xdt, G, hbf, Cd], out_ap=PY.t[0:64, j * 128:(j + 1) * 128])
                    k.op("dve", lambda: nc.vector.tensor_tensor(wc.t[:], v3(Pcs.t[:])[:, :, e], CS.t[:, c, dh0:dh0 + 4], ALU.subtract), [Pcs, CS], [wc])
                    k.op("act", lambda: nc.scalar.activation(wc.t[:], wc.t[:], AF.Exp), [wc], [wc])
                    k.op("pool", lambda: nc.gpsimd.tensor_tensor(xw.t[:], xdt.t[:], wc.t[:].unsqueeze(2).to_broadcast([128, 4, 64]), ALU.mult), [xdt, wc], [xw])
                    k.mm(PS_, [(Btok.t[:, c, :], xw.t[:].rearrange("p a b -> p (a b)"))], [Btok, xw], out_ap=PS_.t[:, 0:256])
                    k.op("pool", lambda: nc.gpsimd.tensor_tensor(h32.t[:], h32.t[:], Er.t[:, :, e:e + 1].to_broadcast([128, 4, 64]), ALU.mult), [h32, Er], [h32])
                    k.op("dve", lambda: nc.vector.tensor_tensor(h32.t[:], h32.t[:], PS_.t[:, 0:256].rearrange("p (a b) -> p a b", b=64), ALU.add), [h32, PS_], [h32])
                    k.op("act", lambda: nc.scalar.copy(hbf.t[:], h32.t[:]), [h32], [hbf])
                    if d == 0:
                        y0 = y0s[it % 2]
                        k.op("act", lambda: nc.scalar.copy(y0.t[:], v3(PY.t[0:64, :])), [PY], [y0])
                        k.dma("sp", self.YS, ysl, y0, y0.t[:])
                    else:
                        k.op("dve", lambda: nc.vector.tensor_tensor(yt.t[:], xc_.t[:], Dcol.t[:, hb * 4:hb * 4 + 4].unsqueeze(2).to_broadcast([64, 4, 128]), ALU.mult), [xc_, Dcol], [yt])
                        k.op("dve", lambda: nc.vector.tensor_tensor(yt.t[:], yt.t[:], v3(PY.t[0:64, :]), ALU.add), [yt, PY], [yt])
                        k.op("dve", lambda: nc.vector.tensor_tensor(yt.t[:], yt.t[:], y0.t[:], ALU.add), [yt, y0], [yt])
                        k.op("act", lambda: nc.scalar.activation(zc_.t[:], zc_.t[:], AF.Silu), [zc_], [zc_])
                        k.op("dve", lambda: nc.vector.tensor_tensor(ob.t[:], yt.t[:], zc_.t[:], ALU.mult), [yt, zc_], [ob])
                        k.dma("sp", self.AO, self.AO.t[2 * hb:2 * hb + 2, :, tok:tok + 128].rearrange("c (two p) t -> p (c two) t", two=2), ob, ob.t[:])
        k.release(mark)

    def prep_weights(self, L):
        m = L % 4
        wc = self.wcache
        if m == 0:
            wc[("in", L)] = self.conv_w_inter("winA", self.a_w_in, self.a_w_sw, 2 * NH, NH)
            wc[("out", L)] = self.conv_w("woA", self.a_w_out, self.a_w_out.t, 1024, D)
        elif m == 1:
            wc[("in", L)] = self.conv_w_inter("winB", self.b_w_in, self.b_w_sw, NH + 2, 2)
            wc[("out", L)] = self.conv_w("woB", self.b_w_out, self.b_w_out.t, 1024, D)
        elif m == 2:
            wc[("in", L)] = self.conv_w("winC", self.c_w_in, self.c_w_in.t, D, 5248)
            wc[("out", L)] = self.conv_w("woC", self.c_w_out, self.c_w_out.t, 2048, D)
        else:
            wc[("in", L)] = self.conv_w("winD", self.d_w_in, self.d_w_in.t, D, 3072)
            wc[("out", L)] = self.conv_w("woD", self.d_w_out, self.d_w_out.t, 1024, D)
        wc[("w1", L)] = self.conv_w("w1_%d" % L, self.mlp_w1, self.mlp_w1.t[L], D, D_FF)
        wc[("w2", L)] = self.conv_w("w2_%d" % L, self.mlp_w2, self.mlp_w2.t[L], D_FF, D)

    def conv_w_inter(self, name, w, wsw, nqk, nv):
        k = self.k
        nch = 2 * nqk + nv
        dst = k.dram(name, [nch, 128, KC, 128], BF16)
        for c in range(nch):
            if c < 2 * nqk:
                src = w if c % 2 == 0 else wsw
                sc = c // 2
            else:
                src = w
                sc = nqk + (c - 2 * nqk)
            k.dma("pool", dst, dst.t[c], src, src.t[:, sc * 128:(sc + 1) * 128].rearrange("(kc p) n -> p kc n", p=128))
        return dst

    def build(self):
        k = self.k
        self.mod_init()
        self.conv_ada(0)
        self.prep_weights(0)
        self.compute_mod(0)
        self.phase_b(-1)
        for L in range(self.nlayers):
            m = L % 4
            self.inproj_all(L)
            if L == 0 and self.nlayers > 1:
                self.conv_ada(1)
                self.prep_weights(1)
            if L == 1 and self.nlayers > 2:
                self.conv_ada(2)
                self.prep_weights(2)
            if m in (0, 1):
                self.phase_a_dense(L)
            elif m == 2:
                self.phase_a_mamba(L)
            else:
                self.phase_a_na(L)
            self.outproj_all(L)
            if L + 1 < self.nlayers:
                self.compute_mod(L + 1)
            if L == 1 and self.nlayers > 3:
                self.conv_ada(3)
                self.prep_weights(3)
            self.phase_b(L)
        k.fence(self.out)
        if self.dbg and self.nlayers < DEPTH:
            k.fence(self.out_ctx)
        k.close()
        return self.nc


def rope_tables(dim):
    t = np.arange(T_LAT)
    row = (t // GRID_W).astype(np.float32)
    col = (t % GRID_W).astype(np.float32)
    n_pairs = dim // 4
    inv = (10000.0 ** (-np.arange(n_pairs, dtype=np.float32) / n_pairs)).astype(np.float32)
    ang = np.concatenate([row[:, None] * inv, col[:, None] * inv], axis=-1).astype(np.float32)
    cos, sin = np.cos(ang), np.sin(ang)
    tab = np.zeros((2, 128, T), np.float32)
    tab[0, :, :T_CTX] = 1.0
    for p in range(128):
        d = p % dim
        j = d // 2
        tab[0, p, T_CTX:] = cos[:, j]
        tab[1, p, T_CTX:] = -sin[:, j] if d % 2 == 0 else sin[:, j]
    return tab


def swap_pairs_cols(w):
    k, n = w.shape
    return np.ascontiguousarray(w.reshape(k, n // 2, 2)[:, :, ::-1].reshape(k, n))


def fm(v):
    s = v.shape
    a = v.reshape(s[:-1] + (s[-1] // 128, 128))
    return np.ascontiguousarray(np.moveaxis(a, -1, 0))


def make_in_maps(inp, cores):
    f = lambda a: np.ascontiguousarray(np.asarray(a, dtype=np.float32))
    x, c, ctx, c_ctx = f(inp["x"]), f(inp["c"]), f(inp["ctx"]), f(inp["c_ctx"])
    shared = {}
    shared["ada_w"] = f(inp["ada_w"])
    shared["ada_bT"] = fm(f(inp["ada_b"]))
    ln = np.stack([f(inp["ln_g"]), f(inp["ln_b"])], axis=2)
    shared["lnT"] = fm(ln)
    shared["mlp_w1"] = f(inp["mlp_w1"])
    shared["mlp_w2"] = f(inp["mlp_w2"])
    shared["ident"] = np.eye(128, dtype=np.float32).astype(ml_dtypes.bfloat16)
    lam = f(inp["a_lambda"])[0]
    shared["a_lam"] = np.ascontiguousarray(np.concatenate([lam[0], lam[2], lam[1], lam[3]])[None, :])
    shared["a_sub_g"] = np.ascontiguousarray(f(inp["a_sub_g"])[0][:, None])
    shared["ropeA"] = rope_tables(64)
    qg, kg = f(inp["b_qn_g"])[0], f(inp["b_kn_g"])[0]
    sw1 = lambda g: g.reshape(64, 2)[:, ::-1].reshape(128)
    shared["b_g"] = np.ascontiguousarray(np.stack([qg, sw1(qg), kg, sw1(kg)], axis=1))
    shared["ropeB"] = rope_tables(128)
    tri = np.zeros((2, 128, 128), np.float32)
    tri[0] = np.triu(np.ones((128, 128), np.float32))
    tri[1] = np.tril(np.ones((128, 128), np.float32))
    shared["c_tri"] = tri
    kc_ = np.arange(128) % 64
    cq = np.arange(64)
    cstart = np.clip(cq - 8, 0, 48)
    valid = (kc_[:, None] >= cstart[None, :]) & (kc_[:, None] < cstart[None, :] + 16)
    shared["d_mask"] = np.ascontiguousarray(np.broadcast_to(np.where(valid, 0.0, -30000.0).astype(np.float32)[:, None, :], (128, 14, 64)))
    aw, bw, cw, dw = f(inp["a_w_in"])[0], f(inp["b_w_in"])[0], f(inp["c_w_in"])[0], f(inp["d_w_in"])[0]
    awo, bwo, cwo, dwo = f(inp["a_w_out"])[0], f(inp["b_w_out"])[0], f(inp["c_w_out"])[0], f(inp["d_w_out"])[0]
    convw, convb = f(inp["c_conv_w"])[0], f(inp["c_conv_b"])[0]
    alog, dtb, dsk = f(inp["c_A_log"])[0], f(inp["c_dt_bias"])[0], f(inp["c_D"])[0]
    ng = f(inp["c_norm_g"])[0]
    rpb = f(inp["d_rpb"])[0]
    idx = np.clip(np.arange(64)[:, None] - np.arange(64)[None, :] + 15, 0, 30)
    Eh = rpb[:, :, idx]
    ee = np.concatenate([Eh[:, 0:14], Eh[:, 1:15]], axis=2)
    ee = np.ascontiguousarray(ee.transpose(0, 2, 1, 3))
    per_rank = []
    for h in range(2):
        pr = {}
        s1 = slice(1024 * h, 1024 * (h + 1))
        qk = np.concatenate([aw[:, 0:2048][:, s1], aw[:, 2048:4096][:, s1]], axis=1)
        pr["a_w_in"] = np.ascontiguousarray(np.concatenate([qk, aw[:, 4096:6144][:, s1]], axis=1))
        pr["a_w_sw"] = swap_pairs_cols(qk)
        pr["a_w_out"] = np.ascontiguousarray(awo[s1, :])
        s2 = slice(256 * h, 256 * (h + 1))
        qkb = np.concatenate([bw[:, 0:2048][:, s1], bw[:, 2048:2560][:, s2]], axis=1)
        pr["b_w_in"] = np.ascontiguousarray(np.concatenate([qkb, bw[:, 2560:3072][:, s2]], axis=1))
        pr["b_w_sw"] = swap_pairs_cols(qkb)
        pr["b_w_out"] = np.ascontiguousarray(bwo[s1, :])
        sc = slice(2048 * h, 2048 * (h + 1))
        sg = slice(512 * h, 512 * (h + 1))
        sh = slice(32 * h, 32 * (h + 1))
        dtc = np.concatenate([cw[:, 10240:10304][:, sh], cw[:, 10304:10368][:, sh], np.zeros((D, 64), np.float32)], axis=1)
        pr["c_w_in"] = np.ascontiguousarray(np.concatenate([cw[:, 0:4096][:, sc], cw[:, 4096:8192][:, sc], cw[:, 8192:9216][:, sg], cw[:, 9216:10240][:, sg], dtc], axis=1))
        cwo_ = np.concatenate([convw[:, 0:4096][:, sc], convw[:, 4096:5120][:, sg], convw[:, 5120:6144][:, sg]], axis=1)
        pr["c_conv_wT"] = np.ascontiguousarray(np.transpose(fm(cwo_), (0, 2, 1)))
        cbo_ = np.concatenate([convb[0:4096][sc], convb[4096:5120][sg], convb[5120:6144][sg]])
        pr["c_conv_bT"] = fm(cbo_)
        own = lambda v: np.concatenate([v[0][sh], v[1][sh]])
        pr["c_hp"] = np.ascontiguousarray(np.stack([own(alog), own(dtb), own(dsk), np.zeros(64, np.float32)], axis=1))
        pr["c_norm_gT"] = fm(ng[sc])
        pr["c_w_out"] = np.ascontiguousarray(cwo[sc, :])
        pr["d_w_in"] = np.ascontiguousarray(np.concatenate([dw[:, 0:2048][:, s1], dw[:, 2048:4096][:, s1], dw[:, 4096:6144][:, s1]], axis=1))
        pr["d_ee"] = np.ascontiguousarray(ee[8 * h:8 * (h + 1)])
        pr["d_w_out"] = np.ascontiguousarray(dwo[s1, :])
        per_rank.append(pr)
    maps = []
    for core in cores:
        b, h = core // 2, core % 2
        mp = dict(shared)
        mp.update(per_rank[h])
        xa = np.concatenate([ctx[b][128 * h:128 * (h + 1)], x[b][2048 * h:2048 * (h + 1)]], axis=0)
        mp["xT"] = np.ascontiguousarray(xa.T.reshape(KC, 128, TO))
        cc = np.stack([c[b], c_ctx], axis=1)
        mp["cT"] = np.ascontiguousarray(cc.reshape(KC, 128, 2).transpose(1, 0, 2))
        maps.append(mp)
    return maps


_CACHE = {}


def kernel(**inputs):
    if "nc" not in _CACHE:
        _CACHE["nc"] = Prog(4).build()
    nc = _CACHE["nc"]
    cores = list(range(8))
    maps = make_in_maps(inputs, cores)
    res = run_bass_kernel_spmd(nc, maps, core_ids=cores)
    out = np.empty((4, T_LAT, D), np.float32)
    for core, r in zip(cores, res.results):
        b, h = core // 2, core % 2
        o = np.asarray(r["out"]).reshape(D, 2048)
        out[b, 2048 * h:2048 * (h + 1), :] = o.T
    return out
```
